# Optimizing a Trainium2 kernel written in Bass

```python
import jax, jax.numpy as jnp
from jax import lax
import numpy as np

D_MODEL = 2048
BATCH = 2
SEQ = 4096
DEPTH = 1
DEC_BATCH = 128
DEC_SEQ = 4
PAST_LEN = 2048
PAGE_SIZE = 128

POOL_WINDOWS = (2, 4, 8, 16)
N_POOL_GROUPS = 4
POOL_WIDTH = D_MODEL // 2
POOL_GROUP = POOL_WIDTH // N_POOL_GROUPS
POOL_BUF = max(POOL_WINDOWS) - 1
HEAD_DIM = 128
N_HEADS = D_MODEL // HEAD_DIM
N_KV_HEADS = 4
N_GROUP = N_HEADS // N_KV_HEADS
ATTN_WIDTH = N_HEADS * HEAD_DIM
KV_WIDTH = N_KV_HEADS * HEAD_DIM
N_IDX_HEADS = 16
IDX_DIM = 64
TOPK_MAX = 256
Q_BLOCK = 128
D_FF = -(-8 * D_MODEL // (3 * 256)) * 256
RMS_EPS = 1e-6
IN_SIZES = (POOL_WIDTH, ATTN_WIDTH, KV_WIDTH, KV_WIDTH, N_IDX_HEADS * IDX_DIM, IDX_DIM, N_IDX_HEADS, D_MODEL, D_MODEL)
IN_WIDTH = sum(IN_SIZES)

kernel_name = 'cond_pool_dsa_hybrid_step'


def rmsnorm(x, g):
    xf = x.astype(jnp.float32)
    y = xf * lax.rsqrt(jnp.mean(xf * xf, axis=-1, keepdims=True) + RMS_EPS)
    return (y * g.astype(jnp.float32)).astype(x.dtype)


def modulate(x, g, shift, scale):
    return rmsnorm(x, g) * (1 + scale[:, None]) + shift[:, None]


def adaln(c, w_ada, b_ada):
    mod = jax.nn.silu(c) @ w_ada + b_ada
    return jnp.split(mod, 6, axis=-1)


def split_proj(p):
    outs, o = [], 0
    for s in IN_SIZES:
        outs.append(p[..., o:o + s])
        o += s
    return outs


def pool_mix(seq, start_pos, n_out, w_grp, scale):
    N, R, _ = seq.shape
    cs = jnp.cumsum(jnp.pad(seq.astype(jnp.float32), ((0, 0), (1, 0), (0, 0))), axis=1)
    out_idx = jnp.arange(R - n_out, R)
    pos = start_pos + out_idx
    cur = seq[:, R - n_out:].astype(jnp.float32)
    outs = []
    for g, w in enumerate(POOL_WINDOWS):
        sl = slice(g * POOL_GROUP, (g + 1) * POOL_GROUP)
        cs_g = cs[..., sl]
        lo = jnp.maximum(out_idx + 1 - w, 0)
        win_sum = cs_g[:, out_idx + 1] - cs_g[:, lo]
        cnt = jnp.minimum(pos + 1, w).astype(jnp.float32)
        outs.append(win_sum / cnt[None, :, None] - cur[..., sl])
    d = jnp.stack(outs, axis=2).astype(seq.dtype)
    y = jnp.einsum('ntgc,gce->ntge', d, w_grp).reshape(N, n_out, POOL_WIDTH)
    return y * scale


def indexer_scores(qi, ki, wi):
    dots = jnp.einsum('nthd,nsd->nths', qi, ki, preferred_element_type=jnp.float32) * (IDX_DIM ** -0.5)
    return jnp.einsum('nths,nth->nts', jax.nn.relu(dots), wi.astype(jnp.float32) * (N_IDX_HEADS ** -0.5))


def select_keys(scores, q_pos, topk):
    S = scores.shape[-1]
    vis = jnp.arange(S)[None, :] <= q_pos[:, None]
    masked = jnp.where(vis[None], scores, -jnp.inf)
    _, idx = lax.top_k(masked, topk)
    valid = idx <= q_pos[None, :, None]
    return idx, valid


def sparse_attend(q, k_sel, v_sel, valid):
    N, T = q.shape[:2]
    qg = q.reshape(N, T, N_KV_HEADS, N_GROUP, HEAD_DIM)
    s = jnp.einsum('ntjgd,ntkjd->ntjgk', qg, k_sel, preferred_element_type=jnp.float32) * (HEAD_DIM ** -0.5)
    s = jnp.where(valid[:, :, None, None, :], s, -jnp.inf)
    p = jax.nn.softmax(s, axis=-1).astype(v_sel.dtype)
    o = jnp.einsum('ntjgk,ntkjd->ntjgd', p, v_sel)
    return o.reshape(N, T, ATTN_WIDTH)


def gather_rows(a, i):
    return jax.vmap(lambda ab, ib: ab[ib])(a, i)


def front(x, c, lw):
    mods = adaln(c, lw['w_ada'], lw['b_ada'])
    u = modulate(x, lw['g_norm1'], mods[0], mods[1])
    return mods, split_proj(u @ lw['w_in'])


def back(x, mods, pool_out, attn_out, ga, gb, lw):
    mix = jax.nn.sigmoid(ga) * (pool_out @ lw['w_up_pool']) + jax.nn.sigmoid(gb) * (attn_out @ lw['w_up_attn'])
    h = x + mods[2][:, None] * (mix @ lw['w_out'])
    hn = modulate(h, lw['g_norm2'], mods[3], mods[4])
    gate, up = jnp.split(hn @ lw['w_ffn_in'], 2, axis=-1)
    return h + mods[5][:, None] * ((jax.nn.silu(gate) * up) @ lw['w_ffn_out'])


def layer_prompt(x, c, lw):
    B, T, _ = x.shape
    mods, (pin, q, k, v, qi, ki, wi, ga, gb) = front(x, c, lw)
    seq = jnp.concatenate([jnp.zeros((B, POOL_BUF, POOL_WIDTH), pin.dtype), pin], axis=1)
    pool_out = pool_mix(seq, -POOL_BUF, T, lw['w_pool_grp'], lw['pool_scale'])
    pool_tail = seq[:, -POOL_BUF:]
    q = q.reshape(B, T, N_HEADS, HEAD_DIM)
    k = k.reshape(B, T, N_KV_HEADS, HEAD_DIM)
    v = v.reshape(B, T, N_KV_HEADS, HEAD_DIM)
    qi = qi.reshape(B, T, N_IDX_HEADS, IDX_DIM)
    topk = min(TOPK_MAX, T // 4)
    nb = T // Q_BLOCK

    def blk(args):
        qb, qib, wib, t0 = args
        q_pos = t0 + jnp.arange(Q_BLOCK)
        idx, valid = select_keys(indexer_scores(qib, ki, wib), q_pos, topk)
        return sparse_attend(qb, gather_rows(k, idx), gather_rows(v, idx), valid)

    to_blocks = lambda a: jnp.swapaxes(a.reshape(B, nb, Q_BLOCK, *a.shape[2:]), 0, 1)
    out = lax.map(blk, (to_blocks(q), to_blocks(qi), to_blocks(wi), jnp.arange(nb) * Q_BLOCK))
    attn = jnp.swapaxes(out, 0, 1).reshape(B, T, ATTN_WIDTH)
    y = back(x, mods, pool_out, attn, ga, gb, lw)
    return y, k, v, ki, pool_tail


def layer_sample(x, c, cache_k, cache_v, cache_ik, st_pool, page_table, lw):
    Bn, Tn, _ = x.shape
    page = cache_k.shape[1]
    past = page_table.shape[1] * page
    mods, (pin, q, k, v, qi, ki, wi, ga, gb) = front(x, c, lw)
    seq = jnp.concatenate([st_pool.astype(pin.dtype), pin], axis=1)
    pool_out = pool_mix(seq, past - POOL_BUF, Tn, lw['w_pool_grp'], lw['pool_scale'])
    pool_tail = seq[:, -POOL_BUF:]
    q = q.reshape(Bn, Tn, N_HEADS, HEAD_DIM)
    k = k.reshape(Bn, Tn, N_KV_HEADS, HEAD_DIM)
    v = v.reshape(Bn, Tn, N_KV_HEADS, HEAD_DIM)
    qi = qi.reshape(Bn, Tn, N_IDX_HEADS, IDX_DIM)
    ik_all = jnp.concatenate([cache_ik[page_table].reshape(Bn, past, IDX_DIM).astype(ki.dtype), ki], axis=1)
    topk = min(TOPK_MAX, (past + Tn) // 4)
    q_pos = past + jnp.arange(Tn)
    idx, valid = select_keys(indexer_scores(qi, ik_all, wi), q_pos, topk)
    is_past = (idx < past)[..., None, None]
    ip = jnp.clip(idx, 0, past - 1)
    phys = jax.vmap(lambda pt, i: pt[i])(page_table, ip // page)
    flat = phys * page + ip % page
    inew = jnp.clip(idx - past, 0, Tn - 1)
    k_sel = jnp.where(is_past, cache_k.reshape(-1, N_KV_HEADS, HEAD_DIM)[flat].astype(k.dtype), gather_rows(k, inew))
    v_sel = jnp.where(is_past, cache_v.reshape(-1, N_KV_HEADS, HEAD_DIM)[flat].astype(v.dtype), gather_rows(v, inew))
    attn = sparse_attend(q, k_sel, v_sel, valid)
    y = back(x, mods, pool_out, attn, ga, gb, lw)
    return y, k, v, ki, pool_tail


def setup_inputs(seed: int = 0) -> dict:
    key = jax.random.key(seed)
    ks = jax.random.split(key, 24)
    n_pages = PAST_LEN // PAGE_SIZE
    n_used = DEC_BATCH * n_pages
    n_phys = n_used + n_used // 4
    nrm = lambda k, shape, s=1.0: s * jax.random.normal(k, shape, jnp.float32)
    page_table = jax.random.permutation(ks[0], n_phys)[:n_used].reshape(DEC_BATCH, n_pages).astype(jnp.int32)
    return {
        'x_prompt': nrm(ks[1], (BATCH, SEQ, D_MODEL)),
        'x_sample': nrm(ks[2], (DEC_BATCH, DEC_SEQ, D_MODEL)),
        'cache_k': nrm(ks[3], (DEPTH, n_phys, PAGE_SIZE, N_KV_HEADS, HEAD_DIM)),
        'cache_v': nrm(ks[4], (DEPTH, n_phys, PAGE_SIZE, N_KV_HEADS, HEAD_DIM)),
        'cache_idx_k': nrm(ks[5], (DEPTH, n_phys, PAGE_SIZE, IDX_DIM)),
        'state_pool': nrm(ks[6], (DEPTH, DEC_BATCH, POOL_BUF, POOL_WIDTH)),
        'page_table': page_table,
        'c_prompt': nrm(ks[7], (BATCH, D_MODEL)),
        'c_sample': nrm(ks[8], (DEC_BATCH, D_MODEL)),
        'w_ada': nrm(ks[9], (DEPTH, D_MODEL, 6 * D_MODEL), 0.5 * D_MODEL ** -0.5),
        'b_ada': nrm(ks[10], (DEPTH, 6 * D_MODEL), 0.01),
        'g_norm1': 1.0 + nrm(ks[11], (DEPTH, D_MODEL), 0.1),
        'w_in': nrm(ks[12], (DEPTH, D_MODEL, IN_WIDTH), D_MODEL ** -0.5),
        'w_pool_grp': nrm(ks[13], (DEPTH, N_POOL_GROUPS, POOL_GROUP, POOL_GROUP), POOL_GROUP ** -0.5),
        'pool_scale': 1.0 + nrm(ks[14], (DEPTH, POOL_WIDTH), 0.1),
        'w_up_pool': nrm(ks[15], (DEPTH, POOL_WIDTH, D_MODEL), POOL_WIDTH ** -0.5),
        'w_up_attn': nrm(ks[16], (DEPTH, ATTN_WIDTH, D_MODEL), ATTN_WIDTH ** -0.5),
        'w_out': nrm(ks[17], (DEPTH, D_MODEL, D_MODEL), D_MODEL ** -0.5),
        'g_norm2': 1.0 + nrm(ks[18], (DEPTH, D_MODEL), 0.1),
        'w_ffn_in': nrm(ks[19], (DEPTH, D_MODEL, 2 * D_FF), D_MODEL ** -0.5),
        'w_ffn_out': nrm(ks[20], (DEPTH, D_FF, D_MODEL), D_FF ** -0.5),
        'g_final': 1.0 + nrm(ks[21], (D_MODEL,), 0.1),
    }


def reference(x_prompt, x_sample, cache_k, cache_v, cache_idx_k, state_pool, page_table, c_prompt, c_sample,
              w_ada, b_ada, g_norm1, w_in, w_pool_grp, pool_scale, w_up_pool, w_up_attn, w_out, g_norm2,
              w_ffn_in, w_ffn_out, g_final):
    hp, hs = x_prompt, x_sample
    kp, vp, ip, pp, ksm, vsm, ism, psm = [], [], [], [], [], [], [], []
    for l in range(DEPTH):
        lw = {'w_ada': w_ada[l], 'b_ada': b_ada[l], 'g_norm1': g_norm1[l], 'w_in': w_in[l],
              'w_pool_grp': w_pool_grp[l], 'pool_scale': pool_scale[l], 'w_up_pool': w_up_pool[l],
              'w_up_attn': w_up_attn[l], 'w_out': w_out[l], 'g_norm2': g_norm2[l],
              'w_ffn_in': w_ffn_in[l], 'w_ffn_out': w_ffn_out[l]}
        hp, k1, v1, i1, p1 = layer_prompt(hp, c_prompt, lw)
        hs, k2, v2, i2, p2 = layer_sample(hs, c_sample, cache_k[l], cache_v[l], cache_idx_k[l], state_pool[l], page_table, lw)
        kp.append(k1); vp.append(v1); ip.append(i1); pp.append(p1)
        ksm.append(k2); vsm.append(v2); ism.append(i2); psm.append(p2)
    y_prompt = rmsnorm(hp, g_final)
    y_sample = rmsnorm(hs, g_final)
    k_prompt, v_prompt, idxk_prompt, pool_prompt = jnp.stack(kp), jnp.stack(vp), jnp.stack(ip), jnp.stack(pp)
    k_sample, v_sample, idxk_sample, pool_sample = jnp.stack(ksm), jnp.stack(vsm), jnp.stack(ism), jnp.stack(psm)
    return (y_prompt, y_sample, k_prompt, v_prompt, idxk_prompt, pool_prompt, k_sample, v_sample, idxk_sample, pool_sample)
```

```python
import contextlib
import numpy as np
import concourse.bass as bass
import concourse.mybir as mybir
from concourse.bass_utils import run_bass_kernel_spmd

F32 = mybir.dt.float32
BF16 = mybir.dt.bfloat16
I32 = mybir.dt.int32
I16 = mybir.dt.int16
ALU = mybir.AluOpType
AF = mybir.ActivationFunctionType
AX = mybir.AxisListType
DT_SIZE = {F32: 4, BF16: 2, I32: 4, I16: 2}

P = 128
D = 2048
KC = 16
NTOK = 1088
NALL = 1216
DFF = 5632
NKB = [4 * (i + 1) for i in range(8)]
NIT = 22
EPS = 1e-6
NEG = -1.0e30
CACHE_ROWS = [2560 * 128]


def own_blocks(q):
    return [q, 7 - q, 8 + q, 15 - q, 16 + q, 23 - q, 24 + q, 31 - q]


class Sched:
    ENGS = ("pe", "act", "dve", "pool", "sp")

    def __init__(self, nc):
        self.nc = nc
        self.ops = []
        self.last_w = {}
        self.readers = {}
        self.sb_lo = nc.sbuf_base
        self.sb_hi = nc.sbuf_top
        self.sb_ptr = self.sb_lo
        self.ntens = 0
        self.last_barrier = None
        self.peak = 0
        self.loop_mode = False

    def sb(self, name, shape, dtype, align=64):
        nbytes = int(np.prod(shape[1:])) * DT_SIZE[dtype]
        off = (self.sb_ptr + align - 1) // align * align
        assert off + nbytes <= self.sb_hi, f"SBUF overflow at {name}: need {off + nbytes - self.sb_lo} of {self.sb_hi - self.sb_lo}"
        self.sb_ptr = off + nbytes
        self.peak = max(self.peak, self.sb_ptr)
        self.ntens += 1
        return self.nc.alloc_sbuf_tensor_at(f"{name}_{self.ntens}", list(shape), dtype, offset=off)

    def mark(self):
        return self.sb_ptr

    def release(self, mark):
        self.sb_ptr = mark

    def add(self, eng, fn, reads=(), writes=(), dma=None):
        idx = len(self.ops)
        psr = [k for k in reads if k.startswith("ps") and k[2:].isdigit()]
        if psr:
            reads = [k for k in reads if k not in psr]
            writes = list(writes) + psr
        deps = set()
        for k in reads:
            if k in self.last_w:
                deps.add(self.last_w[k])
        for k in writes:
            if k in self.last_w:
                deps.add(self.last_w[k])
            for r in self.readers.get(k, ()):
                deps.add(r)
        if self.last_barrier is not None:
            deps.add(self.last_barrier)
        self.ops.append(dict(eng=eng, fn=fn, deps=deps, dma=dma, barrier=False))
        for k in writes:
            self.last_w[k] = idx
            self.readers[k] = []
        for k in reads:
            self.readers.setdefault(k, []).append(idx)
        return idx

    def barrier(self):
        idx = len(self.ops)
        self.ops.append(dict(eng=None, fn=None, deps=set(), dma=None, barrier=True))
        self.last_barrier = idx
        self.last_w = {}
        self.readers = {}

    def emit(self):
        import os
        nc = self.nc
        if os.environ.get("KMAXOPS"):
            self.ops = self.ops[:int(os.environ["KMAXOPS"])]
        ops = self.ops
        n = len(ops)
        last_eng = {}
        last_dma = {}
        for i, op in enumerate(ops):
            if op["barrier"]:
                op["deps"] = set(last_eng.values()) | set(last_dma.values())
                continue
            if op["dma"] is not None:
                last_dma[op["dma"]] = i
            else:
                last_eng[op["eng"]] = i
        fin_deps = set(last_eng.values()) | set(last_dma.values())
        need = [False] * n
        for i, op in enumerate(ops):
            for d in op["deps"]:
                need[d] = True
        for d in fin_deps:
            need[d] = True
        sem_names = set(self.ENGS) | set(op["dma"] for op in ops if op["dma"])
        stack = contextlib.ExitStack()
        if self.loop_mode:
            sems = {s: nc.alloc_semaphore(name=f"s_{s}") for s in sorted(sem_names)}
        else:
            sems = {s: stack.enter_context(nc.semaphore(f"s_{s}")) for s in sorted(sem_names)}
        cnt = {s: 0 for s in sem_names}
        ev = [None] * n
        for i, op in enumerate(ops):
            if op["barrier"]:
                continue
            if op["dma"] is not None:
                cnt[op["dma"]] += 16
                ev[i] = (op["dma"], cnt[op["dma"]])
            elif need[i]:
                cnt[op["eng"]] += 1
                ev[i] = (op["eng"], cnt[op["eng"]])
        bar_events = {}
        cur = {}
        dep_ev = [None] * n
        for i, op in enumerate(ops):
            eng = op["eng"]
            out = {}
            for d in op["deps"]:
                if ops[d]["barrier"]:
                    for s, v in bar_events[d].items():
                        if v > out.get(s, 0):
                            out[s] = v
                    continue
                if ops[d]["dma"] is None and ops[d]["eng"] == eng and eng == "pe":
                    continue
                s, v = ev[d]
                if ops[d]["dma"] is not None:
                    v = cur[s]
                if v > out.get(s, 0):
                    out[s] = v
            dep_ev[i] = out
            if op["barrier"]:
                bar_events[i] = out
            elif op["dma"] is not None:
                cur[op["dma"]] = ev[i][1]

        def dep_events(i, eng):
            return dep_ev[i]

        per_eng = {e: [] for e in self.ENGS}
        for i, op in enumerate(ops):
            if not op["barrier"]:
                per_eng[op["eng"]].append(i)
        fin_events = {}
        for d in fin_deps:
            s, v = ev[d]
            if v > fin_events.get(s, 0):
                fin_events[s] = v
        self.n_waits = 0

        def run_engine(ename, e):
            seen = {}
            for i in per_eng[ename]:
                op = ops[i]
                for s, v in sorted(dep_events(i, ename).items()):
                    if v > seen.get(s, 0):
                        e.wait_ge(sems[s], v)
                        seen[s] = v
                        self.n_waits += 1
                inst = op["fn"](e)
                if ev[i] is not None:
                    s, v = ev[i]
                    inst.then_inc(sems[s], 16 if op["dma"] is not None else 1)
            if ename == "sp":
                for s, v in sorted(fin_events.items()):
                    if v > seen.get(s, 0):
                        e.wait_ge(sems[s], v)

        with nc.allow_non_contiguous_dma(reason="small strided scratch/scalar transfers"), nc.Block() as block:
            @block.sync
            def _(e):
                run_engine("sp", e)

            @block.tensor
            def _(e):
                run_engine("pe", e)

            @block.scalar
            def _(e):
                run_engine("act", e)

            @block.vector
            def _(e):
                run_engine("dve", e)

            @block.gpsimd
            def _(e):
                run_engine("pool", e)
        stack.close()
        self.cnt = cnt


def build(stop_after=99, dbg=(), V=1):
    nc = bass.Bass("TRN2", target_bir_lowering=False)
    S = Sched(nc)
    S.loop_mode = V > 1
    octx = contextlib.ExitStack()
    if V > 1:
        vi = octx.enter_context(nc.Fori(0, V))
        octx.enter_context(nc.cleanup_on_exit())

    def finish():
        if io_copies_out:
            S.barrier()
            need_stage = {"y_own": 99, "pool_last": 3, "pool_s": 3}
            for ci_, (dst_, src_, nm_) in enumerate(io_copies_out):
                if stop_after < need_stage.get(nm_, 4):
                    continue
                add("sp", lambda e, dst_=dst_, src_=src_: e.dma_start(out=dst_, in_=src_), dma=f"io{ci_}")
        S.emit()
        if V > 1:
            nc.all_engine_barrier()
        octx.close()
        return nc, S

    def dr(name, shape, dt=F32, kind="ExternalInput"):
        return nc.dram_tensor(name, list(shape), dt, kind=kind).ap()

    io_copies_in = []
    io_copies_out = []

    def drv(name, shape, dt=F32, kind="ExternalInput", flat=None):
        if V == 1:
            return dr(name, shape, dt, kind)
        t = nc.dram_tensor(name, [V] + list(shape), dt, kind=kind).ap()
        st = nc.dram_tensor(name + "_st", list(shape), dt, kind="Internal").ap()
        dyn = t[vi]
        a, b_ = (st, dyn)
        if flat is not None:
            a = st.rearrange(flat[0], **flat[1])
            b_ = dyn.rearrange(flat[0], **flat[1])
        if kind == "ExternalInput":
            io_copies_in.append((a, b_, name))
        else:
            io_copies_out.append((b_, a, name))
        return st

    x_seq = drv("x_seq", [4096, D])
    x_own = drv("x_own", [NALL, D])
    c_tok = drv("c_tok", [P, D])
    meta_d = drv("meta", [P, 512])
    ptm_d = drv("ptm", [P, 256], I32)
    ck_d = dr("ck", [CACHE_ROWS[0], 512])
    cv_d = dr("cv", [CACHE_ROWS[0], 512])
    cik_d = dr("cik", [CACHE_ROWS[0], 64])
    stp_d = drv("stp", [240, 1024])
    w_ada = dr("w_ada", [D, 6 * D])
    b_ada = dr("b_ada", [1, 6 * D])
    g1_d = dr("g1", [D])
    g2_d = dr("g2", [D])
    gf_d = dr("gf", [1, D])
    w_in = dr("w_in", [D, 9296])
    w_pg = dr("w_pg", [4, 256, 256])
    psc_d = dr("psc", [1024])
    w_upp = dr("w_upp", [1024, D])
    w_upa = dr("w_upa", [D, D])
    w_out = dr("w_out", [D, D])
    w_f1 = dr("w_f1", [D, 2 * DFF])
    w_f2 = dr("w_f2", [DFF, D])
    EO = "ExternalOutput"
    y_own = drv("y_own", [NTOK, D], kind=EO)
    k_all = drv("k_all", [4096, 512], kind=EO)
    v_all = drv("v_all", [4096, 512], kind=EO)
    ki_all = drv("ki_all", [4096, 64], kind=EO, flat=("(a b) c -> a (b c)", dict(b=32)))
    ks_o = drv("ks_o", [64, 512], kind=EO)
    vs_o = drv("vs_o", [64, 512], kind=EO)
    kis_o = drv("kis_o", [64, 64], kind=EO, flat=("(a b) c -> a (b c)", dict(b=8)))
    pool_last = drv("pool_last", [15, 1024], kind=EO)
    pool_s = drv("pool_s", [240, 1024], kind=EO)
    IK = "Internal"
    dk = (lambda n: EO if n in dbg else IK)
    mods_tm = dr("mods_tm", [P, 6 * D], kind=dk("mods_tm"))
    modS_scr = dr("modS_scr", [64, P, 64], kind=dk("modS_scr"))
    qT_scr = dr("qT_scr", [16, P, NTOK], BF16, kind=dk("qT_scr"))
    qiT_scr = dr("qiT_scr", [8, P, NTOK], BF16, kind=dk("qiT_scr"))
    sga_scr = dr("sga_scr", [16, P, NTOK], kind=dk("sga_scr"))
    sgb_scr = dr("sgb_scr", [16, P, NTOK], kind=dk("sgb_scr"))
    po_scr = dr("po_scr", [8, P, NTOK], BF16, kind=dk("po_scr"))
    ao_scr = dr("ao_scr", [16, P, NTOK], BF16, kind=dk("ao_scr"))
    h_scr = dr("h_scr", [NTOK, D], kind=dk("h_scr"))
    aT_scr = dr("aT_scr", [44, P, NTOK], BF16, kind=dk("aT_scr"))
    yp_scr = dr("yp_scr", [NTOK, D], kind=dk("yp_scr"))
    wis_scr = dr("wis_scr", [64, 16], kind=dk("wis_scr"))
    pss_scr = dr("pss_scr", [64, 1024], kind=IK)
    dbg_scr = dr("dbg_scr", [P, 8192], kind=(EO if dbg else IK))

    ps = [nc.alloc_psum_tensor(f"bank{i}", [P, 512], F32) for i in range(8)]
    psk = [f"ps{i}" for i in range(8)]

    add = S.add

    ident = S.sb("ident", [P, P], F32)
    identb = S.sb("identb", [P, P], BF16)
    onesb = S.sb("onesb", [P, P], BF16)
    iota16 = S.sb("iota16", [P, 4224], I16)
    pow2 = S.sb("pow2", [P, NIT + 1], F32)
    modP = S.sb("modP", [P, 64], F32)
    A1P = S.sb("A1P", [P, 16], F32)
    A2P = S.sb("A2P", [P, 16], F32)
    g1T = S.sb("g1T", [P, 16], F32)
    g2T = S.sb("g2T", [P, 16], F32)
    qpos = S.sb("qpos", [P, 9], F32)
    absw = S.sb("absw", [P, 9, 16], F32)
    sgnw = S.sb("sgnw", [P, 9, 16], F32)
    m0 = S.mark()
    ci = S.sb("ci", [P, P], F32)
    ri = S.sb("ri", [P, P], F32)
    add("pool", lambda e: e.iota(ci[:], pattern=[[1, P]], base=0, channel_multiplier=0, allow_small_or_imprecise_dtypes=True), writes=["ci"])
    add("pool", lambda e: e.iota(ri[:], pattern=[[0, P]], base=0, channel_multiplier=1, allow_small_or_imprecise_dtypes=True), writes=["ri"])
    add("pool", lambda e: e.iota(iota16[:], pattern=[[1, 4224]], base=0, channel_multiplier=0, allow_small_or_imprecise_dtypes=True), writes=["iota16"])
    add("dve", lambda e: e.tensor_tensor(out=ident[:], in0=ci[:], in1=ri[:], op=ALU.is_equal), reads=["ci", "ri"], writes=["ident"])
    add("dve", lambda e: e.tensor_copy(out=identb[:], in_=ident[:]), reads=["ident"], writes=["identb"])
    add("pool", lambda e: e.memset(onesb[:], 1.0), writes=["onesb"])
    for k in range(NIT + 1):
        add("pool", lambda e, k=k: e.memset(pow2[:, k:k + 1], float(2.0 ** (-k))), writes=["pow2"])
    for ci_, (dst_, src_, nm_) in enumerate(io_copies_in):
        add("sp", lambda e, dst_=dst_, src_=src_: e.dma_start(out=dst_, in_=src_), writes=["io_" + nm_], dma=f"io{ci_}")
    if io_copies_in:
        S.barrier()
    add("sp", lambda e: e.dma_start(out=qpos[:], in_=meta_d[:, 0:9]), writes=["qpos"], dma="ld0")
    cvt = S.sb("cvt", [16, P], F32)

    def load_colvec(dst, dkey, src, n, sem, bank=7):
        add("sp", lambda e: e.dma_start(out=cvt[0:n, :], in_=src.rearrange("(c p) -> c p", p=P)), writes=["cvt"], dma=sem)
        add("pe", lambda e: e.transpose(ps[bank][:, 0:n], cvt[0:n, :], ident[0:n, 0:n]), reads=["cvt", "ident"], writes=[psk[bank]])
        add("dve", lambda e: e.tensor_copy(out=dst, in_=ps[bank][:, 0:n]), reads=[psk[bank]], writes=[dkey])

    load_colvec(g1T[:], "g1T", g1_d, 16, "ld1")
    load_colvec(g2T[:], "g2T", g2_d, 16, "ld2")

    def evac_alt(i):
        return "act" if i % 2 == 0 else "dve"

    def copy_op(eng, out, in_):
        if eng == "act":
            return lambda e: e.activation(out=out, in_=in_, func=AF.Identity)
        return lambda e: e.tensor_copy(out=out, in_=in_)

    m1 = S.mark()
    c_sb = S.sb("c_sb", [P, D], F32)
    scT = S.sb("scT", [P, KC, P], BF16)
    wbuf = [S.sb(f"wbuf{i}", [P, KC, 512], BF16) for i in range(2)]
    brow = [S.sb(f"brow{i}", [P, 512], F32) for i in range(2)]
    mst = [S.sb(f"mst{i}", [P, 512], F32) for i in range(2)]
    msS = [S.sb(f"msS{i}", [P, 4, 64], F32) for i in range(2)]
    add("sp", lambda e: e.dma_start(out=c_sb[:], in_=c_tok), writes=["c_sb"], dma="ld3")
    add("act", lambda e: e.activation(out=c_sb[:], in_=c_sb[:], func=AF.Silu), reads=["c_sb"], writes=["c_sb"])
    for c in range(KC):
        add("pe", lambda e, c=c: e.transpose(ps[c // 4][:, (c % 4) * P:(c % 4 + 1) * P], c_sb[:, c * P:(c + 1) * P], ident[:]),
            reads=["c_sb", "ident"], writes=[psk[c // 4]])
    for b in range(4):
        add(evac_alt(b), copy_op(evac_alt(b), scT[:, 4 * b:4 * b + 4, :].rearrange("p a n -> p (a n)"), ps[b][:, :]),
            reads=[psk[b]], writes=["scT"])

    def ada_load(g):
        s = g % 2
        add("pool", lambda e: e.dma_start(out=wbuf[s][:], in_=w_ada[:, g * 512:(g + 1) * 512].rearrange("(c p) n -> p c n", p=P)),
            writes=[f"wbuf{s}"], dma=f"w{s}")
        add("sp", lambda e: e.dma_start(out=brow[s][:], in_=b_ada[0:1, g * 512:(g + 1) * 512].partition_broadcast(P).rearrange("p o n -> p (o n)")),
            writes=[f"brow{s}"], dma=f"br{s}")

    ada_load(0)
    MI = {0: 0, 1: 1, 3: 2, 4: 3}
    for g in range(24):
        s = g % 2
        if g + 1 < 24:
            ada_load(g + 1)
        bank = 4 + s
        for c in range(KC):
            add("pe", lambda e, c=c, s=s, bank=bank: e.matmul(ps[bank][:, :], lhsT=scT[:, c, :], rhs=wbuf[s][:, c, :], start=(c == 0), stop=(c == KC - 1)),
                reads=["scT", f"wbuf{s}"], writes=[psk[bank]])
        add("dve", lambda e, s=s, bank=bank: e.tensor_tensor(out=mst[s][:], in0=ps[bank][:, :], in1=brow[s][:], op=ALU.add),
            reads=[psk[bank], f"brow{s}"], writes=[f"mst{s}"])
        add("sp", lambda e, s=s, g=g: e.dma_start(out=mods_tm[:, g * 512:(g + 1) * 512], in_=mst[s][:]), reads=[f"mst{s}"], dma=f"st{s}")
        m = g // 4
        if m in MI:
            mi = MI[m]
            tb = 6 + s
            for k in range(4):
                add("pe", lambda e, k=k, s=s, tb=tb: e.transpose(ps[tb][:, k * P:(k + 1) * P], mst[s][:, k * P:(k + 1) * P], ident[:]),
                    reads=[f"mst{s}", "ident"], writes=[psk[tb]])
            ch0 = mi * 16 + (g % 4) * 4
            add("dve", lambda e, tb=tb, ch0=ch0: e.tensor_copy(out=modP[:, ch0:ch0 + 4], in_=ps[tb][:, :].rearrange("p (a n) -> p a n", a=4)[:, :, 64]),
                reads=[psk[tb]], writes=["modP"])
            add("act", lambda e, tb=tb, s=s: e.activation(out=msS[s][:], in_=ps[tb][:, :].rearrange("p (a n) -> p a n", a=4)[:, :, 0:64], func=AF.Identity),
                reads=[psk[tb], "modP"], writes=[f"msS{s}"])
            add("sp", lambda e, s=s, ch0=ch0: e.dma_start(out=modS_scr[ch0:ch0 + 4].rearrange("a p n -> p a n"), in_=msS[s][:]),
                reads=[f"msS{s}"], dma=f"sm{s}")
    add("dve", lambda e: e.scalar_tensor_tensor(out=A1P[:], in0=modP[:, 16:32], scalar=1.0, in1=g1T[:], op0=ALU.add, op1=ALU.mult),
        reads=["modP", "g1T"], writes=["A1P"])
    add("dve", lambda e: e.scalar_tensor_tensor(out=A2P[:], in0=modP[:, 48:64], scalar=1.0, in1=g2T[:], op0=ALU.add, op1=ALU.mult),
        reads=["modP", "g2T"], writes=["A2P"])
    S.barrier()
    S.release(m1)
    if stop_after <= 1:
        return finish()

    def load_modS(mi_scale, mi_shift, gT, AS, shS, tag):
        add("sp", lambda e: e.dma_start(out=AS[:], in_=modS_scr[mi_scale * 16:(mi_scale + 1) * 16].rearrange("a p n -> p a n")),
            writes=[tag + "AS"], dma="ld4")
        add("sp", lambda e: e.dma_start(out=shS[:], in_=modS_scr[mi_shift * 16:(mi_shift + 1) * 16].rearrange("a p n -> p a n")),
            writes=[tag + "shS"], dma="ld5")
        for c in range(KC):
            add("dve", lambda e, c=c: e.tensor_scalar(out=AS[:, c, :], in0=AS[:, c, :], scalar1=1.0, scalar2=gT[:, c:c + 1], op0=ALU.add, op1=ALU.mult),
                reads=[tag + "AS"], writes=[tag + "AS"])

    class Front:
        def __init__(self, tag, AP_, shP_off, AS, shS, nbuf=2):
            self.tag = tag
            self.AP_ = AP_
            self.shoff = shP_off
            self.AS = AS
            self.shS = shS
            self.xb = [S.sb(f"{tag}xb{i}", [P, D], F32) for i in range(nbuf)]
            self.xs = S.sb(f"{tag}xs", [P, D], F32)
            self.junk = S.sb(f"{tag}junk", [P, D], BF16)
            self.ss = S.sb(f"{tag}ss", [P, 2], F32)
            self.tmpS = S.sb(f"{tag}tmpS", [P, 4, 64], F32)
            self.nbuf = nbuf
            self.i = 0

        def load(self, src, n, slot):
            add("sp", lambda e: e.dma_start(out=self.xb[slot][0:n, :], in_=src), writes=[f"{self.tag}xb{slot}"], dma=f"x{slot}")

        def norm(self, n, slot, src_key=None, xt=None):
            tag = self.tag
            xt = self.xb[slot] if xt is None else xt
            sk = f"{tag}xb{slot}" if src_key is None else src_key
            add("act", lambda e: e.activation(out=self.junk[0:n, :], in_=xt[0:n, :], func=AF.Square, accum_out=self.ss[0:n, 0:1]),
                reads=[sk], writes=[tag + "junk", tag + "ss"])
            add("act", lambda e: e.activation(out=self.ss[0:n, 1:2], in_=self.ss[0:n, 0:1], func=AF.Sqrt, scale=1.0 / D, bias=EPS_AP[0:n, :]),
                reads=[tag + "ss", "eps"], writes=[tag + "ss1"])
            add("dve", lambda e: e.reciprocal(out=self.ss[0:n, 1:2], in_=self.ss[0:n, 1:2]), reads=[tag + "ss1"], writes=[tag + "ss1"])
            add("act", lambda e: e.activation(out=self.xs[0:n, :], in_=xt[0:n, :], func=AF.Identity, scale=self.ss[0:n, 1:2]),
                reads=[sk, tag + "ss1"], writes=[tag + "xs"])

        def transpose_mod(self, n, dst, dst_key, sample, banks=(0, 1, 2, 3)):
            tag = self.tag
            for c in range(KC):
                b = banks[c // 4]
                add("pe", lambda e, c=c, b=b: e.transpose(ps[b][:, (c % 4) * P:(c % 4) * P + n], self.xs[0:n, c * P:(c + 1) * P], ident[0:n, 0:n]),
                    reads=[tag + "xs", "ident"], writes=[psk[b]])
            if not sample:
                for c in range(KC):
                    b = banks[c // 4]
                    src = ps[b][:, (c % 4) * P:(c % 4) * P + n]
                    if c % 2 == 0:
                        add("act", lambda e, c=c, src=src: e.activation(out=dst[:, c, 0:n], in_=src, func=AF.Identity,
                                                                         scale=self.AP_[:, c:c + 1], bias=modP[:, self.shoff + c:self.shoff + c + 1]),
                            reads=[psk[b], "A1P", "A2P", "modP"], writes=[dst_key])
                    else:
                        add("dve", lambda e, c=c, src=src: e.tensor_scalar(out=dst[:, c, 0:n], in0=src, scalar1=self.AP_[:, c:c + 1],
                                                                           scalar2=modP[:, self.shoff + c:self.shoff + c + 1], op0=ALU.mult, op1=ALU.add),
                            reads=[psk[b], "A1P", "A2P", "modP"], writes=[dst_key])
            else:
                for bi in range(4):
                    b = banks[bi]
                    src = ps[b][:, :].rearrange("p (a n) -> p a n", a=4)[:, :, 0:64]
                    add("dve", lambda e, bi=bi, src=src: e.tensor_tensor(out=self.tmpS[:], in0=src, in1=self.AS[:, 4 * bi:4 * bi + 4, :], op=ALU.mult),
                        reads=[psk[b], tag + "AS"], writes=[tag + "tmpS"])
                    add("pool", lambda e, bi=bi: e.tensor_tensor(out=dst[:, 4 * bi:4 * bi + 4, 0:64], in0=self.tmpS[:], in1=self.shS[:, 4 * bi:4 * bi + 4, :], op=ALU.add),
                        reads=[tag + "tmpS", tag + "shS"], writes=[dst_key])

    EPS_AP = S.sb("eps_ap", [P, 1], F32)
    add("pool", lambda e: e.memset(EPS_AP[:], EPS), writes=["eps"])

    m2 = S.mark()
    uT = S.sb("uT", [P, KC, NALL], BF16)
    pinT = S.sb("pinT", [P, 8, 8, 144], F32)
    pinS = S.sb("pinS", [P, 8, 16, 20], F32)
    m2b = S.mark()
    A1S = S.sb("A1S", [P, KC, 64], F32)
    sh1S = S.sb("sh1S", [P, KC, 64], F32)
    load_modS(1, 0, g1T, A1S, sh1S, "f1")
    fr = Front("f1", A1P, 0, A1S, sh1S)
    tiles = [(x_own[i * P:(i + 1) * P, :], P, i * P, False) for i in range(8)]
    tiles.append((x_own[1024:1088, :], 64, 1024, True))
    tiles.append((x_own[1088:1216, :], P, 1088, False))
    fr.load(tiles[0][0], tiles[0][1], 0)
    wwi = S.sb("wwi", [P, KC, 16], BF16)
    add("pool", lambda e: e.dma_start(out=wwi[:], in_=w_in[:, 5184:5200].rearrange("(c p) n -> p c n", p=P)), writes=["wwi"], dma="w0")
    for ti, (src, n, t0, smp) in enumerate(tiles):
        slot = ti % 2
        if ti + 1 < len(tiles):
            fr.load(tiles[ti + 1][0], tiles[ti + 1][1], (ti + 1) % 2)
        fr.norm(n, slot)
        fr.transpose_mod(n, uT[:, :, t0:t0 + n], "uT", smp)
        if ti < 9:
            for c in range(KC):
                add("pe", lambda e, c=c, t0=t0, n=n: e.matmul(ps[4][0:n, 0:16], lhsT=uT[:, c, t0:t0 + n], rhs=wwi[:, c, :], start=(c == 0), stop=(c == KC - 1)),
                    reads=["uT", "wwi"], writes=[psk[4]])
            add("act", lambda e, ti=ti, n=n: e.activation(out=absw[0:n, ti, :], in_=ps[4][0:n, 0:16], func=AF.Abs, scale=1.0 / 32.0),
                reads=[psk[4]], writes=["absw"])
            add("act", lambda e, ti=ti, n=n: e.activation(out=sgnw[0:n, ti, :], in_=ps[4][0:n, 0:16], func=AF.Sign),
                reads=[psk[4]], writes=["sgnw"])
            if ti == 8:
                wst = S.sb("wst", [P, 16], F32)
                add("dve", lambda e: e.tensor_copy(out=wst[0:64, :], in_=ps[4][0:64, 0:16]), reads=[psk[4]], writes=["wst"])
                add("sp", lambda e: e.dma_start(out=wis_scr, in_=wst[0:64, :]), reads=["wst"], dma="st2")
    S.barrier()
    S.release(m2b)
    wb2 = [S.sb(f"wb2_{i}", [P, KC, 512], BF16) for i in range(2)]
    stg = [S.sb(f"stg{i}", [P, NTOK], F32) for i in range(2)]
    stgb = [S.sb(f"stgb{i}", [P, NTOK], BF16) for i in range(2)]
    groups = []
    for gi in range(2):
        groups.append(("pool", gi * 512, gi * 4))
    for gi in range(4):
        groups.append(("q", 1024 + gi * 512, gi * 4))
    for gi in range(2):
        groups.append(("qi", 4096 + gi * 512, gi * 4))
    for gi in range(4):
        groups.append(("ga", 5200 + gi * 512, gi * 4))
    for gi in range(4):
        groups.append(("gb", 7248 + gi * 512, gi * 4))

    def f_load(gi):
        kind, col0, ch0 = groups[gi]
        s = gi % 2
        add("pool", lambda e: e.dma_start(out=wb2[s][:], in_=w_in[:, col0:col0 + 512].rearrange("(c p) n -> p c n", p=P)),
            writes=[f"wb2_{s}"], dma=f"w{s}")

    f_load(0)
    cc_global = 0
    for gi, (kind, col0, ch0) in enumerate(groups):
        s = gi % 2
        if gi + 1 < len(groups):
            f_load(gi + 1)
        for k in range(4):
            ch = ch0 + k
            bset = (cc_global % 2) * 3
            cc_global += 1
            tgs = [(0, 512), (512, 512), (1024, 192 if kind == "pool" else 64)]
            for bi, (t0, tn) in enumerate(tgs):
                bank = bset + bi
                for c in range(KC):
                    add("pe", lambda e, c=c, s=s, k=k, t0=t0, tn=tn, bank=bank: e.matmul(ps[bank][:, 0:tn], lhsT=wb2[s][:, c, k * P:(k + 1) * P],
                                                                                   rhs=uT[:, c, t0:t0 + tn], start=(c == 0), stop=(c == KC - 1)),
                        reads=["uT", f"wb2_{s}"], writes=[psk[bank]])
            ss_ = ch % 2
            if kind == "pool":
                for bi in range(2):
                    add(evac_alt(bi), copy_op(evac_alt(bi), pinT[:, ch, 4 * bi:4 * bi + 4, 16:144], ps[bset + bi][:, :].rearrange("p (a n) -> p a n", a=4)),
                        reads=[psk[bset + bi]], writes=["pinT"])
                add("act", copy_op("act", pinS[:, ch, :, 16:20], ps[bset + 2][:, 0:64].rearrange("p (a n) -> p a n", n=4)), reads=[psk[bset + 2]], writes=["pinS"])
                add("dve", copy_op("dve", pinT[:, ch, :, 0:16], ps[bset + 2][:, 64:192].rearrange("p (a n) -> p a n", n=16)), reads=[psk[bset + 2]], writes=["pinT"])
            elif kind in ("q", "qi"):
                dstt = qT_scr if kind == "q" else qiT_scr
                for bi, (t0, tn) in enumerate(tgs):
                    add(evac_alt(bi), copy_op(evac_alt(bi), stgb[ss_][:, t0:t0 + tn], ps[bset + bi][:, 0:tn]), reads=[psk[bset + bi]], writes=[f"stgb{ss_}"])
                add("sp", lambda e, ch=ch, ss_=ss_, dstt=dstt: e.dma_start(out=dstt[ch], in_=stgb[ss_][:]), reads=[f"stgb{ss_}"], dma=f"st{ss_}")
            else:
                dstt = sga_scr if kind == "ga" else sgb_scr
                for bi, (t0, tn) in enumerate(tgs):
                    add("act", lambda e, bi=bi, t0=t0, tn=tn, ss_=ss_, bset=bset: e.activation(out=stg[ss_][:, t0:t0 + tn], in_=ps[bset + bi][:, 0:tn], func=AF.Sigmoid),
                        reads=[psk[bset + bi]], writes=[f"stg{ss_}"])
                add("sp", lambda e, ch=ch, ss_=ss_, dstt=dstt: e.dma_start(out=dstt[ch], in_=stg[ss_][:]), reads=[f"stg{ss_}"], dma=f"st{ss_ + 2}")
    S.barrier()
    S.release(m2b)
    if stop_after <= 2:
        return finish()

    dT = S.sb("dT", [P, 8, NTOK], BF16)
    wpg = S.sb("wpg", [P, 8, 256], BF16)
    pscT = S.sb("pscT", [P, 8], F32)
    tA = S.sb("tA", [P, 2, 8, 144], F32)
    tB = S.sb("tB", [P, 2, 8, 144], F32)
    sA = S.sb("sA", [P, 2, 16, 20], F32)
    sB = S.sb("sB", [P, 2, 16, 20], F32)
    hm = S.sb("hm", [P, P], F32)
    invc = S.sb("invc", [P, 4, P], F32)
    stt = S.sb("stt", [P, 1024], F32)
    pstg = S.sb("pstg", [P, 1024], F32)
    postg = [S.sb(f"postg{i}", [P, NTOK], BF16) for i in range(2)]
    add("pool", lambda e: e.dma_start(out=wpg[:], in_=w_pg.rearrange("g (a p) n -> p (g a) n", p=P)), writes=["wpg"], dma="w0")
    load_colvec(pscT[:], "pscT", psc_d, 8, "ld0")
    add("sp", lambda e: e.dma_start(out=hm[:], in_=meta_d[:, 144:272]), writes=["hm"], dma="ld1")
    add("sp", lambda e: e.dma_start(out=invc[:, 0, :], in_=meta_d[:, 16:144]), writes=["invc"], dma="ld2")
    for g in (3, 2, 1, 0):
        w = float(2 ** (g + 1))
        add("dve", lambda e, g=g, w=w: e.tensor_scalar(out=invc[:, g, :], in0=invc[:, 0, :], scalar1=1.0, scalar2=w, op0=ALU.add, op1=ALU.min),
            reads=["invc"], writes=["invc"])
        add("dve", lambda e, g=g: e.reciprocal(out=invc[:, g, :], in_=invc[:, g, :]), reads=["invc"], writes=["invc"])
    for ch in range(8):
        add("pool", lambda e, ch=ch: e.tensor_tensor(out=pinT[:, ch, :, 0:16], in0=pinT[:, ch, :, 0:16], in1=hm[:, :].rearrange("p (a n) -> p a n", n=16), op=ALU.mult),
            reads=["pinT", "hm"], writes=["pinT"])
    for half in range(2):
        add("sp", lambda e, half=half: e.dma_start(out=stt[0:120, :], in_=stp_d[half * 120:(half + 1) * 120, :]), writes=["stt"], dma="ld3")
        for ch in range(8):
            b = ch // 4
            add("pe", lambda e, ch=ch, b=b: e.transpose(ps[b][:, (ch % 4) * P:(ch % 4) * P + 120], stt[0:120, ch * P:(ch + 1) * P], ident[0:120, 0:120]),
                reads=["stt", "ident"], writes=[psk[b]])
        for ch in range(8):
            b = ch // 4
            add(evac_alt(ch), copy_op(evac_alt(ch), pinS[:, ch, half * 8:(half + 1) * 8, 1:16], ps[b][:, (ch % 4) * P:(ch % 4) * P + 120].rearrange("p (a n) -> p a n", n=15)),
                reads=[psk[b]], writes=["pinS"])
    add("sp", lambda e: e.dma_start(out=pool_s.rearrange("(s r) c -> s r c", r=15)[:, 0:11, :], in_=stp_d.rearrange("(s r) c -> s r c", r=15)[:, 4:15, :]), dma="st0")
    for ch in range(8):
        b = ch // 4
        add("pe", lambda e, ch=ch, b=b: e.transpose(ps[b][:, (ch % 4) * P:(ch % 4 + 1) * P], pinT[:, ch, 7, 16:144], ident[:]),
            reads=["pinT", "ident"], writes=[psk[b]])
    for b in range(2):
        add(evac_alt(b), copy_op(evac_alt(b), pstg[:, b * 512:(b + 1) * 512], ps[b][:, :]), reads=[psk[b]], writes=["pstg"])
    add("sp", lambda e: e.dma_start(out=pool_last, in_=pstg[113:128, :]), reads=["pstg"], dma="st1")
    pstg2 = S.sb("pstg2", [P, 1024], F32)
    pinSc = S.sb("pinSc", [P, 8, 64], F32)
    for ch in range(8):
        add("dve", lambda e, ch=ch: e.tensor_copy(out=pinSc[:, ch, :].rearrange("p (s t) -> p s t", t=4), in_=pinS[:, ch, :, 16:20]), reads=["pinS"], writes=["pinSc"])
    for ch in range(8):
        b = 2 + ch // 4
        add("pe", lambda e, ch=ch, b=b: e.transpose(ps[b][0:64, (ch % 4) * P:(ch % 4 + 1) * P], pinSc[:, ch, :], ident[:]),
            reads=["pinSc", "ident"], writes=[psk[b]])
    for b in range(2):
        add(evac_alt(b), copy_op(evac_alt(b), pstg2[0:64, b * 512:(b + 1) * 512], ps[2 + b][0:64, :]), reads=[psk[2 + b]], writes=["pstg2"])
    add("sp", lambda e: e.dma_start(out=pss_scr, in_=pstg2[0:64, :]), reads=["pstg2"], writes=["pss_scr"], dma="st2")
    add("sp", lambda e: e.dma_start(out=pool_s.rearrange("(s r) c -> s r c", r=15)[:, 11:15, :], in_=pss_scr.rearrange("(s t) c -> s t c", t=4)),
        reads=["pss_scr"], dma="st3")
    for g in range(4):
        w = 2 ** (g + 1)
        srcP = pinT[:, 2 * g:2 * g + 2, :, :]
        srcS = pinS[:, 2 * g:2 * g + 2, :, :]
        curP, curS = srcP, srcS
        keyP, keyS = "pinT", "pinS"
        sh = 1
        bufs = [(tA, sA, "tA", "sA"), (tB, sB, "tB", "sB")]
        bi = 0
        while sh < w:
            oP, oS, kP, kS = bufs[bi]
            lo = 16 - w + 2 * sh
            add("dve", lambda e, oP=oP, curP=curP, sh=sh, lo=lo: e.tensor_tensor(out=oP[:, :, :, lo:144], in0=curP[:, :, :, lo:144], in1=curP[:, :, :, lo - sh:144 - sh], op=ALU.add),
                reads=[keyP], writes=[kP])
            add("pool", lambda e, oS=oS, curS=curS, sh=sh, lo=lo: e.tensor_tensor(out=oS[:, :, :, lo:20], in0=curS[:, :, :, lo:20], in1=curS[:, :, :, lo - sh:20 - sh], op=ALU.add),
                reads=[keyS], writes=[kS])
            curP, curS, keyP, keyS = oP, oS, kP, kS
            sh *= 2
            bi ^= 1
        for cc in range(2):
            dv = dT[:, 2 * g + cc, 0:1024].rearrange("p (t n) -> p t n", n=P)
            add("dve", lambda e, curP=curP, srcP=srcP, dv=dv, w=w, cc=cc: e.scalar_tensor_tensor(out=dv[:, 1:8, :], in0=curP[:, cc, 1:8, 16:144], scalar=1.0 / w, in1=srcP[:, cc, 1:8, 16:144],
                                                                                        op0=ALU.mult, op1=ALU.subtract),
                reads=[keyP, "pinT"], writes=["dT"])
            add("dve", lambda e, curP=curP, cc=cc, g=g: e.tensor_tensor(out=curP[:, cc, 0, 16:144], in0=curP[:, cc, 0, 16:144], in1=invc[:, g, :], op=ALU.mult),
                reads=[keyP, "invc"], writes=[keyP])
            add("dve", lambda e, curP=curP, srcP=srcP, cc=cc, g=g: e.tensor_tensor(out=dT[:, 2 * g + cc, 0:P], in0=curP[:, cc, 0, 16:144], in1=srcP[:, cc, 0, 16:144], op=ALU.subtract),
                reads=[keyP, "pinT"], writes=["dT"])
            dvs = dT[:, 2 * g + cc, 1024:1088].rearrange("p (s t) -> p s t", t=4)
            add("dve", lambda e, curS=curS, srcS=srcS, dvs=dvs, w=w, cc=cc: e.scalar_tensor_tensor(out=dvs, in0=curS[:, cc, :, 16:20], scalar=1.0 / w, in1=srcS[:, cc, :, 16:20], op0=ALU.mult, op1=ALU.subtract),
                reads=[keyS, "pinS"], writes=["dT"])
    for ec in range(8):
        g = ec // 2
        bset = (ec % 2) * 3
        tgs = [(0, 512), (512, 512), (1024, 64)]
        for bi, (t0, tn) in enumerate(tgs):
            for cc in range(2):
                add("pe", lambda e, g=g, cc=cc, ec=ec, t0=t0, tn=tn, bank=bset + bi: e.matmul(ps[bank][:, 0:tn], lhsT=wpg[:, 2 * g + cc, (ec % 2) * P:(ec % 2 + 1) * P],
                                                                                        rhs=dT[:, 2 * g + cc, t0:t0 + tn], start=(cc == 0), stop=(cc == 1)),
                    reads=["dT", "wpg"], writes=[psk[bset + bi]])
        s_ = ec % 2
        for bi, (t0, tn) in enumerate(tgs):
            add("act", lambda e, ec=ec, bi=bi, t0=t0, tn=tn, s_=s_, bset=bset: e.activation(out=postg[s_][:, t0:t0 + tn], in_=ps[bset + bi][:, 0:tn], func=AF.Identity, scale=pscT[:, ec:ec + 1]),
                reads=[psk[bset + bi], "pscT"], writes=[f"postg{s_}"])
        add("sp", lambda e, ec=ec, s_=s_: e.dma_start(out=po_scr[ec], in_=postg[s_][:]), reads=[f"postg{s_}"], dma=f"st{3 + s_}")
    S.barrier()
    S.release(m2)
    if stop_after <= 3:
        return finish()

    m4 = S.mark()
    KTn = S.sb("KTn", [P, 4, 64], BF16)
    Vn = S.sb("Vn", [P, 512], BF16)
    kiTn = S.sb("kiTn", [P, 64], BF16)
    acc = S.sb("acc", [P, 4096], F32)
    pen = S.sb("pen", [P, 4096], BF16)
    mask01 = S.sb("mask01", [P, 4096], BF16)
    bs = S.sb("bs", [P, 8], F32)
    wtab = S.sb("wtab", [P, NIT + 1], F32)
    wtab2 = S.sb("wtab2", [P, NIT + 1], F32)
    m4big = S.mark()
    KT = S.sb("KT", [P, 4, 4096], BF16)
    Vs = S.sb("Vs", [P, 32, 512], BF16)
    kiT2 = S.sb("kiT2", [P, 4096], BF16)
    m4b = S.mark()
    A1S = S.sb("A1Sb", [P, KC, 64], F32)
    sh1S = S.sb("sh1Sb", [P, KC, 64], F32)
    load_modS(1, 0, g1T, A1S, sh1S, "f4")
    fr = Front("f4", A1P, 0, A1S, sh1S)
    wkvi = S.sb("wkvi", [P, KC, 1088], BF16)
    uTt = S.sb("uTt", [P, KC, P], BF16)
    kvst = [S.sb(f"kvst{i}", [P, 1088], F32) for i in range(2)]
    kb16 = S.sb("kb16", [P, 640], BF16)
    for gi in range(2):
        add("pool", lambda e, gi=gi: e.dma_start(out=wkvi[:, :, gi * 512:(gi + 1) * 512], in_=w_in[:, 3072 + gi * 512:3072 + (gi + 1) * 512].rearrange("(c p) n -> p c n", p=P)),
            writes=[f"wkvi{gi}"], dma=f"w{gi}")
    add("pool", lambda e: e.dma_start(out=wkvi[:, :, 1024:1088], in_=w_in[:, 5120:5184].rearrange("(c p) n -> p c n", p=P)), writes=["wkvi2"], dma="w2")
    tilesA = [(x_seq[i * P:(i + 1) * P, :], P, False) for i in range(32)]
    tilesA.append((x_own[1024:1088, :], 64, True))
    fr.load(tilesA[0][0], tilesA[0][1], 0)
    psb7 = ps[7][:, :].bitcast(BF16)
    for ti, (src, n, smp) in enumerate(tilesA):
        slot = ti % 2
        if ti + 1 < len(tilesA):
            fr.load(tilesA[ti + 1][0], tilesA[ti + 1][1], (ti + 1) % 2)
        fr.norm(n, slot)
        fr.transpose_mod(n, uTt[:, :, 0:n], "uTt", smp)
        for wi_, (c0, cn, bank) in enumerate(((0, 512, 4), (512, 512, 5), (1024, 64, 6))):
            for c in range(KC):
                add("pe", lambda e, c=c, c0=c0, cn=cn, bank=bank, n=n: e.matmul(ps[bank][0:n, 0:cn], lhsT=uTt[:, c, 0:n], rhs=wkvi[:, c, c0:c0 + cn], start=(c == 0), stop=(c == KC - 1)),
                    reads=["uTt", f"wkvi{wi_}"], writes=[psk[bank]])
        ks_ = ti % 2
        add("act", lambda e, ks_=ks_, n=n: e.activation(out=kvst[ks_][0:n, 0:512], in_=ps[4][0:n, :], func=AF.Identity), reads=[psk[4]], writes=[f"kvst{ks_}"])
        add("act", lambda e, ks_=ks_, n=n: e.activation(out=kvst[ks_][0:n, 512:1024], in_=ps[5][0:n, :], func=AF.Identity), reads=[psk[5]], writes=[f"kvst{ks_}"])
        add("act", lambda e, ks_=ks_, n=n: e.activation(out=kvst[ks_][0:n, 1024:1088], in_=ps[6][0:n, 0:64], func=AF.Identity), reads=[psk[6]], writes=[f"kvst{ks_}"])
        add("dve", lambda e, n=n: e.tensor_copy(out=kb16[0:n, 0:512], in_=ps[4][0:n, :]), reads=[psk[4]], writes=["kb16"])
        vdst = Vs[0:n, ti, :] if not smp else Vn[0:n, :]
        add("dve", lambda e, n=n, vdst=vdst: e.tensor_copy(out=vdst, in_=ps[5][0:n, :]), reads=[psk[5]], writes=["Vs"])
        add("dve", lambda e, n=n: e.tensor_copy(out=kb16[0:n, 512:576], in_=ps[6][0:n, 0:64]), reads=[psk[6]], writes=["kb16"])
        add("dve", lambda e, n=n: e.tensor_copy(out=kb16[0:n, 576:640], in_=ps[6][0:n, 0:64]), reads=[psk[6]], writes=["kb16"])
        if not smp:
            r0 = ti * P
            add("sp", lambda e, ks_=ks_, r0=r0: e.dma_start(out=k_all[r0:r0 + P, :], in_=kvst[ks_][:, 0:512]), reads=[f"kvst{ks_}"], dma=f"st{ks_}")
            add("sp", lambda e, ks_=ks_, r0=r0: e.dma_start(out=v_all[r0:r0 + P, :], in_=kvst[ks_][:, 512:1024]), reads=[f"kvst{ks_}"], dma=f"st{ks_}")
            add("sp", lambda e, ks_=ks_, r0=r0: e.dma_start(out=ki_all[r0:r0 + P, :], in_=kvst[ks_][:, 1024:1088]), reads=[f"kvst{ks_}"], dma=f"st{ks_}")
        else:
            add("sp", lambda e, ks_=ks_: e.dma_start(out=ks_o, in_=kvst[ks_][0:64, 0:512]), reads=[f"kvst{ks_}"], dma=f"st{ks_}")
            add("sp", lambda e, ks_=ks_: e.dma_start(out=vs_o, in_=kvst[ks_][0:64, 512:1024]), reads=[f"kvst{ks_}"], dma=f"st{ks_}")
            add("sp", lambda e, ks_=ks_: e.dma_start(out=kis_o, in_=kvst[ks_][0:64, 1024:1088]), reads=[f"kvst{ks_}"], dma=f"st{ks_}")
        for j in range(5):
            add("pe", lambda e, j=j, n=n: e.transpose(psb7[:, j * P:j * P + n], kb16[0:n, j * P:(j + 1) * P], identb[0:n, 0:n]),
                reads=["kb16", "identb"], writes=[psk[7]])
        kc0 = ti * P
        ktd = KT[:, :, kc0:kc0 + n] if not smp else KTn[:, :, 0:n]
        kid = kiT2[:, kc0:kc0 + n] if not smp else kiTn[:, 0:n]
        add("act", lambda e, n=n, ktd=ktd: e.activation(out=ktd, in_=psb7[:, 0:512].rearrange("p (j n) -> p j n", j=4)[:, :, 0:n], func=AF.Identity),
            reads=[psk[7]], writes=["KT"])
        add("dve", lambda e, n=n, kid=kid: e.tensor_copy(out=kid, in_=psb7[:, 512:512 + n]), reads=[psk[7]], writes=["kiT2"])
    S.barrier()
    S.release(m4b)
    if stop_after <= 4:
        return finish()

    maskT = S.sb("maskT", [P, 32, P], BF16)
    qTt = [S.sb(f"qTt{i}", [P, 16, P], BF16) for i in range(2)]
    qiTt = [S.sb(f"qiTt{i}", [P, 8, P], BF16) for i in range(2)]
    Rb = [S.sb(f"Rb{i}", [P, 512], F32) for i in range(3)]
    Eb = [S.sb(f"Eb{i}", [P, 512], BF16) for i in range(2)]
    Pb = [S.sb(f"Pb{i}", [P, 512], BF16) for i in range(3)]
    rden = S.sb("rden", [P, 512], F32)
    aost = [S.sb(f"aost{i}", [P, 16, P], BF16) for i in range(2)]
    SCALE = float(128 ** -0.5)

    def topk(n, Sw, acc_t, pen_t, mask_t, junk_t, qp_ap):
        add("dve", lambda e: e.tensor_reduce(out=bs[0:n, 0:1], in_=acc_t[0:n, 0:Sw], axis=AX.X, op=ALU.max, apply_absolute_value=True),
            reads=["acc"], writes=["bsM"])
        add("dve", lambda e: e.tensor_scalar(out=bs[0:n, 0:1], in0=bs[0:n, 0:1], scalar1=1.001, scalar2=1e-6, op0=ALU.mult, op1=ALU.add), reads=["bsM"], writes=["bsM"])
        add("dve", lambda e: e.tensor_scalar(out=wtab[0:n, :], in0=pow2[0:n, :], scalar1=bs[0:n, 0:1], scalar2=None, op0=ALU.mult), reads=["bsM", "pow2"], writes=["wtab"])
        add("dve", lambda e: e.tensor_scalar(out=wtab2[0:n, :], in0=wtab[0:n, :], scalar1=2.0, scalar2=None, op0=ALU.mult), reads=["wtab"], writes=["wtab2"])
        add("dve", lambda e: e.tensor_scalar(out=pen_t[0:n, 0:Sw], in0=iota16[0:n, 0:Sw], scalar1=qp_ap, scalar2=NEG, op0=ALU.is_gt, op1=ALU.mult),
            reads=["iota16", "qpos"], writes=["pen"])
        add("dve", lambda e: e.tensor_tensor(out=acc_t[0:n, 0:Sw], in0=acc_t[0:n, 0:Sw], in1=pen_t[0:n, 0:Sw], op=ALU.add), reads=["acc", "pen"], writes=["acc"])
        add("dve", lambda e: e.tensor_scalar(out=bs[0:n, 1:2], in0=bs[0:n, 0:1], scalar1=0.0, scalar2=None, op0=ALU.mult), reads=["bsM"], writes=["bsmid"])
        for k in range(NIT):
            add("dve", lambda e: e.tensor_scalar(out=junk_t[0:n, 0:Sw], in0=acc_t[0:n, 0:Sw], scalar1=bs[0:n, 1:2], scalar2=None, op0=ALU.is_ge, op1=ALU.add, accum_out=bs[0:n, 2:3]),
                reads=["acc", "bsmid"], writes=["pen", "bscnt"])
            add("dve", lambda e, k=k: e.tensor_scalar(out=bs[0:n, 3:4], in0=bs[0:n, 2:3], scalar1=255.5, scalar2=wtab2[0:n, k + 1:k + 2], op0=ALU.is_ge, op1=ALU.mult),
                reads=["bscnt", "wtab2"], writes=["bsf"])
            add("dve", lambda e, k=k: e.scalar_tensor_tensor(out=bs[0:n, 1:2], in0=bs[0:n, 1:2], scalar=wtab[0:n, k + 1:k + 2], in1=bs[0:n, 3:4], op0=ALU.subtract, op1=ALU.add),
                reads=["bsmid", "bsf", "wtab"], writes=["bsmid"])
        add("dve", lambda e: e.tensor_tensor(out=bs[0:n, 4:5], in0=bs[0:n, 1:2], in1=wtab[0:n, NIT:NIT + 1], op=ALU.subtract), reads=["bsmid", "wtab"], writes=["bsthr"])
        add("dve", lambda e: e.tensor_scalar(out=mask_t[0:n, 0:Sw], in0=acc_t[0:n, 0:Sw], scalar1=bs[0:n, 4:5], scalar2=None, op0=ALU.is_ge), reads=["acc", "bsthr"], writes=["mask01"])

    def q_load(i):
        s = i % 2
        add("sp", lambda e: e.dma_start(out=qTt[s][:], in_=qT_scr[:, :, i * P:(i + 1) * P].rearrange("c p n -> p c n")), writes=[f"qTt{s}"], dma=f"ld{s}")
        add("sp", lambda e: e.dma_start(out=qiTt[s][:], in_=qiT_scr[:, :, i * P:(i + 1) * P].rearrange("c p n -> p c n")), writes=[f"qiTt{s}"], dma=f"ld{2 + s}")

    with nc.allow_non_contiguous_dma(reason="256B runs for per-tile q loads"):
        q_load(0)
        rcount = 0
        pcount = 0
        for i in range(8):
            s = i % 2
            nkb = NKB[i]
            Sw = nkb * P
            if i + 1 < 8:
                q_load(i + 1)
            add("pool", lambda e, Sw=Sw: e.memset(acc[:, 0:Sw], 0.0), writes=["acc"])
            for kc in range(nkb // 4):
                for h in range(16):
                    c, hp = h // 2, h % 2
                    bank = rcount % 3
                    rb = rcount % 3
                    rcount += 1
                    add("pe", lambda e, c=c, hp=hp, kc=kc, s=s, bank=bank: e.matmul(ps[bank][:, :], lhsT=qiTt[s][hp * 64:(hp + 1) * 64, c, :],
                                                                                rhs=kiT2[hp * 64:(hp + 1) * 64, kc * 512:(kc + 1) * 512], start=True, stop=True),
                        reads=[f"qiTt{s}", "kiT2"], writes=[psk[bank]])
                    add("act", lambda e, bank=bank, rb=rb, i=i, h=h: e.activation(out=Rb[rb][:], in_=ps[bank][:, :], func=AF.Relu, scale=absw[:, i, h:h + 1]),
                        reads=[psk[bank], "absw"], writes=[f"Rb{rb}"])
                    add("dve", lambda e, rb=rb, i=i, h=h, kc=kc: e.scalar_tensor_tensor(out=acc[:, kc * 512:(kc + 1) * 512], in0=Rb[rb][:], scalar=sgnw[:, i, h:h + 1],
                                                                                       in1=acc[:, kc * 512:(kc + 1) * 512], op0=ALU.mult, op1=ALU.add),
                        reads=[f"Rb{rb}", "sgnw", "acc"], writes=["acc"])
            topk(P, Sw, acc, pen, mask01, pen, qpos[:, i:i + 1])
            for kb in range(nkb):
                bank = 3 if (kb // 8) % 2 == 0 else 4
                pb_ = ps[bank][:, :].bitcast(BF16)
                add("pe", lambda e, kb=kb, pb_=pb_: e.transpose(pb_[:, (kb % 8) * P:(kb % 8 + 1) * P], mask01[:, kb * P:(kb + 1) * P], identb[:]),
                    reads=["mask01", "identb"], writes=[psk[bank]])
                if kb % 8 == 7 or kb == nkb - 1:
                    k0 = (kb // 8) * 8
                    nk = kb - k0 + 1
                    add("act", lambda e, k0=k0, nk=nk, pb_=pb_: e.activation(out=maskT[:, k0:k0 + nk, :].rearrange("p a n -> p (a n)"), in_=pb_[:, 0:nk * P], func=AF.Identity),
                        reads=[psk[bank]], writes=["maskT"])
            for j in range(4):
                for kb in range(nkb):
                    sb_ = 3 + (pcount % 2)
                    eb = pcount % 2
                    pbi = pcount % 3
                    pcount += 1
                    add("pe", lambda e, j=j, kb=kb, s=s, sb_=sb_: e.matmul(ps[sb_][:, :], lhsT=KT[:, j, kb * P:(kb + 1) * P], rhs=qTt[s][:, 4 * j:4 * j + 4, :].rearrange("p a n -> p (a n)"),
                                                                      start=True, stop=True),
                        reads=["KT", f"qTt{s}"], writes=[psk[sb_]])
                    add("act", lambda e, sb_=sb_, eb=eb: e.activation(out=Eb[eb][:], in_=ps[sb_][:, :], func=AF.Exp, scale=SCALE), reads=[psk[sb_]], writes=[f"Eb{eb}"])
                    add("dve", lambda e, eb=eb, pbi=pbi, kb=kb: e.tensor_tensor(out=Pb[pbi][:].rearrange("p (a n) -> p a n", a=4), in0=Eb[eb][:].rearrange("p (a n) -> p a n", a=4),
                                                                                in1=maskT[:, kb:kb + 1, :].to_broadcast([P, 4, P]), op=ALU.mult),
                        reads=[f"Eb{eb}", "maskT"], writes=[f"Pb{pbi}"])
                    add("pe", lambda e, j=j, kb=kb, pbi=pbi, nkb=nkb: e.matmul(ps[5][:, :], lhsT=Vs[:, kb, j * P:(j + 1) * P], rhs=Pb[pbi][:], start=(kb == 0), stop=(kb == nkb - 1)),
                        reads=["Vs", f"Pb{pbi}"], writes=[psk[5]])
                    add("pe", lambda e, kb=kb, pbi=pbi, nkb=nkb: e.matmul(ps[6][:, :], lhsT=onesb[:], rhs=Pb[pbi][:], start=(kb == 0), stop=(kb == nkb - 1)),
                        reads=["onesb", f"Pb{pbi}"], writes=[psk[6]])
                add("dve", lambda e: e.reciprocal(out=rden[:], in_=ps[6][:, :]), reads=[psk[6]], writes=["rden"])
                add("dve", lambda e, j=j, s=s: e.tensor_tensor(out=aost[s][:, 4 * j:4 * j + 4, :].rearrange("p a n -> p (a n)"), in0=ps[5][:, :], in1=rden[:], op=ALU.mult),
                    reads=[psk[5], "rden"], writes=[f"aost{s}"])
            add("sp", lambda e, i=i, s=s: e.dma_start(out=ao_scr[:, :, i * P:(i + 1) * P].rearrange("c p n -> p c n"), in_=aost[s][:]), reads=[f"aost{s}"], dma=f"st{s}")
    S.barrier()
    if stop_after <= 5:
        return finish()

    S.release(m4big)
    ptb = S.sb("ptb", [P, 256], I32)
    ridx = S.sb("ridx", [P, 256], I32)
    rowid = S.sb("rowid", [P, 1], F32)
    W2 = [S.sb(f"W2_{hp}", [32, 16], F32) for hp in range(2)]
    Tsel = S.sb("Tsel", [32, 4], F32)
    SelW = [S.sb(f"SelW{hp}", [32, 16, 64], F32) for hp in range(2)]
    RepSel = S.sb("RepSel", [64, 16, 64], BF16)
    BDj = S.sb("BDj", [64, 4], F32)
    qTs = S.sb("qTs", [P, 16, 64], BF16)
    qiTs = S.sb("qiTs", [P, 8, 64], BF16)
    ikp = [S.sb(f"ikp{i}", [P, 16, P], BF16) for i in range(2)]
    kiTs = S.sb("kiTs", [P, 2048], BF16)
    Rs = [[S.sb(f"Rs{i}_{hp}", [32, 512], F32) for hp in range(2)] for i in range(2)]
    mbias = S.sb("mbias", [64, 2052], BF16)
    cf = S.sb("cf", [64, 1024], F32)
    rf = S.sb("rf", [64, 1024], F32)
    add("sp", lambda e: e.dma_start(out=ptb[:], in_=ptm_d), writes=["ptb"], dma="ld0")
    add("pool", lambda e: e.iota(rowid[:], pattern=[[0, 1]], base=0, channel_multiplier=1, allow_small_or_imprecise_dtypes=True), writes=["rowid"])
    add("dve", lambda e: e.tensor_scalar(out=ridx[:], in0=ptb[:], scalar1=128.0, scalar2=rowid[:, 0:1], op0=ALU.mult, op1=ALU.add), reads=["ptb", "rowid"], writes=["ridx"])
    with nc.allow_non_contiguous_dma(reason="small permuted loads"):
        wv = wis_scr.rearrange("(s t) h -> t s h", t=4)
        for hp in range(2):
            for c in range(8):
                add("sp", lambda e, hp=hp, c=c: e.dma_start(out=W2[hp][4 * c:4 * c + 4, :], in_=wv[:, :, 2 * c + hp]), writes=[f"W2_{hp}_{c}"], dma="ld1")
        add("sp", lambda e: e.dma_start(out=qTs[:], in_=qT_scr[:, :, 1024:1088].rearrange("c p n -> p c n")), writes=["qTs"], dma="ld2")
        add("sp", lambda e: e.dma_start(out=qiTs[:], in_=qiT_scr[:, :, 1024:1088].rearrange("c p n -> p c n")), writes=["qiTs"], dma="ld3")
    qiC = S.sb("qiC", [P, 16, 32], BF16)
    add("dve", lambda e: e.tensor_copy(out=qiC[:].rearrange("p s (c t) -> p s c t", t=4), in_=qiTs[:].rearrange("p c (s t) -> p s c t", t=4)), reads=["qiTs"], writes=["qiC"])
    for hp in range(2):
        add("dve", lambda e, hp=hp: e.tensor_scalar(out=W2[hp][:], in0=W2[hp][:], scalar1=1.0 / 32.0, scalar2=None, op0=ALU.mult),
            reads=[f"W2_{h2}_{c}" for c in range(8) for h2 in range(2)], writes=[f"W2s{hp}"])
    add("pool", lambda e: e.iota(cf[0:32, 0:32], pattern=[[4, 8], [1, 4]], base=0, channel_multiplier=0, allow_small_or_imprecise_dtypes=True), writes=["cf"])
    add("pool", lambda e: e.iota(rf[0:32, 0:32], pattern=[[0, 32]], base=0, channel_multiplier=1, allow_small_or_imprecise_dtypes=True), writes=["rf"])
    add("dve", lambda e: e.tensor_tensor(out=cf[0:32, 0:32], in0=cf[0:32, 0:32], in1=rf[0:32, 0:32], op=ALU.is_equal), reads=["cf", "rf"], writes=["cf"])
    add("dve", lambda e: e.tensor_reduce(out=Tsel[:], in_=cf[0:32, 0:32].rearrange("p (c t) -> p t c", t=4), axis=AX.X, op=ALU.add), reads=["cf"], writes=["Tsel"])
    for hp in range(2):
        add("pool", lambda e, hp=hp: e.memset(SelW[hp][:], 0.0), writes=[f"SelW{hp}"])
        for sq in range(16):
            add("dve", lambda e, sq=sq, hp=hp: e.tensor_scalar(out=SelW[hp][:, sq, 4 * sq:4 * sq + 4], in0=Tsel[:], scalar1=W2[hp][:, sq:sq + 1], scalar2=None, op0=ALU.mult),
                reads=["Tsel", f"W2s{hp}", f"SelW{hp}"], writes=[f"SelW{hp}"])
    add("pool", lambda e: e.iota(cf[:], pattern=[[4, 16], [0, 16], [1, 4]], base=0, channel_multiplier=0, allow_small_or_imprecise_dtypes=True), reads=["Tsel"], writes=["cf"])
    add("pool", lambda e: e.iota(rf[:], pattern=[[0, 1024]], base=0, channel_multiplier=1, allow_small_or_imprecise_dtypes=True), reads=["Tsel"], writes=["rf"])
    add("dve", lambda e: e.tensor_tensor(out=RepSel[:].rearrange("p a n -> p (a n)"), in0=cf[:], in1=rf[:], op=ALU.is_equal), reads=["cf", "rf"], writes=["RepSel"])
    add("pool", lambda e: e.iota(cf[:, 0:4], pattern=[[16, 4]], base=0, channel_multiplier=0, allow_small_or_imprecise_dtypes=True), reads=["RepSel"], writes=["cf"])
    add("pool", lambda e: e.iota(rf[:, 0:4], pattern=[[0, 4]], base=0, channel_multiplier=1, allow_small_or_imprecise_dtypes=True), reads=["RepSel"], writes=["rf"])
    add("dve", lambda e: e.tensor_tensor(out=rf[:, 0:4], in0=rf[:, 0:4], in1=cf[:, 0:4], op=ALU.subtract), reads=["cf", "rf"], writes=["rf"])
    add("dve", lambda e: e.tensor_scalar(out=cf[:, 4:8], in0=rf[:, 0:4], scalar1=0.0, scalar2=None, op0=ALU.is_ge), reads=["rf"], writes=["cf2"])
    add("dve", lambda e: e.tensor_scalar(out=rf[:, 4:8], in0=rf[:, 0:4], scalar1=16.0, scalar2=None, op0=ALU.is_lt), reads=["rf"], writes=["rf2"])
    add("dve", lambda e: e.tensor_tensor(out=BDj[:], in0=cf[:, 4:8], in1=rf[:, 4:8], op=ALU.mult), reads=["cf2", "rf2"], writes=["BDj"])

    def gather(dst, dkey, table, sq, pg, sem):
        col = sq * 16 + pg
        wk = [dkey] + ([dkey.rsplit("_", 1)[0] + "_all"] if pg == 15 else [])
        add("pool", lambda e: e.indirect_dma_start(out=dst, out_offset=None, in_=table, in_offset=bass.IndirectOffsetOnAxis(ap=ridx[:, col:col + 1], axis=0)),
            reads=["ridx"], writes=wk, dma=sem)

    def ik_load(sq):
        s = sq % 2
        for pg in range(16):
            gather(ikp[s][:, pg, 0:64], f"ikp{s}_{pg}", cik_d, sq, pg, f"g{s}")

    add("pool", lambda e: e.memset(acc[0:64, 0:2052], 0.0), writes=["acc"])
    ik_load(0)
    for sq in range(16):
        s = sq % 2
        if sq + 1 < 16:
            ik_load(sq + 1)
        for pg in range(16):
            add("dve", lambda e, s=s, pg=pg: e.tensor_copy(out=ikp[s][:, pg, 64:128], in_=ikp[s][:, pg, 0:64]), reads=[f"ikp{s}_{pg}", f"ikp{s}_all"], writes=[f"ikp{s}_{pg}"])
        for pg in range(16):
            bank = pg // 8
            pb_ = ps[bank][:, :].bitcast(BF16)
            add("pe", lambda e, pg=pg, s=s, pb_=pb_: e.transpose(pb_[:, (pg % 8) * P:(pg % 8 + 1) * P], ikp[s][:, pg, :], identb[:]), reads=[f"ikp{s}_{pg}", "identb"], writes=[psk[bank]])
        for bank in range(2):
            pb_ = ps[bank][:, :].bitcast(BF16)
            add(evac_alt(bank), copy_op(evac_alt(bank), kiTs[:, bank * 1024:(bank + 1) * 1024], pb_[:, :]), reads=[psk[bank]], writes=["kiTs"])
        for kc in range(5):
            rs_ = kc % 2
            kn = 512 if kc < 4 else 4
            for hp in range(2):
                bank = 2 + hp
                rhs = kiTs[hp * 64:(hp + 1) * 64, kc * 512:(kc + 1) * 512] if kc < 4 else kiTn[hp * 64:(hp + 1) * 64, 4 * sq:4 * sq + 4]
                add("pe", lambda e, hp=hp, sq=sq, bank=bank, kn=kn, rhs=rhs: e.matmul(ps[bank][0:32, 0:kn], lhsT=qiC[hp * 64:(hp + 1) * 64, sq, :],
                                                                                   rhs=rhs, start=True, stop=True),
                    reads=["qiC", "kiTs", "kiT2"], writes=[psk[bank]])
                add("act", lambda e, bank=bank, kn=kn, rs_=rs_, hp=hp: e.activation(out=Rs[rs_][hp][:, 0:kn], in_=ps[bank][0:32, 0:kn], func=AF.Relu), reads=[psk[bank]], writes=[f"Rs{rs_}_{hp}"])
            hb = 4 + kc % 2
            for hp in range(2):
                add("pe", lambda e, sq=sq, hb=hb, kn=kn, rs_=rs_, hp=hp: e.matmul(ps[hb][0:64, 0:kn], lhsT=SelW[hp][:, sq, :], rhs=Rs[rs_][hp][:, 0:kn], start=(hp == 0), stop=(hp == 1)),
                    reads=[f"SelW{hp}", f"Rs{rs_}_{hp}"], writes=[psk[hb]])
            c0 = kc * 512
            add("dve", lambda e, hb=hb, kn=kn, c0=c0: e.tensor_tensor(out=acc[0:64, c0:c0 + kn], in0=acc[0:64, c0:c0 + kn], in1=ps[hb][0:64, 0:kn], op=ALU.add),
                reads=["acc", psk[hb]], writes=["acc"])
    topk(64, 2052, acc, pen, mask01, pen, qpos[0:64, 8:9])
    add("dve", lambda e: e.tensor_scalar(out=mbias[:], in0=mask01[0:64, 0:2052], scalar1=1.0, scalar2=30000.0, op0=ALU.subtract, op1=ALU.mult), reads=["mask01"], writes=["mbias"])
    kpg = [S.sb(f"kpg{i}", [P, 16, 512], BF16) for i in range(2)]
    vpg = [S.sb(f"vpg{i}", [P, 16, 512], BF16) for i in range(2)]
    KTs = S.sb("KTs", [P, 4, 2048], BF16)
    Qz = S.sb("Qz", [P, 4, 64], BF16)
    Ps_ = S.sb("Ps_", [64, 2048], BF16)
    Pn = S.sb("Pn", [64, 64], BF16)
    PTs = S.sb("PTs", [P, 17, 64], BF16)
    rsum = S.sb("rsum", [64, 8], F32)
    otmp = S.sb("otmp", [64, 4, P], F32)
    osel = S.sb("osel", [64, P], F32)
    aoS = S.sb("aoS", [P, 16, 64], BF16)

    def kv_load(sq):
        s = sq % 2
        for pg in range(16):
            gather(kpg[s][:, pg, :], f"kpg{s}_{pg}", ck_d, sq, pg, f"gk{s}")
            gather(vpg[s][:, pg, :], f"vpg{s}_{pg}", cv_d, sq, pg, f"gv{s}")

    kv_load(0)
    add("pool", lambda e: e.memset(Qz[:], 0.0), writes=["Qz"])
    add("pool", lambda e: e.memset(Pn[:], 0.0), writes=["Pn"])
    for sq in range(16):
        s = sq % 2
        if sq + 1 < 16:
            kv_load(sq + 1)
        for pg in range(16):
            for j in range(4):
                idx_ = pg * 4 + j
                bank = (idx_ // 8) % 2
                pb_ = ps[bank][:, :].bitcast(BF16)
                add("pe", lambda e, pg=pg, j=j, s=s, pb_=pb_, idx_=idx_: e.transpose(pb_[:, (idx_ % 8) * P:(idx_ % 8 + 1) * P], kpg[s][:, pg, j * P:(j + 1) * P], identb[:]),
                    reads=[f"kpg{s}_{pg}", f"kpg{s}_all", "identb"], writes=[psk[bank]])
                if idx_ % 8 == 7:
                    pg0 = pg - 1
                    add(evac_alt(bank), copy_op(evac_alt(bank), KTs[:, :, pg0 * P:(pg0 + 2) * P].rearrange("p j (g n) -> p g j n", g=2),
                                                pb_[:, :].rearrange("p (g j n) -> p g j n", g=2, j=4)), reads=[psk[bank]], writes=["KTs"])
        for j in range(4):
            add("dve", lambda e, j=j, sq=sq: e.tensor_copy(out=Qz[:, j, 16 * j:16 * j + 16].rearrange("p (g t) -> p g t", g=4), in_=qTs[:, 4 * j:4 * j + 4, 4 * sq:4 * sq + 4]),
                reads=["qTs", "Qz"], writes=["Qz"])
        for kc in range(5):
            bank = 2 + kc % 2
            kn = 512 if kc < 4 else 4
            for j in range(4):
                rhs = KTs[:, j, kc * 512:(kc + 1) * 512] if kc < 4 else KTn[:, j, 4 * sq:4 * sq + 4]
                add("pe", lambda e, j=j, bank=bank, kn=kn, rhs=rhs: e.matmul(ps[bank][0:64, 0:kn], lhsT=Qz[:, j, :], rhs=rhs, start=(j == 0), stop=False),
                    reads=["Qz", "KTs", "KT"], writes=[psk[bank]])
            mrhs = mbias[:, kc * 512:kc * 512 + kn]
            add("pe", lambda e, sq=sq, bank=bank, kn=kn, mrhs=mrhs: e.matmul(ps[bank][0:64, 0:kn], lhsT=RepSel[:, sq, :], rhs=mrhs, start=False, stop=True),
                reads=["RepSel", "mbias"], writes=[psk[bank]])
            if kc < 4:
                add("act", lambda e, bank=bank, kc=kc: e.activation(out=Ps_[:, kc * 512:(kc + 1) * 512], in_=ps[bank][0:64, :], func=AF.Exp, scale=SCALE, accum_out=rsum[:, kc:kc + 1]),
                    reads=[psk[bank]], writes=["Ps_", "rsum"])
            else:
                add("act", lambda e, bank=bank, sq=sq: e.activation(out=Pn[:, 4 * sq:4 * sq + 4], in_=ps[bank][0:64, 0:4], func=AF.Exp, scale=SCALE, accum_out=rsum[:, 4:5]),
                    reads=[psk[bank]], writes=["Pn", "rsum"])
        add("dve", lambda e: e.tensor_reduce(out=rsum[:, 5:6], in_=rsum[:, 0:5], axis=AX.X, op=ALU.add), reads=["rsum"], writes=["rsum5"])
        add("dve", lambda e: e.reciprocal(out=rsum[:, 5:6], in_=rsum[:, 5:6]), reads=["rsum5"], writes=["rsum5"])
        pb4 = ps[4][:, :].bitcast(BF16)
        for pg in range(16):
            add("pe", lambda e, pg=pg: e.transpose(pb4[:, pg * 64:(pg + 1) * 64], Ps_[:, pg * P:(pg + 1) * P], identb[0:64, 0:64]), reads=["Ps_", "identb"], writes=[psk[4]])
        add("act", copy_op("act", PTs[:, 0:16, :].rearrange("p a n -> p (a n)"), pb4[:, :]), reads=[psk[4]], writes=["PTs"])
        pb5 = ps[5][:, :].bitcast(BF16)
        add("pe", lambda e: e.transpose(pb5[0:64, 0:64], Pn[:, :], identb[0:64, 0:64]), reads=["Pn", "identb"], writes=[psk[5]])
        add("dve", copy_op("dve", PTs[0:64, 16, :], pb5[0:64, 0:64]), reads=[psk[5]], writes=["PTs"])
        if sq + 1 < 16:
            add("pool", lambda e, sq=sq: e.memset(Pn[:, 4 * sq:4 * sq + 4], 0.0), writes=["Pn"])
        for pg in range(16):
            add("pe", lambda e, pg=pg, s=s: e.matmul(ps[6][0:64, :], lhsT=PTs[:, pg, :], rhs=vpg[s][:, pg, :], start=(pg == 0), stop=False), reads=["PTs", f"vpg{s}_{pg}", f"vpg{s}_all"], writes=[psk[6]])
        add("pe", lambda e: e.matmul(ps[6][0:64, :], lhsT=PTs[0:64, 16, :], rhs=Vn[0:64, :], start=False, stop=True), reads=["PTs", "Vs"], writes=[psk[6]])
        add("dve", lambda e: e.tensor_tensor(out=otmp[:], in0=ps[6][0:64, :].rearrange("p (j d) -> p j d", j=4), in1=BDj[:].unsqueeze(2).to_broadcast([64, 4, P]), op=ALU.mult),
            reads=[psk[6], "BDj"], writes=["otmp"])
        add("dve", lambda e: e.tensor_reduce(out=osel[:], in_=otmp[:].rearrange("p j d -> p d j"), axis=AX.X, op=ALU.add), reads=["otmp"], writes=["osel"])
        add("dve", lambda e: e.tensor_scalar(out=osel[:], in0=osel[:], scalar1=rsum[:, 5:6], scalar2=None, op0=ALU.mult), reads=["osel", "rsum5"], writes=["osel"])
        add("pe", lambda e: e.transpose(ps[7][:, 0:64], osel[:, :], ident[0:64, 0:64]), reads=["osel", "ident"], writes=[psk[7]])
        add("act", lambda e, sq=sq: e.activation(out=aoS[:, :, 4 * sq:4 * sq + 4], in_=ps[7][:, 0:64].rearrange("p (h t) -> p h t", t=4), func=AF.Identity), reads=[psk[7]], writes=["aoS"])
    with nc.allow_non_contiguous_dma(reason="128B runs sample attn out"):
        add("sp", lambda e: e.dma_start(out=ao_scr[:, :, 1024:1088].rearrange("c p n -> p c n"), in_=aoS[:]), reads=["aoS"], dma="st0")
    S.barrier()
    S.release(m4)
    if stop_after <= 6:
        return finish()

    m7 = S.mark()
    mixT = S.sb("mixT", [P, KC, NTOK], BF16)
    m7b = S.mark()
    poT = S.sb("poT", [P, 8, NTOK], BF16)
    aoT = S.sb("aoT", [P, KC, NTOK], BF16)
    wup = [S.sb(f"wup{i}", [P, 24, 512], BF16) for i in range(2)]
    sgA = [S.sb(f"sgA{i}", [P, NTOK], F32) for i in range(2)]
    sgB = [S.sb(f"sgB{i}", [P, NTOK], F32) for i in range(2)]
    t1 = S.sb("t1", [P, 512], F32)
    t2 = S.sb("t2", [P, 512], F32)
    add("sp", lambda e: e.dma_start(out=poT[:], in_=po_scr.rearrange("c p n -> p c n")), writes=["poT"], dma="ld0")
    add("sp", lambda e: e.dma_start(out=aoT[:], in_=ao_scr.rearrange("c p n -> p c n")), writes=["aoT"], dma="ld1")

    def up_load(g):
        s = g % 2
        add("pool", lambda e: e.dma_start(out=wup[s][:, 0:8, :], in_=w_upp[:, g * 512:(g + 1) * 512].rearrange("(c p) n -> p c n", p=P)), writes=[f"wupP{s}"], dma=f"wa{s}")
        add("pool", lambda e: e.dma_start(out=wup[s][:, 8:24, :], in_=w_upa[:, g * 512:(g + 1) * 512].rearrange("(c p) n -> p c n", p=P)), writes=[f"wupA{s}"], dma=f"wb{s}")

    def sg_load(fc):
        s = fc % 2
        add("sp", lambda e: e.dma_start(out=sgA[s][:], in_=sga_scr[fc]), writes=[f"sgA{s}"], dma=f"ld{2 + s}")
        add("sp", lambda e: e.dma_start(out=sgB[s][:], in_=sgb_scr[fc]), writes=[f"sgB{s}"], dma=f"ld{4 + s}")

    up_load(0)
    sg_load(0)
    tgs = [(0, 512), (512, 512), (1024, 64)]
    pc = 0
    for g in range(4):
        s = g % 2
        if g + 1 < 4:
            up_load(g + 1)
        for k in range(4):
            fc = g * 4 + k
            fs = fc % 2
            if fc + 1 < 16:
                sg_load(fc + 1)
            for (t0, tn) in tgs:
                b0 = (pc % 2) * 2
                pc += 1
                for c in range(8):
                    add("pe", lambda e, c=c, s=s, k=k, t0=t0, tn=tn, b0=b0: e.matmul(ps[b0][:, 0:tn], lhsT=wup[s][:, c, k * P:(k + 1) * P], rhs=poT[:, c, t0:t0 + tn], start=(c == 0), stop=(c == 7)),
                        reads=[f"wupP{s}", "poT"], writes=[psk[b0]])
                for c in range(KC):
                    add("pe", lambda e, c=c, s=s, k=k, t0=t0, tn=tn, b0=b0: e.matmul(ps[b0 + 1][:, 0:tn], lhsT=wup[s][:, 8 + c, k * P:(k + 1) * P], rhs=aoT[:, c, t0:t0 + tn], start=(c == 0), stop=(c == KC - 1)),
                        reads=[f"wupA{s}", "aoT"], writes=[psk[b0 + 1]])
                add("dve", lambda e, t0=t0, tn=tn, b0=b0, fs=fs: e.tensor_tensor(out=t1[:, 0:tn], in0=ps[b0][:, 0:tn], in1=sgA[fs][:, t0:t0 + tn], op=ALU.mult), reads=[psk[b0], f"sgA{fs}"], writes=["t1"])
                add("dve", lambda e, t0=t0, tn=tn, b0=b0, fs=fs: e.tensor_tensor(out=t2[:, 0:tn], in0=ps[b0 + 1][:, 0:tn], in1=sgB[fs][:, t0:t0 + tn], op=ALU.mult), reads=[psk[b0 + 1], f"sgB{fs}"], writes=["t2"])
                add("pool", lambda e, t0=t0, tn=tn, fc=fc: e.tensor_tensor(out=mixT[:, fc, t0:t0 + tn], in0=t1[:, 0:tn], in1=t2[:, 0:tn], op=ALU.add), reads=["t1", "t2"], writes=["mixT"])
    S.barrier()
    S.release(m7b)
    wo = [S.sb(f"wo{i}", [P, KC, 512], BF16) for i in range(2)]
    al1 = S.sb("al1", [P, D], F32)
    al1s = S.sb("al1s", [64, D], F32)
    xq = [S.sb(f"xq{i}", [P, 512], F32) for i in range(3)]
    hq = [S.sb(f"hq{i}", [P, 512], F32) for i in range(3)]
    add("sp", lambda e: e.dma_start(out=al1[:], in_=mods_tm[64:65, 2 * D:3 * D].partition_broadcast(P).rearrange("p o n -> p (o n)")), writes=["al1"], dma="ld0")
    add("sp", lambda e: e.dma_start(out=al1s[:], in_=mods_tm[0:64, 2 * D:3 * D]), writes=["al1s"], dma="ld1")

    def wo_load(g):
        s = g % 2
        add("pool", lambda e: e.dma_start(out=wo[s][:], in_=w_out[:, g * 512:(g + 1) * 512].rearrange("(c p) n -> p c n", p=P)), writes=[f"wo{s}"], dma=f"w{s}")

    wo_load(0)
    it = 0
    for g in range(4):
        s = g % 2
        if g + 1 < 4:
            wo_load(g + 1)
        for ti in range(9):
            n = P if ti < 8 else 64
            r0 = ti * P
            xs_ = it % 3
            b0 = 4 + it % 2
            it += 1
            add("sp", lambda e, xs_=xs_, r0=r0, n=n, g=g: e.dma_start(out=xq[xs_][0:n, :], in_=x_own[r0:r0 + n, g * 512:(g + 1) * 512]), writes=[f"xq{xs_}"], dma=f"x{xs_}")
            for c in range(KC):
                add("pe", lambda e, c=c, s=s, r0=r0, n=n, b0=b0: e.matmul(ps[b0][0:n, :], lhsT=mixT[:, c, r0:r0 + n], rhs=wo[s][:, c, :], start=(c == 0), stop=(c == KC - 1)),
                    reads=["mixT", f"wo{s}"], writes=[psk[b0]])
            alp = al1 if ti < 8 else al1s
            akey = "al1" if ti < 8 else "al1s"
            add("dve", lambda e, xs_=xs_, n=n, b0=b0, g=g, alp=alp: e.tensor_tensor(out=hq[xs_][0:n, :], in0=ps[b0][0:n, :], in1=alp[0:n, g * 512:(g + 1) * 512], op=ALU.mult),
                reads=[psk[b0], akey], writes=[f"hq{xs_}"])
            add("pool", lambda e, xs_=xs_, n=n: e.tensor_tensor(out=hq[xs_][0:n, :], in0=hq[xs_][0:n, :], in1=xq[xs_][0:n, :], op=ALU.add), reads=[f"hq{xs_}", f"xq{xs_}"], writes=[f"hq{xs_}"])
            add("sp", lambda e, xs_=xs_, r0=r0, n=n, g=g: e.dma_start(out=h_scr[r0:r0 + n, g * 512:(g + 1) * 512], in_=hq[xs_][0:n, :]), reads=[f"hq{xs_}"], dma=f"st{xs_}")
    S.barrier()
    S.release(m7b)
    hnT = S.sb("hnT", [P, KC, NTOK], BF16)
    m7c = S.mark()
    A2S = S.sb("A2S", [P, KC, 64], F32)
    sh2S = S.sb("sh2S", [P, KC, 64], F32)
    load_modS(3, 2, g2T, A2S, sh2S, "f7")
    fr = Front("f7", A2P, 32, A2S, sh2S)
    tl = [(h_scr[i * P:(i + 1) * P, :], P, i * P, False) for i in range(8)] + [(h_scr[1024:1088, :], 64, 1024, True)]
    fr.load(tl[0][0], tl[0][1], 0)
    for ti, (src, n, t0, smp) in enumerate(tl):
        if ti + 1 < len(tl):
            fr.load(tl[ti + 1][0], tl[ti + 1][1], (ti + 1) % 2)
        fr.norm(n, ti % 2)
        fr.transpose_mod(n, hnT[:, :, t0:t0 + n], "hnT", smp)
    S.barrier()
    S.release(m7c)
    if stop_after <= 7:
        return finish()

    wf = [S.sb(f"wf{i}", [P, KC, 1024], BF16) for i in range(2)]
    sgt = [S.sb(f"sgt{i}", [P, 512], F32) for i in range(2)]
    ast = [S.sb(f"ast{i}", [P, NTOK], BF16) for i in range(2)]

    def f1_load(g):
        s = g % 2
        add("pool", lambda e: e.dma_start(out=wf[s][:, :, 0:512], in_=w_f1[:, g * 512:(g + 1) * 512].rearrange("(c p) n -> p c n", p=P)), writes=[f"wfG{s}"], dma=f"wa{s}")
        add("pool", lambda e: e.dma_start(out=wf[s][:, :, 512:1024], in_=w_f1[:, DFF + g * 512:DFF + (g + 1) * 512].rearrange("(c p) n -> p c n", p=P)), writes=[f"wfU{s}"], dma=f"wb{s}")

    f1_load(0)
    pc = 0
    for g in range(11):
        s = g % 2
        if g + 1 < 11:
            f1_load(g + 1)
        for k in range(4):
            f = g * 4 + k
            as_ = f % 2
            for (t0, tn) in tgs:
                b0 = (pc % 4) * 2
                sg_ = pc % 2
                pc += 1
                for c in range(KC):
                    add("pe", lambda e, c=c, s=s, k=k, t0=t0, tn=tn, b0=b0: e.matmul(ps[b0][:, 0:tn], lhsT=wf[s][:, c, k * P:(k + 1) * P], rhs=hnT[:, c, t0:t0 + tn], start=(c == 0), stop=(c == KC - 1)),
                        reads=[f"wfG{s}", "hnT"], writes=[psk[b0]])
                for c in range(KC):
                    add("pe", lambda e, c=c, s=s, k=k, t0=t0, tn=tn, b0=b0: e.matmul(ps[b0 + 1][:, 0:tn], lhsT=wf[s][:, c, 512 + k * P:512 + (k + 1) * P], rhs=hnT[:, c, t0:t0 + tn], start=(c == 0), stop=(c == KC - 1)),
                        reads=[f"wfU{s}", "hnT"], writes=[psk[b0 + 1]])
                add("act", lambda e, tn=tn, b0=b0, sg_=sg_: e.activation(out=sgt[sg_][:, 0:tn], in_=ps[b0][:, 0:tn], func=AF.Silu), reads=[psk[b0]], writes=[f"sgt{sg_}"])
                add("dve", lambda e, t0=t0, tn=tn, b0=b0, sg_=sg_, as_=as_: e.tensor_tensor(out=ast[as_][:, t0:t0 + tn], in0=ps[b0 + 1][:, 0:tn], in1=sgt[sg_][:, 0:tn], op=ALU.mult),
                    reads=[psk[b0 + 1], f"sgt{sg_}"], writes=[f"ast{as_}"])
            add("sp", lambda e, f=f, as_=as_: e.dma_start(out=aT_scr[f], in_=ast[as_][:]), reads=[f"ast{as_}"], dma=f"st{as_}")
    S.barrier()
    S.release(m7)
    if stop_after <= 8:
        return finish()

    w2b = [S.sb(f"w2b{i}", [P, 44, 512], BF16) for i in range(2)]
    atl = [S.sb(f"atl{i}", [P, 44, P], BF16) for i in range(2)]
    al2 = S.sb("al2", [P, D], F32)
    al2s = S.sb("al2s", [64, D], F32)
    hq = [S.sb(f"hq9_{i}", [P, 512], F32) for i in range(3)]
    yq = [S.sb(f"yq9_{i}", [P, 512], F32) for i in range(3)]
    ssq = S.sb("ssq", [P, 9, 4], F32)
    junk9 = S.sb("junk9", [P, 512], BF16)
    add("sp", lambda e: e.dma_start(out=al2[:], in_=mods_tm[64:65, 5 * D:6 * D].partition_broadcast(P).rearrange("p o n -> p (o n)")), writes=["al2"], dma="ld0")
    add("sp", lambda e: e.dma_start(out=al2s[:], in_=mods_tm[0:64, 5 * D:6 * D]), writes=["al2s"], dma="ld1")

    def f2_load(g):
        s = g % 2
        for q4 in range(4):
            add("pool", lambda e, q4=q4: e.dma_start(out=w2b[s][:, q4 * 11:(q4 + 1) * 11, :], in_=w_f2[q4 * 11 * P:(q4 + 1) * 11 * P, g * 512:(g + 1) * 512].rearrange("(c p) n -> p c n", p=P)),
                writes=[f"w2b{s}_{q4}"], dma=f"wq{s}_{q4}")

    def at_load(it_):
        ti = it_ % 9
        s = it_ % 2
        n = P if ti < 8 else 64
        add("sp", lambda e: e.dma_start(out=atl[s][:, :, 0:n], in_=aT_scr[:, :, ti * P:ti * P + n].rearrange("c p n -> p c n")), writes=[f"atl{s}"], dma=f"ld{2 + s}")

    f2_load(0)
    with nc.allow_non_contiguous_dma(reason="256B runs a^T tile loads"):
        at_load(0)
        it = 0
        for g in range(4):
            s = g % 2
            if g + 1 < 4:
                f2_load(g + 1)
            for ti in range(9):
                n = P if ti < 8 else 64
                r0 = ti * P
                as_ = it % 2
                xs_ = it % 3
                b0 = it % 2
                if it + 1 < 36:
                    at_load(it + 1)
                it += 1
                add("sp", lambda e, xs_=xs_, r0=r0, n=n, g=g: e.dma_start(out=hq[xs_][0:n, :], in_=h_scr[r0:r0 + n, g * 512:(g + 1) * 512]), writes=[f"hq9_{xs_}"], dma=f"x{xs_}")
                for c in range(44):
                    add("pe", lambda e, c=c, s=s, as_=as_, n=n, b0=b0: e.matmul(ps[b0][0:n, :], lhsT=atl[as_][:, c, 0:n], rhs=w2b[s][:, c, :], start=(c == 0), stop=(c == 43)),
                        reads=[f"atl{as_}", f"w2b{s}_{c // 11}"], writes=[psk[b0]])
                alp = al2 if ti < 8 else al2s
                akey = "al2" if ti < 8 else "al2s"
                add("dve", lambda e, xs_=xs_, n=n, b0=b0, g=g, alp=alp: e.tensor_tensor(out=yq[xs_][0:n, :], in0=ps[b0][0:n, :], in1=alp[0:n, g * 512:(g + 1) * 512], op=ALU.mult),
                    reads=[psk[b0], akey], writes=[f"yq9_{xs_}"])
                add("pool", lambda e, xs_=xs_, n=n: e.tensor_tensor(out=yq[xs_][0:n, :], in0=yq[xs_][0:n, :], in1=hq[xs_][0:n, :], op=ALU.add), reads=[f"yq9_{xs_}", f"hq9_{xs_}"], writes=[f"yq9_{xs_}"])
                add("act", lambda e, xs_=xs_, n=n, ti=ti, g=g: e.activation(out=junk9[0:n, :], in_=yq[xs_][0:n, :], func=AF.Square, accum_out=ssq[0:n, ti, g:g + 1]),
                    reads=[f"yq9_{xs_}"], writes=["junk9", "ssq"])
                add("sp", lambda e, xs_=xs_, r0=r0, n=n, g=g: e.dma_start(out=yp_scr[r0:r0 + n, g * 512:(g + 1) * 512], in_=yq[xs_][0:n, :]), reads=[f"yq9_{xs_}"], dma=f"st{xs_}")
    S.barrier()
    m10 = S.mark()
    gfr = S.sb("gfr", [P, D], F32)
    yb = [S.sb(f"yb{i}", [P, D], F32) for i in range(2)]
    yo = [S.sb(f"yo{i}", [P, D], F32) for i in range(2)]
    rs10 = S.sb("rs10", [P, 9, 2], F32)
    add("sp", lambda e: e.dma_start(out=gfr[:], in_=gf_d.partition_broadcast(P).rearrange("p o n -> p (o n)")), writes=["gfr"], dma="ld0")
    add("dve", lambda e: e.tensor_reduce(out=rs10[:, :, 0], in_=ssq[:], axis=AX.X, op=ALU.add), reads=["ssq"], writes=["rs10"])
    add("act", lambda e: e.activation(out=rs10[:, :, 1], in_=rs10[:, :, 0], func=AF.Sqrt, scale=1.0 / D, bias=EPS_AP[:, :]), reads=["rs10", "eps"], writes=["rs10b"])
    add("dve", lambda e: e.reciprocal(out=rs10[:, :, 1], in_=rs10[:, :, 1]), reads=["rs10b"], writes=["rs10b"])

    def y_load(ti):
        n = P if ti < 8 else 64
        add("sp", lambda e: e.dma_start(out=yb[ti % 2][0:n, :], in_=yp_scr[ti * P:ti * P + n, :]), writes=[f"yb{ti % 2}"], dma=f"x{ti % 2}")

    y_load(0)
    for ti in range(9):
        n = P if ti < 8 else 64
        s = ti % 2
        if ti + 1 < 9:
            y_load(ti + 1)
        add("act", lambda e, s=s, n=n, ti=ti: e.activation(out=yb[s][0:n, :], in_=yb[s][0:n, :], func=AF.Identity, scale=rs10[0:n, ti, 1:2]), reads=[f"yb{s}", "rs10b"], writes=[f"yb{s}"])
        add("dve", lambda e, s=s, n=n: e.tensor_tensor(out=yo[s][0:n, :], in0=yb[s][0:n, :], in1=gfr[0:n, :], op=ALU.mult), reads=[f"yb{s}", "gfr"], writes=[f"yo{s}"])
        add("sp", lambda e, s=s, n=n, ti=ti: e.dma_start(out=y_own[ti * P:ti * P + n, :], in_=yo[s][0:n, :]), reads=[f"yo{s}"], dma=f"st{s}")
    return finish()


def make_in_maps(inp, cores=range(8)):
    f = np.float32
    xp = np.asarray(inp["x_prompt"], f)
    xs = np.asarray(inp["x_sample"], f)
    pt = np.asarray(inp["page_table"], np.int32)
    ck = np.ascontiguousarray(np.asarray(inp["cache_k"])[0].reshape(2560 * 128, 512)[:CACHE_ROWS[0]], dtype=f)
    cv = np.ascontiguousarray(np.asarray(inp["cache_v"])[0].reshape(2560 * 128, 512)[:CACHE_ROWS[0]], dtype=f)
    cik = np.ascontiguousarray(np.asarray(inp["cache_idx_k"])[0].reshape(2560 * 128, 64)[:CACHE_ROWS[0]], dtype=f)
    stp = np.asarray(inp["state_pool"], f)[0]
    cp = np.asarray(inp["c_prompt"], f)
    cs = np.asarray(inp["c_sample"], f)
    shared = dict(
        ck=ck, cv=cv, cik=cik,
        w_ada=np.asarray(inp["w_ada"], f)[0], b_ada=np.asarray(inp["b_ada"], f)[0][None, :],
        g1=np.asarray(inp["g_norm1"], f)[0], g2=np.asarray(inp["g_norm2"], f)[0], gf=np.asarray(inp["g_final"], f)[None, :],
        w_in=np.asarray(inp["w_in"], f)[0], w_pg=np.asarray(inp["w_pool_grp"], f)[0], psc=np.asarray(inp["pool_scale"], f)[0],
        w_upp=np.asarray(inp["w_up_pool"], f)[0], w_upa=np.asarray(inp["w_up_attn"], f)[0], w_out=np.asarray(inp["w_out"], f)[0],
        w_f1=np.asarray(inp["w_ffn_in"], f)[0], w_f2=np.asarray(inp["w_ffn_out"], f)[0],
    )
    maps = []
    for c in cores:
        b, q = c // 4, c % 4
        blks = own_blocks(q)
        x_own = np.zeros((NALL, D), f)
        qpos = np.zeros((P, 9), f)
        hmask = np.zeros((1, P), f)
        for i, blk in enumerate(blks):
            x_own[i * P:(i + 1) * P] = xp[b, blk * P:(blk + 1) * P]
            qpos[:, i] = blk * P + np.arange(P)
            if blk > 0:
                x_own[1088 + 16 * i:1088 + 16 * (i + 1)] = xp[b, blk * P - 16:blk * P]
                hmask[0, 16 * i:16 * (i + 1)] = 1.0
        x_own[1024:1088] = xs[16 * c:16 * (c + 1)].reshape(64, D)
        qpos[0:64, 8] = 2048 + (np.arange(64) % 4)
        c_tok = np.empty((P, D), f)
        c_tok[0:64] = np.repeat(cs[16 * c:16 * (c + 1)], 4, axis=0)
        c_tok[64:128] = cp[b][None, :]
        meta = np.zeros((P, 512), f)
        meta[:, 0:9] = qpos
        meta[:, 16:144] = (blks[0] * P + np.arange(P, dtype=f))[None, :]
        meta[:, 144:272] = hmask
        m = dict(shared)
        m.update(
            x_seq=np.ascontiguousarray(xp[b]), x_own=x_own, c_tok=c_tok, meta=meta,
            ptm=np.ascontiguousarray(np.broadcast_to(pt[16 * c:16 * (c + 1)].reshape(1, 256), (P, 256))),
            stp=np.ascontiguousarray(stp[16 * c:16 * (c + 1)].reshape(240, 1024)),
        )
        maps.append(m)
    return maps


PER_V = ("x_seq", "x_own", "c_tok", "meta", "ptm", "stp")


def make_in_map_stacked(inp, vcores):
    maps = make_in_maps(inp, cores=vcores)
    m = dict(maps[0])
    for k in PER_V:
        m[k] = np.stack([mm[k] for mm in maps], 0)
    return m


_NC_CACHE = {}


N_PHYS = 1
V_PASS = 8 // N_PHYS


def kernel(**inp):
    if "nc" not in _NC_CACHE:
        _NC_CACHE["nc"] = build(V=V_PASS)[0]
    nc = _NC_CACHE["nc"]
    maps = [make_in_map_stacked(inp, list(range(pc * V_PASS, (pc + 1) * V_PASS))) for pc in range(N_PHYS)]
    res = run_bass_kernel_spmd(nc, maps, core_ids=list(range(N_PHYS))).results
    f = np.float32
    y_prompt = np.empty((2, 4096, D), f)
    y_sample = np.empty((128, 4, D), f)
    k_prompt = np.empty((1, 2, 4096, 4, 128), f)
    v_prompt = np.empty((1, 2, 4096, 4, 128), f)
    idxk_prompt = np.empty((1, 2, 4096, 64), f)
    pool_prompt = np.empty((1, 2, 15, 1024), f)
    k_sample = np.empty((1, 128, 4, 4, 128), f)
    v_sample = np.empty((1, 128, 4, 4, 128), f)
    idxk_sample = np.empty((1, 128, 4, 64), f)
    pool_sample = np.empty((1, 128, 15, 1024), f)
    for c in range(8):
        b, q = c // 4, c % 4
        r = {k: np.asarray(v)[c % V_PASS] for k, v in res[c // V_PASS].items()}
        for i, blk in enumerate(own_blocks(q)):
            y_prompt[b, blk * P:(blk + 1) * P] = r["y_own"][i * P:(i + 1) * P]
        y_sample[16 * c:16 * (c + 1)] = r["y_own"][1024:1088].reshape(16, 4, D)
        if q == 0:
            k_prompt[0, b] = r["k_all"].reshape(4096, 4, 128)
            v_prompt[0, b] = r["v_all"].reshape(4096, 4, 128)
            idxk_prompt[0, b] = r["ki_all"]
            pool_prompt[0, b] = r["pool_last"]
        k_sample[0, 16 * c:16 * (c + 1)] = r["ks_o"].reshape(16, 4, 4, 128)
        v_sample[0, 16 * c:16 * (c + 1)] = r["vs_o"].reshape(16, 4, 4, 128)
        idxk_sample[0, 16 * c:16 * (c + 1)] = r["kis_o"].reshape(16, 4, 64)
        pool_sample[0, 16 * c:16 * (c + 1)] = r["pool_s"].reshape(16, 15, 1024)
    return (y_prompt, y_sample, k_prompt, v_prompt, idxk_prompt, pool_prompt, k_sample, v_sample, idxk_sample, pool_sample)
```

```python
import contextlib
import numpy as np
import concourse.bass as bass
import concourse.mybir as mybir
from concourse.bass_utils import run_bass_kernel_spmd

F32 = mybir.dt.float32
BF16 = mybir.dt.bfloat16
I32 = mybir.dt.int32
I16 = mybir.dt.int16
ALU = mybir.AluOpType
AF = mybir.ActivationFunctionType
AX = mybir.AxisListType
DT_SIZE = {F32: 4, BF16: 2, I32: 4, I16: 2}

P = 128
D = 2048
KC = 16
NTOK = 1088
NALL = 1216
DFF = 5632
NKB = [4 * (i + 1) for i in range(8)]
NIT = 22
EPS = 1e-6
NEG = -1.0e30
CACHE_ROWS = [2560 * 128]


def own_blocks(q):
    return [q, 7 - q, 8 + q, 15 - q, 16 + q, 23 - q, 24 + q, 31 - q]


class Sched:
    ENGS = ("pe", "act", "dve", "pool", "sp")

    def __init__(self, nc):
        self.nc = nc
        self.ops = []
        self.last_w = {}
        self.readers = {}
        self.sb_lo = nc.sbuf_base
        self.sb_hi = nc.sbuf_top
        self.sb_ptr = self.sb_lo
        self.ntens = 0
        self.last_barrier = None
        self.peak = 0
        self.loop_mode = False

    def sb(self, name, shape, dtype, align=64):
        nbytes = int(np.prod(shape[1:])) * DT_SIZE[dtype]
        off = (self.sb_ptr + align - 1) // align * align
        assert off + nbytes <= self.sb_hi, f"SBUF overflow at {name}: need {off + nbytes - self.sb_lo} of {self.sb_hi - self.sb_lo}"
        self.sb_ptr = off + nbytes
        self.peak = max(self.peak, self.sb_ptr)
        self.ntens += 1
        return self.nc.alloc_sbuf_tensor_at(f"{name}_{self.ntens}", list(shape), dtype, offset=off)

    def mark(self):
        return self.sb_ptr

    def release(self, mark):
        self.sb_ptr = mark

    def add(self, eng, fn, reads=(), writes=(), dma=None):
        idx = len(self.ops)
        psr = [k for k in reads if k.startswith("ps") and k[2:].isdigit()]
        if psr:
            reads = [k for k in reads if k not in psr]
            writes = list(writes) + psr
        deps = set()
        for k in reads:
            if k in self.last_w:
                deps.add(self.last_w[k])
        for k in writes:
            if k in self.last_w:
                deps.add(self.last_w[k])
            for r in self.readers.get(k, ()):
                deps.add(r)
        if self.last_barrier is not None:
            deps.add(self.last_barrier)
        self.ops.append(dict(eng=eng, fn=fn, deps=deps, dma=dma, barrier=False))
        for k in writes:
            self.last_w[k] = idx
            self.readers[k] = []
        for k in reads:
            self.readers.setdefault(k, []).append(idx)
        return idx

    def barrier(self):
        idx = len(self.ops)
        self.ops.append(dict(eng=None, fn=None, deps=set(), dma=None, barrier=True))
        self.last_barrier = idx
        self.last_w = {}
        self.readers = {}

    def emit(self):
        import os
        nc = self.nc
        if os.environ.get("KMAXOPS"):
            self.ops = self.ops[:int(os.environ["KMAXOPS"])]
        ops = self.ops
        n = len(ops)
        last_eng = {}
        last_dma = {}
        for i, op in enumerate(ops):
            if op["barrier"]:
                op["deps"] = set(last_eng.values()) | set(last_dma.values())
                continue
            if op["dma"] is not None:
                last_dma[op["dma"]] = i
            else:
                last_eng[op["eng"]] = i
        fin_deps = set(last_eng.values()) | set(last_dma.values())
        need = [False] * n
        for i, op in enumerate(ops):
            for d in op["deps"]:
                need[d] = True
        for d in fin_deps:
            need[d] = True
        sem_names = set(self.ENGS) | set(op["dma"] for op in ops if op["dma"])
        stack = contextlib.ExitStack()
        if self.loop_mode:
            sems = {s: nc.alloc_semaphore(name=f"s_{s}") for s in sorted(sem_names)}
        else:
            sems = {s: stack.enter_context(nc.semaphore(f"s_{s}")) for s in sorted(sem_names)}
        cnt = {s: 0 for s in sem_names}
        ev = [None] * n
        for i, op in enumerate(ops):
            if op["barrier"]:
                continue
            if op["dma"] is not None:
                cnt[op["dma"]] += 16
                ev[i] = (op["dma"], cnt[op["dma"]])
            elif need[i]:
                cnt[op["eng"]] += 1
                ev[i] = (op["eng"], cnt[op["eng"]])
        bar_events = {}
        cur = {}
        dep_ev = [None] * n
        for i, op in enumerate(ops):
            eng = op["eng"]
            out = {}
            for d in op["deps"]:
                if ops[d]["barrier"]:
                    for s, v in bar_events[d].items():
                        if v > out.get(s, 0):
                            out[s] = v
                    continue
                if ops[d]["dma"] is None and ops[d]["eng"] == eng and eng == "pe":
                    continue
                s, v = ev[d]
                if ops[d]["dma"] is not None:
                    v = cur[s]
                if v > out.get(s, 0):
                    out[s] = v
            dep_ev[i] = out
            if op["barrier"]:
                bar_events[i] = out
            elif op["dma"] is not None:
                cur[op["dma"]] = ev[i][1]

        def dep_events(i, eng):
            return dep_ev[i]

        per_eng = {e: [] for e in self.ENGS}
        for i, op in enumerate(ops):
            if not op["barrier"]:
                per_eng[op["eng"]].append(i)
        fin_events = {}
        for d in fin_deps:
            s, v = ev[d]
            if v > fin_events.get(s, 0):
                fin_events[s] = v
        self.n_waits = 0

        def run_engine(ename, e):
            seen = {}
            for i in per_eng[ename]:
                op = ops[i]
                for s, v in sorted(dep_events(i, ename).items()):
                    if v > seen.get(s, 0):
                        e.wait_ge(sems[s], v)
                        seen[s] = v
                        self.n_waits += 1
                inst = op["fn"](e)
                if ev[i] is not None:
                    s, v = ev[i]
                    inst.then_inc(sems[s], 16 if op["dma"] is not None else 1)
            if ename == "sp":
                for s, v in sorted(fin_events.items()):
                    if v > seen.get(s, 0):
                        e.wait_ge(sems[s], v)

        with nc.allow_non_contiguous_dma(reason="small strided scratch/scalar transfers"), nc.Block() as block:
            @block.sync
            def _(e):
                run_engine("sp", e)

            @block.tensor
            def _(e):
                run_engine("pe", e)

            @block.scalar
            def _(e):
                run_engine("act", e)

            @block.vector
            def _(e):
                run_engine("dve", e)

            @block.gpsimd
            def _(e):
                run_engine("pool", e)
        stack.close()
        self.cnt = cnt


def build(stop_after=99, dbg=(), V=1):
    nc = bass.Bass("TRN2", target_bir_lowering=False)
    S = Sched(nc)
    S.loop_mode = V > 1
    octx = contextlib.ExitStack()
    if V > 1:
        vi = octx.enter_context(nc.Fori(0, V))
        octx.enter_context(nc.cleanup_on_exit())

    def finish():
        if io_copies_out:
            S.barrier()
            need_stage = {"y_own": 99, "pool_last": 3, "pool_s": 3}
            for ci_, (dst_, src_, nm_) in enumerate(io_copies_out):
                if stop_after < need_stage.get(nm_, 4):
                    continue
                add("sp", lambda e, dst_=dst_, src_=src_: e.dma_start(out=dst_, in_=src_), dma=f"io{ci_}")
        S.emit()
        if V > 1:
            nc.all_engine_barrier()
        octx.close()
        return nc, S

    def dr(name, shape, dt=F32, kind="ExternalInput"):
        return nc.dram_tensor(name, list(shape), dt, kind=kind).ap()

    io_copies_in = []
    io_copies_out = []

    def drv(name, shape, dt=F32, kind="ExternalInput", flat=None):
        if V == 1:
            return dr(name, shape, dt, kind)
        t = nc.dram_tensor(name, [V] + list(shape), dt, kind=kind).ap()
        st = nc.dram_tensor(name + "_st", list(shape), dt, kind="Internal").ap()
        dyn = t[vi]
        a, b_ = (st, dyn)
        if flat is not None:
            a = st.rearrange(flat[0], **flat[1])
            b_ = dyn.rearrange(flat[0], **flat[1])
        if kind == "ExternalInput":
            io_copies_in.append((a, b_, name))
        else:
            io_copies_out.append((b_, a, name))
        return st

    x_seq = drv("x_seq", [4096, D])
    x_own = drv("x_own", [NALL, D])
    c_tok = drv("c_tok", [P, D])
    meta_d = drv("meta", [P, 512])
    ptm_d = drv("ptm", [P, 256], I32)
    ck_d = dr("ck", [CACHE_ROWS[0], 512])
    cv_d = dr("cv", [CACHE_ROWS[0], 512])
    cik_d = dr("cik", [CACHE_ROWS[0], 64])
    stp_d = drv("stp", [240, 1024])
    w_ada = dr("w_ada", [D, 6 * D])
    b_ada = dr("b_ada", [1, 6 * D])
    g1_d = dr("g1", [D])
    g2_d = dr("g2", [D])
    gf_d = dr("gf", [1, D])
    w_in = dr("w_in", [D, 9296])
    w_pg = dr("w_pg", [4, 256, 256])
    psc_d = dr("psc", [1024])
    w_upp = dr("w_upp", [1024, D])
    w_upa = dr("w_upa", [D, D])
    w_out = dr("w_out", [D, D])
    w_f1 = dr("w_f1", [D, 2 * DFF])
    w_f2 = dr("w_f2", [DFF, D])
    EO = "ExternalOutput"
    y_own = drv("y_own", [NTOK, D], kind=EO)
    k_all = drv("k_all", [4096, 512], kind=EO)
    v_all = drv("v_all", [4096, 512], kind=EO)
    ki_all = drv("ki_all", [4096, 64], kind=EO, flat=("(a b) c -> a (b c)", dict(b=32)))
    ks_o = drv("ks_o", [64, 512], kind=EO)
    vs_o = drv("vs_o", [64, 512], kind=EO)
    kis_o = drv("kis_o", [64, 64], kind=EO, flat=("(a b) c -> a (b c)", dict(b=8)))
    pool_last = drv("pool_last", [15, 1024], kind=EO)
    pool_s = drv("pool_s", [240, 1024], kind=EO)
    IK = "Internal"
    dk = (lambda n: EO if n in dbg else IK)
    mods_tm = dr("mods_tm", [P, 6 * D], kind=dk("mods_tm"))
    modS_scr = dr("modS_scr", [64, P, 64], kind=dk("modS_scr"))
    qT_scr = dr("qT_scr", [16, P, NTOK], BF16, kind=dk("qT_scr"))
    qiT_scr = dr("qiT_scr", [8, P, NTOK], BF16, kind=dk("qiT_scr"))
    sga_scr = dr("sga_scr", [16, P, NTOK], kind=dk("sga_scr"))
    sgb_scr = dr("sgb_scr", [16, P, NTOK], kind=dk("sgb_scr"))
    po_scr = dr("po_scr", [8, P, NTOK], BF16, kind=dk("po_scr"))
    ao_scr = dr("ao_scr", [16, P, NTOK], BF16, kind=dk("ao_scr"))
    h_scr = dr("h_scr", [NTOK, D], kind=dk("h_scr"))
    aT_scr = dr("aT_scr", [9, P, 44, P], BF16, kind=dk("aT_scr"))
    yp_scr = dr("yp_scr", [NTOK, D], kind=dk("yp_scr"))
    wis_scr = dr("wis_scr", [64, 16], kind=dk("wis_scr"))
    pss_scr = dr("pss_scr", [64, 1024], kind=IK)
    dbg_scr = dr("dbg_scr", [P, 8192], kind=(EO if dbg else IK))

    ps = [nc.alloc_psum_tensor(f"bank{i}", [P, 512], F32) for i in range(8)]
    psk = [f"ps{i}" for i in range(8)]

    add = S.add

    ident = S.sb("ident", [P, P], F32)
    identb = S.sb("identb", [P, P], BF16)
    onesb = S.sb("onesb", [P, P], BF16)
    iota16 = S.sb("iota16", [P, 4224], I16)
    pow2 = S.sb("pow2", [P, NIT + 1], F32)
    modP = S.sb("modP", [P, 64], F32)
    A1P = S.sb("A1P", [P, 16], F32)
    A2P = S.sb("A2P", [P, 16], F32)
    g1T = S.sb("g1T", [P, 16], F32)
    g2T = S.sb("g2T", [P, 16], F32)
    qpos = S.sb("qpos", [P, 9], F32)
    absw = S.sb("absw", [P, 9, 16], F32)
    sgnw = S.sb("sgnw", [P, 9, 16], F32)
    m0 = S.mark()
    ci = S.sb("ci", [P, P], F32)
    ri = S.sb("ri", [P, P], F32)
    add("pool", lambda e: e.iota(ci[:], pattern=[[1, P]], base=0, channel_multiplier=0, allow_small_or_imprecise_dtypes=True), writes=["ci"])
    add("pool", lambda e: e.iota(ri[:], pattern=[[0, P]], base=0, channel_multiplier=1, allow_small_or_imprecise_dtypes=True), writes=["ri"])
    add("pool", lambda e: e.iota(iota16[:], pattern=[[1, 4224]], base=0, channel_multiplier=0, allow_small_or_imprecise_dtypes=True), writes=["iota16"])
    add("dve", lambda e: e.tensor_tensor(out=ident[:], in0=ci[:], in1=ri[:], op=ALU.is_equal), reads=["ci", "ri"], writes=["ident"])
    add("dve", lambda e: e.tensor_copy(out=identb[:], in_=ident[:]), reads=["ident"], writes=["identb"])
    add("pool", lambda e: e.memset(onesb[:], 1.0), writes=["onesb"])
    for k in range(NIT + 1):
        add("pool", lambda e, k=k: e.memset(pow2[:, k:k + 1], float(2.0 ** (-k))), writes=["pow2"])
    for ci_, (dst_, src_, nm_) in enumerate(io_copies_in):
        add("sp", lambda e, dst_=dst_, src_=src_: e.dma_start(out=dst_, in_=src_), writes=["io_" + nm_], dma=f"io{ci_}")
    if io_copies_in:
        S.barrier()
    add("sp", lambda e: e.dma_start(out=qpos[:], in_=meta_d[:, 0:9]), writes=["qpos"], dma="ld0")
    cvt = S.sb("cvt", [16, P], F32)

    def load_colvec(dst, dkey, src, n, sem, bank=7):
        add("sp", lambda e: e.dma_start(out=cvt[0:n, :], in_=src.rearrange("(c p) -> c p", p=P)), writes=["cvt"], dma=sem)
        add("pe", lambda e: e.transpose(ps[bank][:, 0:n], cvt[0:n, :], ident[0:n, 0:n]), reads=["cvt", "ident"], writes=[psk[bank]])
        add("dve", lambda e: e.tensor_copy(out=dst, in_=ps[bank][:, 0:n]), reads=[psk[bank]], writes=[dkey])

    load_colvec(g1T[:], "g1T", g1_d, 16, "ld1")
    load_colvec(g2T[:], "g2T", g2_d, 16, "ld2")

    def evac_alt(i):
        return "act" if i % 2 == 0 else "dve"

    def copy_op(eng, out, in_):
        if eng == "act":
            return lambda e: e.activation(out=out, in_=in_, func=AF.Identity)
        return lambda e: e.tensor_copy(out=out, in_=in_)

    m1 = S.mark()
    c_sb = S.sb("c_sb", [P, D], F32)
    scT = S.sb("scT", [P, KC, P], BF16)
    wbuf = [S.sb(f"wbuf{i}", [P, KC, 512], BF16) for i in range(2)]
    brow = [S.sb(f"brow{i}", [P, 512], F32) for i in range(2)]
    mst = [S.sb(f"mst{i}", [P, 512], F32) for i in range(2)]
    msS = [S.sb(f"msS{i}", [P, 4, 64], F32) for i in range(2)]
    add("sp", lambda e: e.dma_start(out=c_sb[:], in_=c_tok), writes=["c_sb"], dma="ld3")
    add("act", lambda e: e.activation(out=c_sb[:], in_=c_sb[:], func=AF.Silu), reads=["c_sb"], writes=["c_sb"])
    for c in range(KC):
        add("pe", lambda e, c=c: e.transpose(ps[c // 4][:, (c % 4) * P:(c % 4 + 1) * P], c_sb[:, c * P:(c + 1) * P], ident[:]),
            reads=["c_sb", "ident"], writes=[psk[c // 4]])
    for b in range(4):
        add(evac_alt(b), copy_op(evac_alt(b), scT[:, 4 * b:4 * b + 4, :].rearrange("p a n -> p (a n)"), ps[b][:, :]),
            reads=[psk[b]], writes=["scT"])

    def ada_load(g):
        s = g % 2
        add("pool", lambda e: e.dma_start(out=wbuf[s][:], in_=w_ada[:, g * 512:(g + 1) * 512].rearrange("(c p) n -> p c n", p=P)),
            writes=[f"wbuf{s}"], dma=f"w{s}")
        add("sp", lambda e: e.dma_start(out=brow[s][:], in_=b_ada[0:1, g * 512:(g + 1) * 512].partition_broadcast(P).rearrange("p o n -> p (o n)")),
            writes=[f"brow{s}"], dma=f"br{s}")

    ada_load(0)
    MI = {0: 0, 1: 1, 3: 2, 4: 3}
    for g in range(24):
        s = g % 2
        if g + 1 < 24:
            ada_load(g + 1)
        bank = 4 + s
        for c in range(KC):
            add("pe", lambda e, c=c, s=s, bank=bank: e.matmul(ps[bank][:, :], lhsT=scT[:, c, :], rhs=wbuf[s][:, c, :], start=(c == 0), stop=(c == KC - 1)),
                reads=["scT", f"wbuf{s}"], writes=[psk[bank]])
        add("dve", lambda e, s=s, bank=bank: e.tensor_tensor(out=mst[s][:], in0=ps[bank][:, :], in1=brow[s][:], op=ALU.add),
            reads=[psk[bank], f"brow{s}"], writes=[f"mst{s}"])
        add("sp", lambda e, s=s, g=g: e.dma_start(out=mods_tm[:, g * 512:(g + 1) * 512], in_=mst[s][:]), reads=[f"mst{s}"], dma=f"st{s}")
        m = g // 4
        if m in MI:
            mi = MI[m]
            tb = 6 + s
            for k in range(4):
                add("pe", lambda e, k=k, s=s, tb=tb: e.transpose(ps[tb][:, k * P:(k + 1) * P], mst[s][:, k * P:(k + 1) * P], ident[:]),
                    reads=[f"mst{s}", "ident"], writes=[psk[tb]])
            ch0 = mi * 16 + (g % 4) * 4
            add("dve", lambda e, tb=tb, ch0=ch0: e.tensor_copy(out=modP[:, ch0:ch0 + 4], in_=ps[tb][:, :].rearrange("p (a n) -> p a n", a=4)[:, :, 64]),
                reads=[psk[tb]], writes=["modP"])
            add("act", lambda e, tb=tb, s=s: e.activation(out=msS[s][:], in_=ps[tb][:, :].rearrange("p (a n) -> p a n", a=4)[:, :, 0:64], func=AF.Identity),
                reads=[psk[tb], "modP"], writes=[f"msS{s}"])
            add("sp", lambda e, s=s, ch0=ch0: e.dma_start(out=modS_scr[ch0:ch0 + 4].rearrange("a p n -> p a n"), in_=msS[s][:]),
                reads=[f"msS{s}"], dma=f"sm{s}")
    add("dve", lambda e: e.scalar_tensor_tensor(out=A1P[:], in0=modP[:, 16:32], scalar=1.0, in1=g1T[:], op0=ALU.add, op1=ALU.mult),
        reads=["modP", "g1T"], writes=["A1P"])
    add("dve", lambda e: e.scalar_tensor_tensor(out=A2P[:], in0=modP[:, 48:64], scalar=1.0, in1=g2T[:], op0=ALU.add, op1=ALU.mult),
        reads=["modP", "g2T"], writes=["A2P"])
    S.barrier()
    S.release(m1)
    if stop_after <= 1:
        return finish()

    def load_modS(mi_scale, mi_shift, gT, AS, shS, tag):
        add("sp", lambda e: e.dma_start(out=AS[:], in_=modS_scr[mi_scale * 16:(mi_scale + 1) * 16].rearrange("a p n -> p a n")),
            writes=[tag + "AS"], dma="ld4")
        add("sp", lambda e: e.dma_start(out=shS[:], in_=modS_scr[mi_shift * 16:(mi_shift + 1) * 16].rearrange("a p n -> p a n")),
            writes=[tag + "shS"], dma="ld5")
        for c in range(KC):
            add("dve", lambda e, c=c: e.tensor_scalar(out=AS[:, c, :], in0=AS[:, c, :], scalar1=1.0, scalar2=gT[:, c:c + 1], op0=ALU.add, op1=ALU.mult),
                reads=[tag + "AS"], writes=[tag + "AS"])

    class Front:
        def __init__(self, tag, AP_, shP_off, AS, shS, nbuf=2):
            self.tag = tag
            self.AP_ = AP_
            self.shoff = shP_off
            self.AS = AS
            self.shS = shS
            self.xb = [S.sb(f"{tag}xb{i}", [P, D], F32) for i in range(nbuf)]
            self.xs = S.sb(f"{tag}xs", [P, D], F32)
            self.junk = S.sb(f"{tag}junk", [P, D], BF16)
            self.ss = S.sb(f"{tag}ss", [P, 2], F32)
            self.tmpS = S.sb(f"{tag}tmpS", [P, 4, 64], F32)
            self.nbuf = nbuf
            self.i = 0

        def load(self, src, n, slot):
            add("sp", lambda e: e.dma_start(out=self.xb[slot][0:n, :], in_=src), writes=[f"{self.tag}xb{slot}"], dma=f"x{slot}")

        def norm(self, n, slot, src_key=None, xt=None):
            tag = self.tag
            xt = self.xb[slot] if xt is None else xt
            sk = f"{tag}xb{slot}" if src_key is None else src_key
            add("act", lambda e: e.activation(out=self.junk[0:n, :], in_=xt[0:n, :], func=AF.Square, accum_out=self.ss[0:n, 0:1]),
                reads=[sk], writes=[tag + "junk", tag + "ss"])
            add("act", lambda e: e.activation(out=self.ss[0:n, 1:2], in_=self.ss[0:n, 0:1], func=AF.Sqrt, scale=1.0 / D, bias=EPS_AP[0:n, :]),
                reads=[tag + "ss", "eps"], writes=[tag + "ss1"])
            add("dve", lambda e: e.reciprocal(out=self.ss[0:n, 1:2], in_=self.ss[0:n, 1:2]), reads=[tag + "ss1"], writes=[tag + "ss1"])
            add("act", lambda e: e.activation(out=self.xs[0:n, :], in_=xt[0:n, :], func=AF.Identity, scale=self.ss[0:n, 1:2]),
                reads=[sk, tag + "ss1"], writes=[tag + "xs"])

        def transpose_mod(self, n, dst, dst_key, sample, banks=(0, 1, 2, 3)):
            tag = self.tag
            for c in range(KC):
                b = banks[c // 4]
                add("pe", lambda e, c=c, b=b: e.transpose(ps[b][:, (c % 4) * P:(c % 4) * P + n], self.xs[0:n, c * P:(c + 1) * P], ident[0:n, 0:n]),
                    reads=[tag + "xs", "ident"], writes=[psk[b]])
            if not sample:
                for c in range(KC):
                    b = banks[c // 4]
                    src = ps[b][:, (c % 4) * P:(c % 4) * P + n]
                    if c % 2 == 0:
                        add("act", lambda e, c=c, src=src: e.activation(out=dst[:, c, 0:n], in_=src, func=AF.Identity,
                                                                         scale=self.AP_[:, c:c + 1], bias=modP[:, self.shoff + c:self.shoff + c + 1]),
                            reads=[psk[b], "A1P", "A2P", "modP"], writes=[dst_key])
                    else:
                        add("dve", lambda e, c=c, src=src: e.tensor_scalar(out=dst[:, c, 0:n], in0=src, scalar1=self.AP_[:, c:c + 1],
                                                                           scalar2=modP[:, self.shoff + c:self.shoff + c + 1], op0=ALU.mult, op1=ALU.add),
                            reads=[psk[b], "A1P", "A2P", "modP"], writes=[dst_key])
            else:
                for bi in range(4):
                    b = banks[bi]
                    src = ps[b][:, :].rearrange("p (a n) -> p a n", a=4)[:, :, 0:64]
                    add("dve", lambda e, bi=bi, src=src: e.tensor_tensor(out=self.tmpS[:], in0=src, in1=self.AS[:, 4 * bi:4 * bi + 4, :], op=ALU.mult),
                        reads=[psk[b], tag + "AS"], writes=[tag + "tmpS"])
                    add("pool", lambda e, bi=bi: e.tensor_tensor(out=dst[:, 4 * bi:4 * bi + 4, 0:64], in0=self.tmpS[:], in1=self.shS[:, 4 * bi:4 * bi + 4, :], op=ALU.add),
                        reads=[tag + "tmpS", tag + "shS"], writes=[dst_key])

    EPS_AP = S.sb("eps_ap", [P, 1], F32)
    add("pool", lambda e: e.memset(EPS_AP[:], EPS), writes=["eps"])

    m2 = S.mark()
    uT = S.sb("uT", [P, KC, NALL], BF16)
    pinT = S.sb("pinT", [P, 8, 8, 144], F32)
    pinS = S.sb("pinS", [P, 8, 16, 20], F32)
    m2b = S.mark()
    A1S = S.sb("A1S", [P, KC, 64], F32)
    sh1S = S.sb("sh1S", [P, KC, 64], F32)
    load_modS(1, 0, g1T, A1S, sh1S, "f1")
    fr = Front("f1", A1P, 0, A1S, sh1S)
    tiles = [(x_own[i * P:(i + 1) * P, :], P, i * P, False) for i in range(8)]
    tiles.append((x_own[1024:1088, :], 64, 1024, True))
    tiles.append((x_own[1088:1216, :], P, 1088, False))
    fr.load(tiles[0][0], tiles[0][1], 0)
    wwi = S.sb("wwi", [P, KC, 16], BF16)
    add("pool", lambda e: e.dma_start(out=wwi[:], in_=w_in[:, 5184:5200].rearrange("(c p) n -> p c n", p=P)), writes=["wwi"], dma="w0")
    for ti, (src, n, t0, smp) in enumerate(tiles):
        slot = ti % 2
        if ti + 1 < len(tiles):
            fr.load(tiles[ti + 1][0], tiles[ti + 1][1], (ti + 1) % 2)
        fr.norm(n, slot)
        fr.transpose_mod(n, uT[:, :, t0:t0 + n], "uT", smp)
        if ti < 9:
            for c in range(KC):
                add("pe", lambda e, c=c, t0=t0, n=n: e.matmul(ps[4][0:n, 0:16], lhsT=uT[:, c, t0:t0 + n], rhs=wwi[:, c, :], start=(c == 0), stop=(c == KC - 1)),
                    reads=["uT", "wwi"], writes=[psk[4]])
            add("act", lambda e, ti=ti, n=n: e.activation(out=absw[0:n, ti, :], in_=ps[4][0:n, 0:16], func=AF.Abs, scale=1.0 / 32.0),
                reads=[psk[4]], writes=["absw"])
            add("act", lambda e, ti=ti, n=n: e.activation(out=sgnw[0:n, ti, :], in_=ps[4][0:n, 0:16], func=AF.Sign),
                reads=[psk[4]], writes=["sgnw"])
            if ti == 8:
                wst = S.sb("wst", [P, 16], F32)
                add("dve", lambda e: e.tensor_copy(out=wst[0:64, :], in_=ps[4][0:64, 0:16]), reads=[psk[4]], writes=["wst"])
                add("sp", lambda e: e.dma_start(out=wis_scr, in_=wst[0:64, :]), reads=["wst"], dma="st2")
    S.barrier()
    S.release(m2b)
    wb2 = [S.sb(f"wb2_{i}", [P, KC, 512], BF16) for i in range(2)]
    stg = [S.sb(f"stg{i}", [P, NTOK], F32) for i in range(2)]
    stgb = [S.sb(f"stgb{i}", [P, NTOK], BF16) for i in range(2)]
    groups = []
    for gi in range(2):
        groups.append(("pool", gi * 512, gi * 4))
    for gi in range(4):
        groups.append(("q", 1024 + gi * 512, gi * 4))
    for gi in range(2):
        groups.append(("qi", 4096 + gi * 512, gi * 4))
    for gi in range(4):
        groups.append(("ga", 5200 + gi * 512, gi * 4))
    for gi in range(4):
        groups.append(("gb", 7248 + gi * 512, gi * 4))

    def f_load(gi):
        kind, col0, ch0 = groups[gi]
        s = gi % 2
        add("pool", lambda e: e.dma_start(out=wb2[s][:], in_=w_in[:, col0:col0 + 512].rearrange("(c p) n -> p c n", p=P)),
            writes=[f"wb2_{s}"], dma=f"w{s}")

    f_load(0)
    cc_global = 0
    for gi, (kind, col0, ch0) in enumerate(groups):
        s = gi % 2
        if gi + 1 < len(groups):
            f_load(gi + 1)
        for k in range(4):
            ch = ch0 + k
            bset = (cc_global % 2) * 3
            cc_global += 1
            tgs = [(0, 512), (512, 512), (1024, 192 if kind == "pool" else 64)]
            for bi, (t0, tn) in enumerate(tgs):
                bank = bset + bi
                for c in range(KC):
                    add("pe", lambda e, c=c, s=s, k=k, t0=t0, tn=tn, bank=bank: e.matmul(ps[bank][:, 0:tn], lhsT=wb2[s][:, c, k * P:(k + 1) * P],
                                                                                   rhs=uT[:, c, t0:t0 + tn], start=(c == 0), stop=(c == KC - 1)),
                        reads=["uT", f"wb2_{s}"], writes=[psk[bank]])
            ss_ = ch % 2
            if kind == "pool":
                for bi in range(2):
                    add(evac_alt(bi), copy_op(evac_alt(bi), pinT[:, ch, 4 * bi:4 * bi + 4, 16:144], ps[bset + bi][:, :].rearrange("p (a n) -> p a n", a=4)),
                        reads=[psk[bset + bi]], writes=["pinT"])
                add("act", copy_op("act", pinS[:, ch, :, 16:20], ps[bset + 2][:, 0:64].rearrange("p (a n) -> p a n", n=4)), reads=[psk[bset + 2]], writes=["pinS"])
                add("dve", copy_op("dve", pinT[:, ch, :, 0:16], ps[bset + 2][:, 64:192].rearrange("p (a n) -> p a n", n=16)), reads=[psk[bset + 2]], writes=["pinT"])
            elif kind in ("q", "qi"):
                dstt = qT_scr if kind == "q" else qiT_scr
                for bi, (t0, tn) in enumerate(tgs):
                    add(evac_alt(bi), copy_op(evac_alt(bi), stgb[ss_][:, t0:t0 + tn], ps[bset + bi][:, 0:tn]), reads=[psk[bset + bi]], writes=[f"stgb{ss_}"])
                add("sp", lambda e, ch=ch, ss_=ss_, dstt=dstt: e.dma_start(out=dstt[ch], in_=stgb[ss_][:]), reads=[f"stgb{ss_}"], dma=f"st{ss_}")
            else:
                dstt = sga_scr if kind == "ga" else sgb_scr
                for bi, (t0, tn) in enumerate(tgs):
                    add("act", lambda e, bi=bi, t0=t0, tn=tn, ss_=ss_, bset=bset: e.activation(out=stg[ss_][:, t0:t0 + tn], in_=ps[bset + bi][:, 0:tn], func=AF.Sigmoid),
                        reads=[psk[bset + bi]], writes=[f"stg{ss_}"])
                add("sp", lambda e, ch=ch, ss_=ss_, dstt=dstt: e.dma_start(out=dstt[ch], in_=stg[ss_][:]), reads=[f"stg{ss_}"], dma=f"st{ss_ + 2}")
    S.barrier()
    S.release(m2b)
    if stop_after <= 2:
        return finish()

    dT = S.sb("dT", [P, 8, NTOK], BF16)
    wpg = S.sb("wpg", [P, 8, 256], BF16)
    pscT = S.sb("pscT", [P, 8], F32)
    tA = S.sb("tA", [P, 2, 8, 144], F32)
    tB = S.sb("tB", [P, 2, 8, 144], F32)
    sA = S.sb("sA", [P, 2, 16, 20], F32)
    sB = S.sb("sB", [P, 2, 16, 20], F32)
    hm = S.sb("hm", [P, P], F32)
    invc = S.sb("invc", [P, 4, P], F32)
    stt = S.sb("stt", [P, 1024], F32)
    pstg = S.sb("pstg", [P, 1024], F32)
    postg = [S.sb(f"postg{i}", [P, NTOK], BF16) for i in range(2)]
    add("pool", lambda e: e.dma_start(out=wpg[:], in_=w_pg.rearrange("g (a p) n -> p (g a) n", p=P)), writes=["wpg"], dma="w0")
    load_colvec(pscT[:], "pscT", psc_d, 8, "ld0")
    add("sp", lambda e: e.dma_start(out=hm[:], in_=meta_d[:, 144:272]), writes=["hm"], dma="ld1")
    add("sp", lambda e: e.dma_start(out=invc[:, 0, :], in_=meta_d[:, 16:144]), writes=["invc"], dma="ld2")
    for g in (3, 2, 1, 0):
        w = float(2 ** (g + 1))
        add("dve", lambda e, g=g, w=w: e.tensor_scalar(out=invc[:, g, :], in0=invc[:, 0, :], scalar1=1.0, scalar2=w, op0=ALU.add, op1=ALU.min),
            reads=["invc"], writes=["invc"])
        add("dve", lambda e, g=g: e.reciprocal(out=invc[:, g, :], in_=invc[:, g, :]), reads=["invc"], writes=["invc"])
    for ch in range(8):
        add("pool", lambda e, ch=ch: e.tensor_tensor(out=pinT[:, ch, :, 0:16], in0=pinT[:, ch, :, 0:16], in1=hm[:, :].rearrange("p (a n) -> p a n", n=16), op=ALU.mult),
            reads=["pinT", "hm"], writes=["pinT"])
    for half in range(2):
        add("sp", lambda e, half=half: e.dma_start(out=stt[0:120, :], in_=stp_d[half * 120:(half + 1) * 120, :]), writes=["stt"], dma="ld3")
        for ch in range(8):
            b = ch // 4
            add("pe", lambda e, ch=ch, b=b: e.transpose(ps[b][:, (ch % 4) * P:(ch % 4) * P + 120], stt[0:120, ch * P:(ch + 1) * P], ident[0:120, 0:120]),
                reads=["stt", "ident"], writes=[psk[b]])
        for ch in range(8):
            b = ch // 4
            add(evac_alt(ch), copy_op(evac_alt(ch), pinS[:, ch, half * 8:(half + 1) * 8, 1:16], ps[b][:, (ch % 4) * P:(ch % 4) * P + 120].rearrange("p (a n) -> p a n", n=15)),
                reads=[psk[b]], writes=["pinS"])
    add("sp", lambda e: e.dma_start(out=pool_s.rearrange("(s r) c -> s r c", r=15)[:, 0:11, :], in_=stp_d.rearrange("(s r) c -> s r c", r=15)[:, 4:15, :]), dma="st0")
    for ch in range(8):
        b = ch // 4
        add("pe", lambda e, ch=ch, b=b: e.transpose(ps[b][:, (ch % 4) * P:(ch % 4 + 1) * P], pinT[:, ch, 7, 16:144], ident[:]),
            reads=["pinT", "ident"], writes=[psk[b]])
    for b in range(2):
        add(evac_alt(b), copy_op(evac_alt(b), pstg[:, b * 512:(b + 1) * 512], ps[b][:, :]), reads=[psk[b]], writes=["pstg"])
    add("sp", lambda e: e.dma_start(out=pool_last, in_=pstg[113:128, :]), reads=["pstg"], dma="st1")
    pstg2 = S.sb("pstg2", [P, 1024], F32)
    pinSc = S.sb("pinSc", [P, 8, 64], F32)
    for ch in range(8):
        add("dve", lambda e, ch=ch: e.tensor_copy(out=pinSc[:, ch, :].rearrange("p (s t) -> p s t", t=4), in_=pinS[:, ch, :, 16:20]), reads=["pinS"], writes=["pinSc"])
    for ch in range(8):
        b = 2 + ch // 4
        add("pe", lambda e, ch=ch, b=b: e.transpose(ps[b][0:64, (ch % 4) * P:(ch % 4 + 1) * P], pinSc[:, ch, :], ident[:]),
            reads=["pinSc", "ident"], writes=[psk[b]])
    for b in range(2):
        add(evac_alt(b), copy_op(evac_alt(b), pstg2[0:64, b * 512:(b + 1) * 512], ps[2 + b][0:64, :]), reads=[psk[2 + b]], writes=["pstg2"])
    add("sp", lambda e: e.dma_start(out=pss_scr, in_=pstg2[0:64, :]), reads=["pstg2"], writes=["pss_scr"], dma="st2")
    add("sp", lambda e: e.dma_start(out=pool_s.rearrange("(s r) c -> s r c", r=15)[:, 11:15, :], in_=pss_scr.rearrange("(s t) c -> s t c", t=4)),
        reads=["pss_scr"], dma="st3")
    for g in range(4):
        w = 2 ** (g + 1)
        srcP = pinT[:, 2 * g:2 * g + 2, :, :]
        srcS = pinS[:, 2 * g:2 * g + 2, :, :]
        curP, curS = srcP, srcS
        keyP, keyS = "pinT", "pinS"
        sh = 1
        bufs = [(tA, sA, "tA", "sA"), (tB, sB, "tB", "sB")]
        bi = 0
        while sh < w:
            oP, oS, kP, kS = bufs[bi]
            lo = 16 - w + 2 * sh
            add("dve", lambda e, oP=oP, curP=curP, sh=sh, lo=lo: e.tensor_tensor(out=oP[:, :, :, lo:144], in0=curP[:, :, :, lo:144], in1=curP[:, :, :, lo - sh:144 - sh], op=ALU.add),
                reads=[keyP], writes=[kP])
            add("pool", lambda e, oS=oS, curS=curS, sh=sh, lo=lo: e.tensor_tensor(out=oS[:, :, :, lo:20], in0=curS[:, :, :, lo:20], in1=curS[:, :, :, lo - sh:20 - sh], op=ALU.add),
                reads=[keyS], writes=[kS])
            curP, curS, keyP, keyS = oP, oS, kP, kS
            sh *= 2
            bi ^= 1
        for cc in range(2):
            dv = dT[:, 2 * g + cc, 0:1024].rearrange("p (t n) -> p t n", n=P)
            add("dve", lambda e, curP=curP, srcP=srcP, dv=dv, w=w, cc=cc: e.scalar_tensor_tensor(out=dv[:, 1:8, :], in0=curP[:, cc, 1:8, 16:144], scalar=1.0 / w, in1=srcP[:, cc, 1:8, 16:144],
                                                                                        op0=ALU.mult, op1=ALU.subtract),
                reads=[keyP, "pinT"], writes=["dT"])
            add("dve", lambda e, curP=curP, cc=cc, g=g: e.tensor_tensor(out=curP[:, cc, 0, 16:144], in0=curP[:, cc, 0, 16:144], in1=invc[:, g, :], op=ALU.mult),
                reads=[keyP, "invc"], writes=[keyP])
            add("dve", lambda e, curP=curP, srcP=srcP, cc=cc, g=g: e.tensor_tensor(out=dT[:, 2 * g + cc, 0:P], in0=curP[:, cc, 0, 16:144], in1=srcP[:, cc, 0, 16:144], op=ALU.subtract),
                reads=[keyP, "pinT"], writes=["dT"])
            dvs = dT[:, 2 * g + cc, 1024:1088].rearrange("p (s t) -> p s t", t=4)
            add("dve", lambda e, curS=curS, srcS=srcS, dvs=dvs, w=w, cc=cc: e.scalar_tensor_tensor(out=dvs, in0=curS[:, cc, :, 16:20], scalar=1.0 / w, in1=srcS[:, cc, :, 16:20], op0=ALU.mult, op1=ALU.subtract),
                reads=[keyS, "pinS"], writes=["dT"])
    for ec in range(8):
        g = ec // 2
        bset = (ec % 2) * 3
        tgs = [(0, 512), (512, 512), (1024, 64)]
        for bi, (t0, tn) in enumerate(tgs):
            for cc in range(2):
                add("pe", lambda e, g=g, cc=cc, ec=ec, t0=t0, tn=tn, bank=bset + bi: e.matmul(ps[bank][:, 0:tn], lhsT=wpg[:, 2 * g + cc, (ec % 2) * P:(ec % 2 + 1) * P],
                                                                                        rhs=dT[:, 2 * g + cc, t0:t0 + tn], start=(cc == 0), stop=(cc == 1)),
                    reads=["dT", "wpg"], writes=[psk[bset + bi]])
        s_ = ec % 2
        for bi, (t0, tn) in enumerate(tgs):
            add("act", lambda e, ec=ec, bi=bi, t0=t0, tn=tn, s_=s_, bset=bset: e.activation(out=postg[s_][:, t0:t0 + tn], in_=ps[bset + bi][:, 0:tn], func=AF.Identity, scale=pscT[:, ec:ec + 1]),
                reads=[psk[bset + bi], "pscT"], writes=[f"postg{s_}"])
        add("sp", lambda e, ec=ec, s_=s_: e.dma_start(out=po_scr[ec], in_=postg[s_][:]), reads=[f"postg{s_}"], dma=f"st{3 + s_}")
    S.barrier()
    S.release(m2)
    if stop_after <= 3:
        return finish()

    m4 = S.mark()
    KTn = S.sb("KTn", [P, 4, 64], BF16)
    Vn = S.sb("Vn", [P, 512], BF16)
    kiTn = S.sb("kiTn", [P, 64], BF16)
    acc = S.sb("acc", [P, 4096], F32)
    pen = S.sb("pen", [P, 4096], BF16)
    mask01 = S.sb("mask01", [P, 4096], BF16)
    bs = S.sb("bs", [P, 8], F32)
    wtab = S.sb("wtab", [P, NIT + 1], F32)
    wtab2 = S.sb("wtab2", [P, NIT + 1], F32)
    m4big = S.mark()
    KT = S.sb("KT", [P, 4, 4096], BF16)
    Vs = S.sb("Vs", [P, 32, 512], BF16)
    kiT2 = S.sb("kiT2", [P, 4096], BF16)
    m4b = S.mark()
    A1S = S.sb("A1Sb", [P, KC, 64], F32)
    sh1S = S.sb("sh1Sb", [P, KC, 64], F32)
    load_modS(1, 0, g1T, A1S, sh1S, "f4")
    fr = Front("f4", A1P, 0, A1S, sh1S)
    wkvi = S.sb("wkvi", [P, KC, 1088], BF16)
    uTt = S.sb("uTt", [P, KC, P], BF16)
    kvst = [S.sb(f"kvst{i}", [P, 1088], F32) for i in range(2)]
    kb16 = S.sb("kb16", [P, 640], BF16)
    for gi in range(2):
        add("pool", lambda e, gi=gi: e.dma_start(out=wkvi[:, :, gi * 512:(gi + 1) * 512], in_=w_in[:, 3072 + gi * 512:3072 + (gi + 1) * 512].rearrange("(c p) n -> p c n", p=P)),
            writes=[f"wkvi{gi}"], dma=f"w{gi}")
    add("pool", lambda e: e.dma_start(out=wkvi[:, :, 1024:1088], in_=w_in[:, 5120:5184].rearrange("(c p) n -> p c n", p=P)), writes=["wkvi2"], dma="w2")
    tilesA = [(x_seq[i * P:(i + 1) * P, :], P, False) for i in range(32)]
    tilesA.append((x_own[1024:1088, :], 64, True))
    fr.load(tilesA[0][0], tilesA[0][1], 0)
    psb7 = ps[7][:, :].bitcast(BF16)
    for ti, (src, n, smp) in enumerate(tilesA):
        slot = ti % 2
        if ti + 1 < len(tilesA):
            fr.load(tilesA[ti + 1][0], tilesA[ti + 1][1], (ti + 1) % 2)
        fr.norm(n, slot)
        fr.transpose_mod(n, uTt[:, :, 0:n], "uTt", smp)
        for wi_, (c0, cn, bank) in enumerate(((0, 512, 4), (512, 512, 5), (1024, 64, 6))):
            for c in range(KC):
                add("pe", lambda e, c=c, c0=c0, cn=cn, bank=bank, n=n: e.matmul(ps[bank][0:n, 0:cn], lhsT=uTt[:, c, 0:n], rhs=wkvi[:, c, c0:c0 + cn], start=(c == 0), stop=(c == KC - 1)),
                    reads=["uTt", f"wkvi{wi_}"], writes=[psk[bank]])
        ks_ = ti % 2
        add("act", lambda e, ks_=ks_, n=n: e.activation(out=kvst[ks_][0:n, 0:512], in_=ps[4][0:n, :], func=AF.Identity), reads=[psk[4]], writes=[f"kvst{ks_}"])
        add("act", lambda e, ks_=ks_, n=n: e.activation(out=kvst[ks_][0:n, 512:1024], in_=ps[5][0:n, :], func=AF.Identity), reads=[psk[5]], writes=[f"kvst{ks_}"])
        add("act", lambda e, ks_=ks_, n=n: e.activation(out=kvst[ks_][0:n, 1024:1088], in_=ps[6][0:n, 0:64], func=AF.Identity), reads=[psk[6]], writes=[f"kvst{ks_}"])
        add("dve", lambda e, n=n: e.tensor_copy(out=kb16[0:n, 0:512], in_=ps[4][0:n, :]), reads=[psk[4]], writes=["kb16"])
        vdst = Vs[0:n, ti, :] if not smp else Vn[0:n, :]
        add("dve", lambda e, n=n, vdst=vdst: e.tensor_copy(out=vdst, in_=ps[5][0:n, :]), reads=[psk[5]], writes=["Vs"])
        add("dve", lambda e, n=n: e.tensor_copy(out=kb16[0:n, 512:576], in_=ps[6][0:n, 0:64]), reads=[psk[6]], writes=["kb16"])
        add("dve", lambda e, n=n: e.tensor_copy(out=kb16[0:n, 576:640], in_=ps[6][0:n, 0:64]), reads=[psk[6]], writes=["kb16"])
        if not smp:
            r0 = ti * P
            add("sp", lambda e, ks_=ks_, r0=r0: e.dma_start(out=k_all[r0:r0 + P, :], in_=kvst[ks_][:, 0:512]), reads=[f"kvst{ks_}"], dma=f"st{ks_}")
            add("sp", lambda e, ks_=ks_, r0=r0: e.dma_start(out=v_all[r0:r0 + P, :], in_=kvst[ks_][:, 512:1024]), reads=[f"kvst{ks_}"], dma=f"st{ks_}")
            add("sp", lambda e, ks_=ks_, r0=r0: e.dma_start(out=ki_all[r0:r0 + P, :], in_=kvst[ks_][:, 1024:1088]), reads=[f"kvst{ks_}"], dma=f"st{ks_}")
        else:
            add("sp", lambda e, ks_=ks_: e.dma_start(out=ks_o, in_=kvst[ks_][0:64, 0:512]), reads=[f"kvst{ks_}"], dma=f"st{ks_}")
            add("sp", lambda e, ks_=ks_: e.dma_start(out=vs_o, in_=kvst[ks_][0:64, 512:1024]), reads=[f"kvst{ks_}"], dma=f"st{ks_}")
            add("sp", lambda e, ks_=ks_: e.dma_start(out=kis_o, in_=kvst[ks_][0:64, 1024:1088]), reads=[f"kvst{ks_}"], dma=f"st{ks_}")
        for j in range(5):
            add("pe", lambda e, j=j, n=n: e.transpose(psb7[:, j * P:j * P + n], kb16[0:n, j * P:(j + 1) * P], identb[0:n, 0:n]),
                reads=["kb16", "identb"], writes=[psk[7]])
        kc0 = ti * P
        ktd = KT[:, :, kc0:kc0 + n] if not smp else KTn[:, :, 0:n]
        kid = kiT2[:, kc0:kc0 + n] if not smp else kiTn[:, 0:n]
        add("act", lambda e, n=n, ktd=ktd: e.activation(out=ktd, in_=psb7[:, 0:512].rearrange("p (j n) -> p j n", j=4)[:, :, 0:n], func=AF.Identity),
            reads=[psk[7]], writes=["KT"])
        add("dve", lambda e, n=n, kid=kid: e.tensor_copy(out=kid, in_=psb7[:, 512:512 + n]), reads=[psk[7]], writes=["kiT2"])
    S.barrier()
    S.release(m4b)
    if stop_after <= 4:
        return finish()

    maskT = S.sb("maskT", [P, 32, P], BF16)
    qTt = [S.sb(f"qTt{i}", [P, 16, P], BF16) for i in range(2)]
    qiTt = [S.sb(f"qiTt{i}", [P, 8, P], BF16) for i in range(2)]
    Rb = [S.sb(f"Rb{i}", [P, 512], F32) for i in range(3)]
    Eb = [S.sb(f"Eb{i}", [P, 512], BF16) for i in range(2)]
    Pb = [S.sb(f"Pb{i}", [P, 512], BF16) for i in range(3)]
    rden = S.sb("rden", [P, 512], F32)
    aost = [S.sb(f"aost{i}", [P, 16, P], BF16) for i in range(2)]
    SCALE = float(128 ** -0.5)

    def topk(n, Sw, acc_t, pen_t, mask_t, junk_t, qp_ap):
        add("dve", lambda e: e.tensor_reduce(out=bs[0:n, 0:1], in_=acc_t[0:n, 0:Sw], axis=AX.X, op=ALU.max, apply_absolute_value=True),
            reads=["acc"], writes=["bsM"])
        add("dve", lambda e: e.tensor_scalar(out=bs[0:n, 0:1], in0=bs[0:n, 0:1], scalar1=1.001, scalar2=1e-6, op0=ALU.mult, op1=ALU.add), reads=["bsM"], writes=["bsM"])
        add("dve", lambda e: e.tensor_scalar(out=wtab[0:n, :], in0=pow2[0:n, :], scalar1=bs[0:n, 0:1], scalar2=None, op0=ALU.mult), reads=["bsM", "pow2"], writes=["wtab"])
        add("dve", lambda e: e.tensor_scalar(out=wtab2[0:n, :], in0=wtab[0:n, :], scalar1=2.0, scalar2=None, op0=ALU.mult), reads=["wtab"], writes=["wtab2"])
        add("dve", lambda e: e.tensor_scalar(out=pen_t[0:n, 0:Sw], in0=iota16[0:n, 0:Sw], scalar1=qp_ap, scalar2=NEG, op0=ALU.is_gt, op1=ALU.mult),
            reads=["iota16", "qpos"], writes=["pen"])
        add("dve", lambda e: e.tensor_tensor(out=acc_t[0:n, 0:Sw], in0=acc_t[0:n, 0:Sw], in1=pen_t[0:n, 0:Sw], op=ALU.add), reads=["acc", "pen"], writes=["acc"])
        add("dve", lambda e: e.tensor_scalar(out=bs[0:n, 1:2], in0=bs[0:n, 0:1], scalar1=0.0, scalar2=None, op0=ALU.mult), reads=["bsM"], writes=["bsmid"])
        for k in range(NIT):
            add("dve", lambda e: e.tensor_scalar(out=junk_t[0:n, 0:Sw], in0=acc_t[0:n, 0:Sw], scalar1=bs[0:n, 1:2], scalar2=None, op0=ALU.is_ge, op1=ALU.add, accum_out=bs[0:n, 2:3]),
                reads=["acc", "bsmid"], writes=["pen", "bscnt"])
            add("dve", lambda e, k=k: e.tensor_scalar(out=bs[0:n, 3:4], in0=bs[0:n, 2:3], scalar1=255.5, scalar2=wtab2[0:n, k + 1:k + 2], op0=ALU.is_ge, op1=ALU.mult),
                reads=["bscnt", "wtab2"], writes=["bsf"])
            add("dve", lambda e, k=k: e.scalar_tensor_tensor(out=bs[0:n, 1:2], in0=bs[0:n, 1:2], scalar=wtab[0:n, k + 1:k + 2], in1=bs[0:n, 3:4], op0=ALU.subtract, op1=ALU.add),
                reads=["bsmid", "bsf", "wtab"], writes=["bsmid"])
        add("dve", lambda e: e.tensor_tensor(out=bs[0:n, 4:5], in0=bs[0:n, 1:2], in1=wtab[0:n, NIT:NIT + 1], op=ALU.subtract), reads=["bsmid", "wtab"], writes=["bsthr"])
        add("dve", lambda e: e.tensor_scalar(out=mask_t[0:n, 0:Sw], in0=acc_t[0:n, 0:Sw], scalar1=bs[0:n, 4:5], scalar2=None, op0=ALU.is_ge), reads=["acc", "bsthr"], writes=["mask01"])

    def q_load(i):
        s = i % 2
        add("sp", lambda e: e.dma_start(out=qTt[s][:], in_=qT_scr[:, :, i * P:(i + 1) * P].rearrange("c p n -> p c n")), writes=[f"qTt{s}"], dma=f"ld{s}")
        add("sp", lambda e: e.dma_start(out=qiTt[s][:], in_=qiT_scr[:, :, i * P:(i + 1) * P].rearrange("c p n -> p c n")), writes=[f"qiTt{s}"], dma=f"ld{2 + s}")

    with nc.allow_non_contiguous_dma(reason="256B runs for per-tile q loads"):
        q_load(0)
        rcount = 0
        pcount = 0
        for i in range(8):
            s = i % 2
            nkb = NKB[i]
            Sw = nkb * P
            if i + 1 < 8:
                q_load(i + 1)
            add("pool", lambda e, Sw=Sw: e.memset(acc[:, 0:Sw], 0.0), writes=["acc"])
            for kc in range(nkb // 4):
                for h in range(16):
                    c, hp = h // 2, h % 2
                    bank = rcount % 3
                    rb = rcount % 3
                    rcount += 1
                    add("pe", lambda e, c=c, hp=hp, kc=kc, s=s, bank=bank: e.matmul(ps[bank][:, :], lhsT=qiTt[s][hp * 64:(hp + 1) * 64, c, :],
                                                                                rhs=kiT2[hp * 64:(hp + 1) * 64, kc * 512:(kc + 1) * 512], start=True, stop=True),
                        reads=[f"qiTt{s}", "kiT2"], writes=[psk[bank]])
                    add("act", lambda e, bank=bank, rb=rb, i=i, h=h: e.activation(out=Rb[rb][:], in_=ps[bank][:, :], func=AF.Relu, scale=absw[:, i, h:h + 1]),
                        reads=[psk[bank], "absw"], writes=[f"Rb{rb}"])
                    add("dve", lambda e, rb=rb, i=i, h=h, kc=kc: e.scalar_tensor_tensor(out=acc[:, kc * 512:(kc + 1) * 512], in0=Rb[rb][:], scalar=sgnw[:, i, h:h + 1],
                                                                                       in1=acc[:, kc * 512:(kc + 1) * 512], op0=ALU.mult, op1=ALU.add),
                        reads=[f"Rb{rb}", "sgnw", "acc"], writes=["acc"])
            topk(P, Sw, acc, pen, mask01, pen, qpos[:, i:i + 1])
            for kb in range(nkb):
                bank = 3 if (kb // 8) % 2 == 0 else 4
                pb_ = ps[bank][:, :].bitcast(BF16)
                add("pe", lambda e, kb=kb, pb_=pb_: e.transpose(pb_[:, (kb % 8) * P:(kb % 8 + 1) * P], mask01[:, kb * P:(kb + 1) * P], identb[:]),
                    reads=["mask01", "identb"], writes=[psk[bank]])
                if kb % 8 == 7 or kb == nkb - 1:
                    k0 = (kb // 8) * 8
                    nk = kb - k0 + 1
                    add("act", lambda e, k0=k0, nk=nk, pb_=pb_: e.activation(out=maskT[:, k0:k0 + nk, :].rearrange("p a n -> p (a n)"), in_=pb_[:, 0:nk * P], func=AF.Identity),
                        reads=[psk[bank]], writes=["maskT"])
            for j in range(4):
                for kb in range(nkb):
                    sb_ = 3 + (pcount % 2)
                    eb = pcount % 2
                    pbi = pcount % 3
                    pcount += 1
                    add("pe", lambda e, j=j, kb=kb, s=s, sb_=sb_: e.matmul(ps[sb_][:, :], lhsT=KT[:, j, kb * P:(kb + 1) * P], rhs=qTt[s][:, 4 * j:4 * j + 4, :].rearrange("p a n -> p (a n)"),
                                                                      start=True, stop=True),
                        reads=["KT", f"qTt{s}"], writes=[psk[sb_]])
                    add("act", lambda e, sb_=sb_, eb=eb: e.activation(out=Eb[eb][:], in_=ps[sb_][:, :], func=AF.Exp, scale=SCALE), reads=[psk[sb_]], writes=[f"Eb{eb}"])
                    add("dve", lambda e, eb=eb, pbi=pbi, kb=kb: e.tensor_tensor(out=Pb[pbi][:].rearrange("p (a n) -> p a n", a=4), in0=Eb[eb][:].rearrange("p (a n) -> p a n", a=4),
                                                                                in1=maskT[:, kb:kb + 1, :].to_broadcast([P, 4, P]), op=ALU.mult),
                        reads=[f"Eb{eb}", "maskT"], writes=[f"Pb{pbi}"])
                    add("pe", lambda e, j=j, kb=kb, pbi=pbi, nkb=nkb: e.matmul(ps[5][:, :], lhsT=Vs[:, kb, j * P:(j + 1) * P], rhs=Pb[pbi][:], start=(kb == 0), stop=(kb == nkb - 1)),
                        reads=["Vs", f"Pb{pbi}"], writes=[psk[5]])
                    add("pe", lambda e, kb=kb, pbi=pbi, nkb=nkb: e.matmul(ps[6][:, :], lhsT=onesb[:], rhs=Pb[pbi][:], start=(kb == 0), stop=(kb == nkb - 1)),
                        reads=["onesb", f"Pb{pbi}"], writes=[psk[6]])
                add("dve", lambda e: e.reciprocal(out=rden[:], in_=ps[6][:, :]), reads=[psk[6]], writes=["rden"])
                add("dve", lambda e, j=j, s=s: e.tensor_tensor(out=aost[s][:, 4 * j:4 * j + 4, :].rearrange("p a n -> p (a n)"), in0=ps[5][:, :], in1=rden[:], op=ALU.mult),
                    reads=[psk[5], "rden"], writes=[f"aost{s}"])
            add("sp", lambda e, i=i, s=s: e.dma_start(out=ao_scr[:, :, i * P:(i + 1) * P].rearrange("c p n -> p c n"), in_=aost[s][:]), reads=[f"aost{s}"], dma=f"st{s}")
    S.barrier()
    if stop_after <= 5:
        return finish()

    S.release(m4big)
    ptb = S.sb("ptb", [P, 256], I32)
    ridx = S.sb("ridx", [P, 256], I32)
    rowid = S.sb("rowid", [P, 1], F32)
    W2 = [S.sb(f"W2_{hp}", [32, 16], F32) for hp in range(2)]
    Tsel = S.sb("Tsel", [32, 4], F32)
    SelW = [S.sb(f"SelW{hp}", [32, 16, 64], F32) for hp in range(2)]
    RepSel = S.sb("RepSel", [64, 16, 64], BF16)
    BDj = S.sb("BDj", [64, 4], F32)
    qTs = S.sb("qTs", [P, 16, 64], BF16)
    qiTs = S.sb("qiTs", [P, 8, 64], BF16)
    ikp = [S.sb(f"ikp{i}", [P, 16, P], BF16) for i in range(2)]
    kiTs = S.sb("kiTs", [P, 2048], BF16)
    Rs = [[S.sb(f"Rs{i}_{hp}", [32, 512], F32) for hp in range(2)] for i in range(2)]
    mbias = S.sb("mbias", [64, 2052], BF16)
    cf = S.sb("cf", [64, 1024], F32)
    rf = S.sb("rf", [64, 1024], F32)
    add("sp", lambda e: e.dma_start(out=ptb[:], in_=ptm_d), writes=["ptb"], dma="ld0")
    add("pool", lambda e: e.iota(rowid[:], pattern=[[0, 1]], base=0, channel_multiplier=1, allow_small_or_imprecise_dtypes=True), writes=["rowid"])
    add("dve", lambda e: e.tensor_scalar(out=ridx[:], in0=ptb[:], scalar1=128.0, scalar2=rowid[:, 0:1], op0=ALU.mult, op1=ALU.add), reads=["ptb", "rowid"], writes=["ridx"])
    with nc.allow_non_contiguous_dma(reason="small permuted loads"):
        wv = wis_scr.rearrange("(s t) h -> t s h", t=4)
        for hp in range(2):
            for c in range(8):
                add("sp", lambda e, hp=hp, c=c: e.dma_start(out=W2[hp][4 * c:4 * c + 4, :], in_=wv[:, :, 2 * c + hp]), writes=[f"W2_{hp}_{c}"], dma="ld1")
        add("sp", lambda e: e.dma_start(out=qTs[:], in_=qT_scr[:, :, 1024:1088].rearrange("c p n -> p c n")), writes=["qTs"], dma="ld2")
        add("sp", lambda e: e.dma_start(out=qiTs[:], in_=qiT_scr[:, :, 1024:1088].rearrange("c p n -> p c n")), writes=["qiTs"], dma="ld3")
    qiC = S.sb("qiC", [P, 16, 32], BF16)
    add("dve", lambda e: e.tensor_copy(out=qiC[:].rearrange("p s (c t) -> p s c t", t=4), in_=qiTs[:].rearrange("p c (s t) -> p s c t", t=4)), reads=["qiTs"], writes=["qiC"])
    for hp in range(2):
        add("dve", lambda e, hp=hp: e.tensor_scalar(out=W2[hp][:], in0=W2[hp][:], scalar1=1.0 / 32.0, scalar2=None, op0=ALU.mult),
            reads=[f"W2_{h2}_{c}" for c in range(8) for h2 in range(2)], writes=[f"W2s{hp}"])
    add("pool", lambda e: e.iota(cf[0:32, 0:32], pattern=[[4, 8], [1, 4]], base=0, channel_multiplier=0, allow_small_or_imprecise_dtypes=True), writes=["cf"])
    add("pool", lambda e: e.iota(rf[0:32, 0:32], pattern=[[0, 32]], base=0, channel_multiplier=1, allow_small_or_imprecise_dtypes=True), writes=["rf"])
    add("dve", lambda e: e.tensor_tensor(out=cf[0:32, 0:32], in0=cf[0:32, 0:32], in1=rf[0:32, 0:32], op=ALU.is_equal), reads=["cf", "rf"], writes=["cf"])
    add("dve", lambda e: e.tensor_reduce(out=Tsel[:], in_=cf[0:32, 0:32].rearrange("p (c t) -> p t c", t=4), axis=AX.X, op=ALU.add), reads=["cf"], writes=["Tsel"])
    for hp in range(2):
        add("pool", lambda e, hp=hp: e.memset(SelW[hp][:], 0.0), writes=[f"SelW{hp}"])
        for sq in range(16):
            add("dve", lambda e, sq=sq, hp=hp: e.tensor_scalar(out=SelW[hp][:, sq, 4 * sq:4 * sq + 4], in0=Tsel[:], scalar1=W2[hp][:, sq:sq + 1], scalar2=None, op0=ALU.mult),
                reads=["Tsel", f"W2s{hp}", f"SelW{hp}"], writes=[f"SelW{hp}"])
    add("pool", lambda e: e.iota(cf[:], pattern=[[4, 16], [0, 16], [1, 4]], base=0, channel_multiplier=0, allow_small_or_imprecise_dtypes=True), reads=["Tsel"], writes=["cf"])
    add("pool", lambda e: e.iota(rf[:], pattern=[[0, 1024]], base=0, channel_multiplier=1, allow_small_or_imprecise_dtypes=True), reads=["Tsel"], writes=["rf"])
    add("dve", lambda e: e.tensor_tensor(out=RepSel[:].rearrange("p a n -> p (a n)"), in0=cf[:], in1=rf[:], op=ALU.is_equal), reads=["cf", "rf"], writes=["RepSel"])
    add("pool", lambda e: e.iota(cf[:, 0:4], pattern=[[16, 4]], base=0, channel_multiplier=0, allow_small_or_imprecise_dtypes=True), reads=["RepSel"], writes=["cf"])
    add("pool", lambda e: e.iota(rf[:, 0:4], pattern=[[0, 4]], base=0, channel_multiplier=1, allow_small_or_imprecise_dtypes=True), reads=["RepSel"], writes=["rf"])
    add("dve", lambda e: e.tensor_tensor(out=rf[:, 0:4], in0=rf[:, 0:4], in1=cf[:, 0:4], op=ALU.subtract), reads=["cf", "rf"], writes=["rf"])
    add("dve", lambda e: e.tensor_scalar(out=cf[:, 4:8], in0=rf[:, 0:4], scalar1=0.0, scalar2=None, op0=ALU.is_ge), reads=["rf"], writes=["cf2"])
    add("dve", lambda e: e.tensor_scalar(out=rf[:, 4:8], in0=rf[:, 0:4], scalar1=16.0, scalar2=None, op0=ALU.is_lt), reads=["rf"], writes=["rf2"])
    add("dve", lambda e: e.tensor_tensor(out=BDj[:], in0=cf[:, 4:8], in1=rf[:, 4:8], op=ALU.mult), reads=["cf2", "rf2"], writes=["BDj"])

    def gather(dst, dkey, table, sq, pg, sem):
        col = sq * 16 + pg
        wk = [dkey] + ([dkey.rsplit("_", 1)[0] + "_all"] if pg == 15 else [])
        add("pool", lambda e: e.indirect_dma_start(out=dst, out_offset=None, in_=table, in_offset=bass.IndirectOffsetOnAxis(ap=ridx[:, col:col + 1], axis=0)),
            reads=["ridx"], writes=wk, dma=sem)

    def ik_load(sq):
        s = sq % 2
        for pg in range(16):
            gather(ikp[s][:, pg, 0:64], f"ikp{s}_{pg}", cik_d, sq, pg, f"g{s}")

    add("pool", lambda e: e.memset(acc[0:64, 0:2052], 0.0), writes=["acc"])
    ik_load(0)
    for sq in range(16):
        s = sq % 2
        if sq + 1 < 16:
            ik_load(sq + 1)
        for pg in range(16):
            add("dve", lambda e, s=s, pg=pg: e.tensor_copy(out=ikp[s][:, pg, 64:128], in_=ikp[s][:, pg, 0:64]), reads=[f"ikp{s}_{pg}", f"ikp{s}_all"], writes=[f"ikp{s}_{pg}"])
        for pg in range(16):
            bank = pg // 8
            pb_ = ps[bank][:, :].bitcast(BF16)
            add("pe", lambda e, pg=pg, s=s, pb_=pb_: e.transpose(pb_[:, (pg % 8) * P:(pg % 8 + 1) * P], ikp[s][:, pg, :], identb[:]), reads=[f"ikp{s}_{pg}", "identb"], writes=[psk[bank]])
        for bank in range(2):
            pb_ = ps[bank][:, :].bitcast(BF16)
            add(evac_alt(bank), copy_op(evac_alt(bank), kiTs[:, bank * 1024:(bank + 1) * 1024], pb_[:, :]), reads=[psk[bank]], writes=["kiTs"])
        for kc in range(5):
            rs_ = kc % 2
            kn = 512 if kc < 4 else 4
            for hp in range(2):
                bank = 2 + hp
                rhs = kiTs[hp * 64:(hp + 1) * 64, kc * 512:(kc + 1) * 512] if kc < 4 else kiTn[hp * 64:(hp + 1) * 64, 4 * sq:4 * sq + 4]
                add("pe", lambda e, hp=hp, sq=sq, bank=bank, kn=kn, rhs=rhs: e.matmul(ps[bank][0:32, 0:kn], lhsT=qiC[hp * 64:(hp + 1) * 64, sq, :],
                                                                                   rhs=rhs, start=True, stop=True),
                    reads=["qiC", "kiTs", "kiT2"], writes=[psk[bank]])
                add("act", lambda e, bank=bank, kn=kn, rs_=rs_, hp=hp: e.activation(out=Rs[rs_][hp][:, 0:kn], in_=ps[bank][0:32, 0:kn], func=AF.Relu), reads=[psk[bank]], writes=[f"Rs{rs_}_{hp}"])
            hb = 4 + kc % 2
            for hp in range(2):
                add("pe", lambda e, sq=sq, hb=hb, kn=kn, rs_=rs_, hp=hp: e.matmul(ps[hb][0:64, 0:kn], lhsT=SelW[hp][:, sq, :], rhs=Rs[rs_][hp][:, 0:kn], start=(hp == 0), stop=(hp == 1)),
                    reads=[f"SelW{hp}", f"Rs{rs_}_{hp}"], writes=[psk[hb]])
            c0 = kc * 512
            add("dve", lambda e, hb=hb, kn=kn, c0=c0: e.tensor_tensor(out=acc[0:64, c0:c0 + kn], in0=acc[0:64, c0:c0 + kn], in1=ps[hb][0:64, 0:kn], op=ALU.add),
                reads=["acc", psk[hb]], writes=["acc"])
    topk(64, 2052, acc, pen, mask01, pen, qpos[0:64, 8:9])
    add("dve", lambda e: e.tensor_scalar(out=mbias[:], in0=mask01[0:64, 0:2052], scalar1=1.0, scalar2=30000.0, op0=ALU.subtract, op1=ALU.mult), reads=["mask01"], writes=["mbias"])
    kpg = [S.sb(f"kpg{i}", [P, 16, 512], BF16) for i in range(2)]
    vpg = [S.sb(f"vpg{i}", [P, 16, 512], BF16) for i in range(2)]
    KTs = S.sb("KTs", [P, 4, 2048], BF16)
    Qz = S.sb("Qz", [P, 4, 64], BF16)
    Ps_ = S.sb("Ps_", [64, 2048], BF16)
    Pn = S.sb("Pn", [64, 64], BF16)
    PTs = S.sb("PTs", [P, 17, 64], BF16)
    rsum = S.sb("rsum", [64, 8], F32)
    otmp = S.sb("otmp", [64, 4, P], F32)
    osel = S.sb("osel", [64, P], F32)
    aoS = S.sb("aoS", [P, 16, 64], BF16)

    def kv_load(sq):
        s = sq % 2
        for pg in range(16):
            gather(kpg[s][:, pg, :], f"kpg{s}_{pg}", ck_d, sq, pg, f"gk{s}")
            gather(vpg[s][:, pg, :], f"vpg{s}_{pg}", cv_d, sq, pg, f"gv{s}")

    kv_load(0)
    add("pool", lambda e: e.memset(Qz[:], 0.0), writes=["Qz"])
    add("pool", lambda e: e.memset(Pn[:], 0.0), writes=["Pn"])
    for sq in range(16):
        s = sq % 2
        if sq + 1 < 16:
            kv_load(sq + 1)
        for pg in range(16):
            for j in range(4):
                idx_ = pg * 4 + j
                bank = (idx_ // 8) % 2
                pb_ = ps[bank][:, :].bitcast(BF16)
                add("pe", lambda e, pg=pg, j=j, s=s, pb_=pb_, idx_=idx_: e.transpose(pb_[:, (idx_ % 8) * P:(idx_ % 8 + 1) * P], kpg[s][:, pg, j * P:(j + 1) * P], identb[:]),
                    reads=[f"kpg{s}_{pg}", f"kpg{s}_all", "identb"], writes=[psk[bank]])
                if idx_ % 8 == 7:
                    pg0 = pg - 1
                    add(evac_alt(bank), copy_op(evac_alt(bank), KTs[:, :, pg0 * P:(pg0 + 2) * P].rearrange("p j (g n) -> p g j n", g=2),
                                                pb_[:, :].rearrange("p (g j n) -> p g j n", g=2, j=4)), reads=[psk[bank]], writes=["KTs"])
        for j in range(4):
            add("dve", lambda e, j=j, sq=sq: e.tensor_copy(out=Qz[:, j, 16 * j:16 * j + 16].rearrange("p (g t) -> p g t", g=4), in_=qTs[:, 4 * j:4 * j + 4, 4 * sq:4 * sq + 4]),
                reads=["qTs", "Qz"], writes=["Qz"])
        for kc in range(5):
            bank = 2 + kc % 2
            kn = 512 if kc < 4 else 4
            for j in range(4):
                rhs = KTs[:, j, kc * 512:(kc + 1) * 512] if kc < 4 else KTn[:, j, 4 * sq:4 * sq + 4]
                add("pe", lambda e, j=j, bank=bank, kn=kn, rhs=rhs: e.matmul(ps[bank][0:64, 0:kn], lhsT=Qz[:, j, :], rhs=rhs, start=(j == 0), stop=False),
                    reads=["Qz", "KTs", "KT"], writes=[psk[bank]])
            mrhs = mbias[:, kc * 512:kc * 512 + kn]
            add("pe", lambda e, sq=sq, bank=bank, kn=kn, mrhs=mrhs: e.matmul(ps[bank][0:64, 0:kn], lhsT=RepSel[:, sq, :], rhs=mrhs, start=False, stop=True),
                reads=["RepSel", "mbias"], writes=[psk[bank]])
            if kc < 4:
                add("act", lambda e, bank=bank, kc=kc: e.activation(out=Ps_[:, kc * 512:(kc + 1) * 512], in_=ps[bank][0:64, :], func=AF.Exp, scale=SCALE, accum_out=rsum[:, kc:kc + 1]),
                    reads=[psk[bank]], writes=["Ps_", "rsum"])
            else:
                add("act", lambda e, bank=bank, sq=sq: e.activation(out=Pn[:, 4 * sq:4 * sq + 4], in_=ps[bank][0:64, 0:4], func=AF.Exp, scale=SCALE, accum_out=rsum[:, 4:5]),
                    reads=[psk[bank]], writes=["Pn", "rsum"])
        add("dve", lambda e: e.tensor_reduce(out=rsum[:, 5:6], in_=rsum[:, 0:5], axis=AX.X, op=ALU.add), reads=["rsum"], writes=["rsum5"])
        add("dve", lambda e: e.reciprocal(out=rsum[:, 5:6], in_=rsum[:, 5:6]), reads=["rsum5"], writes=["rsum5"])
        pb4 = ps[4][:, :].bitcast(BF16)
        for pg in range(16):
            add("pe", lambda e, pg=pg: e.transpose(pb4[:, pg * 64:(pg + 1) * 64], Ps_[:, pg * P:(pg + 1) * P], identb[0:64, 0:64]), reads=["Ps_", "identb"], writes=[psk[4]])
        add("act", copy_op("act", PTs[:, 0:16, :].rearrange("p a n -> p (a n)"), pb4[:, :]), reads=[psk[4]], writes=["PTs"])
        pb5 = ps[5][:, :].bitcast(BF16)
        add("pe", lambda e: e.transpose(pb5[0:64, 0:64], Pn[:, :], identb[0:64, 0:64]), reads=["Pn", "identb"], writes=[psk[5]])
        add("dve", copy_op("dve", PTs[0:64, 16, :], pb5[0:64, 0:64]), reads=[psk[5]], writes=["PTs"])
        if sq + 1 < 16:
            add("pool", lambda e, sq=sq: e.memset(Pn[:, 4 * sq:4 * sq + 4], 0.0), writes=["Pn"])
        for pg in range(16):
            add("pe", lambda e, pg=pg, s=s: e.matmul(ps[6][0:64, :], lhsT=PTs[:, pg, :], rhs=vpg[s][:, pg, :], start=(pg == 0), stop=False), reads=["PTs", f"vpg{s}_{pg}", f"vpg{s}_all"], writes=[psk[6]])
        add("pe", lambda e: e.matmul(ps[6][0:64, :], lhsT=PTs[0:64, 16, :], rhs=Vn[0:64, :], start=False, stop=True), reads=["PTs", "Vs"], writes=[psk[6]])
        add("dve", lambda e: e.tensor_tensor(out=otmp[:], in0=ps[6][0:64, :].rearrange("p (j d) -> p j d", j=4), in1=BDj[:].unsqueeze(2).to_broadcast([64, 4, P]), op=ALU.mult),
            reads=[psk[6], "BDj"], writes=["otmp"])
        add("dve", lambda e: e.tensor_reduce(out=osel[:], in_=otmp[:].rearrange("p j d -> p d j"), axis=AX.X, op=ALU.add), reads=["otmp"], writes=["osel"])
        add("dve", lambda e: e.tensor_scalar(out=osel[:], in0=osel[:], scalar1=rsum[:, 5:6], scalar2=None, op0=ALU.mult), reads=["osel", "rsum5"], writes=["osel"])
        add("pe", lambda e: e.transpose(ps[7][:, 0:64], osel[:, :], ident[0:64, 0:64]), reads=["osel", "ident"], writes=[psk[7]])
        add("act", lambda e, sq=sq: e.activation(out=aoS[:, :, 4 * sq:4 * sq + 4], in_=ps[7][:, 0:64].rearrange("p (h t) -> p h t", t=4), func=AF.Identity), reads=[psk[7]], writes=["aoS"])
    with nc.allow_non_contiguous_dma(reason="128B runs sample attn out"):
        add("sp", lambda e: e.dma_start(out=ao_scr[:, :, 1024:1088].rearrange("c p n -> p c n"), in_=aoS[:]), reads=["aoS"], dma="st0")
    S.barrier()
    S.release(m4)
    if stop_after <= 6:
        return finish()

    m7 = S.mark()
    mixT = S.sb("mixT", [P, KC, NTOK], BF16)
    m7b = S.mark()
    poT = S.sb("poT", [P, 8, NTOK], BF16)
    aoT = S.sb("aoT", [P, KC, NTOK], BF16)
    wup = [S.sb(f"wup{i}", [P, 24, 512], BF16) for i in range(2)]
    sgA = [S.sb(f"sgA{i}", [P, NTOK], F32) for i in range(2)]
    sgB = [S.sb(f"sgB{i}", [P, NTOK], F32) for i in range(2)]
    t1 = S.sb("t1", [P, 512], F32)
    t2 = S.sb("t2", [P, 512], F32)
    add("sp", lambda e: e.dma_start(out=poT[:], in_=po_scr.rearrange("c p n -> p c n")), writes=["poT"], dma="ld0")
    add("sp", lambda e: e.dma_start(out=aoT[:], in_=ao_scr.rearrange("c p n -> p c n")), writes=["aoT"], dma="ld1")

    def up_load(g):
        s = g % 2
        add("pool", lambda e: e.dma_start(out=wup[s][:, 0:8, :], in_=w_upp[:, g * 512:(g + 1) * 512].rearrange("(c p) n -> p c n", p=P)), writes=[f"wupP{s}"], dma=f"wa{s}")
        add("pool", lambda e: e.dma_start(out=wup[s][:, 8:24, :], in_=w_upa[:, g * 512:(g + 1) * 512].rearrange("(c p) n -> p c n", p=P)), writes=[f"wupA{s}"], dma=f"wb{s}")

    def sg_load(fc):
        s = fc % 2
        add("sp", lambda e: e.dma_start(out=sgA[s][:], in_=sga_scr[fc]), writes=[f"sgA{s}"], dma=f"ld{2 + s}")
        add("sp", lambda e: e.dma_start(out=sgB[s][:], in_=sgb_scr[fc]), writes=[f"sgB{s}"], dma=f"ld{4 + s}")

    up_load(0)
    sg_load(0)
    tgs = [(0, 512), (512, 512), (1024, 64)]
    pc = 0
    for g in range(4):
        s = g % 2
        if g + 1 < 4:
            up_load(g + 1)
        for k in range(4):
            fc = g * 4 + k
            fs = fc % 2
            if fc + 1 < 16:
                sg_load(fc + 1)
            for (t0, tn) in tgs:
                b0 = (pc % 2) * 2
                pc += 1
                for c in range(8):
                    add("pe", lambda e, c=c, s=s, k=k, t0=t0, tn=tn, b0=b0: e.matmul(ps[b0][:, 0:tn], lhsT=wup[s][:, c, k * P:(k + 1) * P], rhs=poT[:, c, t0:t0 + tn], start=(c == 0), stop=(c == 7)),
                        reads=[f"wupP{s}", "poT"], writes=[psk[b0]])
                for c in range(KC):
                    add("pe", lambda e, c=c, s=s, k=k, t0=t0, tn=tn, b0=b0: e.matmul(ps[b0 + 1][:, 0:tn], lhsT=wup[s][:, 8 + c, k * P:(k + 1) * P], rhs=aoT[:, c, t0:t0 + tn], start=(c == 0), stop=(c == KC - 1)),
                        reads=[f"wupA{s}", "aoT"], writes=[psk[b0 + 1]])
                add("dve", lambda e, t0=t0, tn=tn, b0=b0, fs=fs: e.tensor_tensor(out=t1[:, 0:tn], in0=ps[b0][:, 0:tn], in1=sgA[fs][:, t0:t0 + tn], op=ALU.mult), reads=[psk[b0], f"sgA{fs}"], writes=["t1"])
                add("dve", lambda e, t0=t0, tn=tn, b0=b0, fs=fs: e.tensor_tensor(out=t2[:, 0:tn], in0=ps[b0 + 1][:, 0:tn], in1=sgB[fs][:, t0:t0 + tn], op=ALU.mult), reads=[psk[b0 + 1], f"sgB{fs}"], writes=["t2"])
                add("pool", lambda e, t0=t0, tn=tn, fc=fc: e.tensor_tensor(out=mixT[:, fc, t0:t0 + tn], in0=t1[:, 0:tn], in1=t2[:, 0:tn], op=ALU.add), reads=["t1", "t2"], writes=["mixT"])
    S.barrier()
    S.release(m7b)
    wo = [S.sb(f"wo{i}", [P, KC, 512], BF16) for i in range(2)]
    al1 = S.sb("al1", [P, D], F32)
    al1s = S.sb("al1s", [64, D], F32)
    xq = [S.sb(f"xq{i}", [P, 512], F32) for i in range(3)]
    hq = [S.sb(f"hq{i}", [P, 512], F32) for i in range(3)]
    add("sp", lambda e: e.dma_start(out=al1[:], in_=mods_tm[64:65, 2 * D:3 * D].partition_broadcast(P).rearrange("p o n -> p (o n)")), writes=["al1"], dma="ld0")
    add("sp", lambda e: e.dma_start(out=al1s[:], in_=mods_tm[0:64, 2 * D:3 * D]), writes=["al1s"], dma="ld1")

    def wo_load(g):
        s = g % 2
        add("pool", lambda e: e.dma_start(out=wo[s][:], in_=w_out[:, g * 512:(g + 1) * 512].rearrange("(c p) n -> p c n", p=P)), writes=[f"wo{s}"], dma=f"w{s}")

    wo_load(0)
    it = 0
    for g in range(4):
        s = g % 2
        if g + 1 < 4:
            wo_load(g + 1)
        for ti in range(9):
            n = P if ti < 8 else 64
            r0 = ti * P
            xs_ = it % 3
            b0 = 4 + it % 2
            it += 1
            add("sp", lambda e, xs_=xs_, r0=r0, n=n, g=g: e.dma_start(out=xq[xs_][0:n, :], in_=x_own[r0:r0 + n, g * 512:(g + 1) * 512]), writes=[f"xq{xs_}"], dma=f"x{xs_}")
            for c in range(KC):
                add("pe", lambda e, c=c, s=s, r0=r0, n=n, b0=b0: e.matmul(ps[b0][0:n, :], lhsT=mixT[:, c, r0:r0 + n], rhs=wo[s][:, c, :], start=(c == 0), stop=(c == KC - 1)),
                    reads=["mixT", f"wo{s}"], writes=[psk[b0]])
            alp = al1 if ti < 8 else al1s
            akey = "al1" if ti < 8 else "al1s"
            add("dve", lambda e, xs_=xs_, n=n, b0=b0, g=g, alp=alp: e.tensor_tensor(out=hq[xs_][0:n, :], in0=ps[b0][0:n, :], in1=alp[0:n, g * 512:(g + 1) * 512], op=ALU.mult),
                reads=[psk[b0], akey], writes=[f"hq{xs_}"])
            add("pool", lambda e, xs_=xs_, n=n: e.tensor_tensor(out=hq[xs_][0:n, :], in0=hq[xs_][0:n, :], in1=xq[xs_][0:n, :], op=ALU.add), reads=[f"hq{xs_}", f"xq{xs_}"], writes=[f"hq{xs_}"])
            add("sp", lambda e, xs_=xs_, r0=r0, n=n, g=g: e.dma_start(out=h_scr[r0:r0 + n, g * 512:(g + 1) * 512], in_=hq[xs_][0:n, :]), reads=[f"hq{xs_}"], dma=f"st{xs_}")
    S.barrier()
    S.release(m7b)
    hnT = S.sb("hnT", [P, KC, NTOK], BF16)
    m7c = S.mark()
    A2S = S.sb("A2S", [P, KC, 64], F32)
    sh2S = S.sb("sh2S", [P, KC, 64], F32)
    load_modS(3, 2, g2T, A2S, sh2S, "f7")
    fr = Front("f7", A2P, 32, A2S, sh2S)
    tl = [(h_scr[i * P:(i + 1) * P, :], P, i * P, False) for i in range(8)] + [(h_scr[1024:1088, :], 64, 1024, True)]
    fr.load(tl[0][0], tl[0][1], 0)
    for ti, (src, n, t0, smp) in enumerate(tl):
        if ti + 1 < len(tl):
            fr.load(tl[ti + 1][0], tl[ti + 1][1], (ti + 1) % 2)
        fr.norm(n, ti % 2)
        fr.transpose_mod(n, hnT[:, :, t0:t0 + n], "hnT", smp)
    S.barrier()
    S.release(m7c)
    if stop_after <= 7:
        return finish()

    wf = [S.sb(f"wf{i}", [P, KC, 1024], BF16) for i in range(2)]
    sgt = [S.sb(f"sgt{i}", [P, 512], F32) for i in range(2)]
    ast = [S.sb(f"ast{i}", [P, NTOK], BF16) for i in range(2)]

    def f1_load(g):
        s = g % 2
        add("pool", lambda e: e.dma_start(out=wf[s][:, :, 0:512], in_=w_f1[:, g * 512:(g + 1) * 512].rearrange("(c p) n -> p c n", p=P)), writes=[f"wfG{s}"], dma=f"wa{s}")
        add("pool", lambda e: e.dma_start(out=wf[s][:, :, 512:1024], in_=w_f1[:, DFF + g * 512:DFF + (g + 1) * 512].rearrange("(c p) n -> p c n", p=P)), writes=[f"wfU{s}"], dma=f"wb{s}")

    f1_load(0)
    pc = 0
    for g in range(11):
        s = g % 2
        if g + 1 < 11:
            f1_load(g + 1)
        for k in range(4):
            f = g * 4 + k
            as_ = f % 2
            for (t0, tn) in tgs:
                b0 = (pc % 4) * 2
                sg_ = pc % 2
                pc += 1
                for c in range(KC):
                    add("pe", lambda e, c=c, s=s, k=k, t0=t0, tn=tn, b0=b0: e.matmul(ps[b0][:, 0:tn], lhsT=wf[s][:, c, k * P:(k + 1) * P], rhs=hnT[:, c, t0:t0 + tn], start=(c == 0), stop=(c == KC - 1)),
                        reads=[f"wfG{s}", "hnT"], writes=[psk[b0]])
                for c in range(KC):
                    add("pe", lambda e, c=c, s=s, k=k, t0=t0, tn=tn, b0=b0: e.matmul(ps[b0 + 1][:, 0:tn], lhsT=wf[s][:, c, 512 + k * P:512 + (k + 1) * P], rhs=hnT[:, c, t0:t0 + tn], start=(c == 0), stop=(c == KC - 1)),
                        reads=[f"wfU{s}", "hnT"], writes=[psk[b0 + 1]])
                add("act", lambda e, tn=tn, b0=b0, sg_=sg_: e.activation(out=sgt[sg_][:, 0:tn], in_=ps[b0][:, 0:tn], func=AF.Silu), reads=[psk[b0]], writes=[f"sgt{sg_}"])
                add("dve", lambda e, t0=t0, tn=tn, b0=b0, sg_=sg_, as_=as_: e.tensor_tensor(out=ast[as_][:, t0:t0 + tn], in0=ps[b0 + 1][:, 0:tn], in1=sgt[sg_][:, 0:tn], op=ALU.mult),
                    reads=[psk[b0 + 1], f"sgt{sg_}"], writes=[f"ast{as_}"])
            add("sp", lambda e, f=f, as_=as_: e.dma_start(out=aT_scr[0:8, :, f, :].rearrange("t p n -> p t n"), in_=ast[as_][:, 0:1024].rearrange("p (t n) -> p t n", n=P)),
                reads=[f"ast{as_}"], dma=f"st{as_}")
            add("sp", lambda e, f=f, as_=as_: e.dma_start(out=aT_scr[8, :, f, 0:64], in_=ast[as_][:, 1024:1088]), reads=[f"ast{as_}"], dma=f"st{2 + as_}")
    S.barrier()
    S.release(m7)
    if stop_after <= 8:
        return finish()

    w2b = [S.sb(f"w2b{i}", [P, 44, 512], BF16) for i in range(2)]
    atl = [S.sb(f"atl{i}", [P, 44, P], BF16) for i in range(2)]
    al2 = S.sb("al2", [P, D], F32)
    al2s = S.sb("al2s", [64, D], F32)
    hq = [S.sb(f"hq9_{i}", [P, 512], F32) for i in range(3)]
    yq = [S.sb(f"yq9_{i}", [P, 512], F32) for i in range(3)]
    ssq = S.sb("ssq", [P, 9, 4], F32)
    junk9 = S.sb("junk9", [P, 512], BF16)
    add("sp", lambda e: e.dma_start(out=al2[:], in_=mods_tm[64:65, 5 * D:6 * D].partition_broadcast(P).rearrange("p o n -> p (o n)")), writes=["al2"], dma="ld0")
    add("sp", lambda e: e.dma_start(out=al2s[:], in_=mods_tm[0:64, 5 * D:6 * D]), writes=["al2s"], dma="ld1")

    def f2_load(g):
        s = g % 2
        for q4 in range(4):
            add("pool", lambda e, q4=q4: e.dma_start(out=w2b[s][:, q4 * 11:(q4 + 1) * 11, :], in_=w_f2[q4 * 11 * P:(q4 + 1) * 11 * P, g * 512:(g + 1) * 512].rearrange("(c p) n -> p c n", p=P)),
                writes=[f"w2b{s}_{q4}"], dma=f"wq{s}_{q4}")

    def at_load(it_):
        ti = it_ % 9
        s = it_ % 2
        n = P if ti < 8 else 64
        add("sp", lambda e: e.dma_start(out=atl[s][:, :, 0:n], in_=aT_scr[ti, :, :, 0:n]), writes=[f"atl{s}"], dma=f"ld{2 + s}")

    f2_load(0)
    with nc.allow_non_contiguous_dma(reason="256B runs a^T tile loads"):
        at_load(0)
        it = 0
        for g in range(4):
            s = g % 2
            if g + 1 < 4:
                f2_load(g + 1)
            for ti in range(9):
                n = P if ti < 8 else 64
                r0 = ti * P
                as_ = it % 2
                xs_ = it % 3
                b0 = it % 2
                if it + 1 < 36:
                    at_load(it + 1)
                it += 1
                add("sp", lambda e, xs_=xs_, r0=r0, n=n, g=g: e.dma_start(out=hq[xs_][0:n, :], in_=h_scr[r0:r0 + n, g * 512:(g + 1) * 512]), writes=[f"hq9_{xs_}"], dma=f"x{xs_}")
                for c in range(44):
                    add("pe", lambda e, c=c, s=s, as_=as_, n=n, b0=b0: e.matmul(ps[b0][0:n, :], lhsT=atl[as_][:, c, 0:n], rhs=w2b[s][:, c, :], start=(c == 0), stop=(c == 43)),
                        reads=[f"atl{as_}", f"w2b{s}_{c // 11}"], writes=[psk[b0]])
                alp = al2 if ti < 8 else al2s
                akey = "al2" if ti < 8 else "al2s"
                add("dve", lambda e, xs_=xs_, n=n, b0=b0, g=g, alp=alp: e.tensor_tensor(out=yq[xs_][0:n, :], in0=ps[b0][0:n, :], in1=alp[0:n, g * 512:(g + 1) * 512], op=ALU.mult),
                    reads=[psk[b0], akey], writes=[f"yq9_{xs_}"])
                add("pool", lambda e, xs_=xs_, n=n: e.tensor_tensor(out=yq[xs_][0:n, :], in0=yq[xs_][0:n, :], in1=hq[xs_][0:n, :], op=ALU.add), reads=[f"yq9_{xs_}", f"hq9_{xs_}"], writes=[f"yq9_{xs_}"])
                add("act", lambda e, xs_=xs_, n=n, ti=ti, g=g: e.activation(out=junk9[0:n, :], in_=yq[xs_][0:n, :], func=AF.Square, accum_out=ssq[0:n, ti, g:g + 1]),
                    reads=[f"yq9_{xs_}"], writes=["junk9", "ssq"])
                add("sp", lambda e, xs_=xs_, r0=r0, n=n, g=g: e.dma_start(out=yp_scr[r0:r0 + n, g * 512:(g + 1) * 512], in_=yq[xs_][0:n, :]), reads=[f"yq9_{xs_}"], dma=f"st{xs_}")
    S.barrier()
    m10 = S.mark()
    gfr = S.sb("gfr", [P, D], F32)
    yb = [S.sb(f"yb{i}", [P, D], F32) for i in range(2)]
    yo = [S.sb(f"yo{i}", [P, D], F32) for i in range(2)]
    rs10 = S.sb("rs10", [P, 9, 2], F32)
    add("sp", lambda e: e.dma_start(out=gfr[:], in_=gf_d.partition_broadcast(P).rearrange("p o n -> p (o n)")), writes=["gfr"], dma="ld0")
    add("dve", lambda e: e.tensor_reduce(out=rs10[:, :, 0], in_=ssq[:], axis=AX.X, op=ALU.add), reads=["ssq"], writes=["rs10"])
    add("act", lambda e: e.activation(out=rs10[:, :, 1], in_=rs10[:, :, 0], func=AF.Sqrt, scale=1.0 / D, bias=EPS_AP[:, :]), reads=["rs10", "eps"], writes=["rs10b"])
    add("dve", lambda e: e.reciprocal(out=rs10[:, :, 1], in_=rs10[:, :, 1]), reads=["rs10b"], writes=["rs10b"])

    def y_load(ti):
        n = P if ti < 8 else 64
        add("sp", lambda e: e.dma_start(out=yb[ti % 2][0:n, :], in_=yp_scr[ti * P:ti * P + n, :]), writes=[f"yb{ti % 2}"], dma=f"x{ti % 2}")

    y_load(0)
    for ti in range(9):
        n = P if ti < 8 else 64
        s = ti % 2
        if ti + 1 < 9:
            y_load(ti + 1)
        add("act", lambda e, s=s, n=n, ti=ti: e.activation(out=yb[s][0:n, :], in_=yb[s][0:n, :], func=AF.Identity, scale=rs10[0:n, ti, 1:2]), reads=[f"yb{s}", "rs10b"], writes=[f"yb{s}"])
        add("dve", lambda e, s=s, n=n: e.tensor_tensor(out=yo[s][0:n, :], in0=yb[s][0:n, :], in1=gfr[0:n, :], op=ALU.mult), reads=[f"yb{s}", "gfr"], writes=[f"yo{s}"])
        add("sp", lambda e, s=s, n=n, ti=ti: e.dma_start(out=y_own[ti * P:ti * P + n, :], in_=yo[s][0:n, :]), reads=[f"yo{s}"], dma=f"st{s}")
    return finish()


def make_in_maps(inp, cores=range(8)):
    f = np.float32
    xp = np.asarray(inp["x_prompt"], f)
    xs = np.asarray(inp["x_sample"], f)
    pt = np.asarray(inp["page_table"], np.int32)
    ck = np.ascontiguousarray(np.asarray(inp["cache_k"])[0].reshape(2560 * 128, 512)[:CACHE_ROWS[0]], dtype=f)
    cv = np.ascontiguousarray(np.asarray(inp["cache_v"])[0].reshape(2560 * 128, 512)[:CACHE_ROWS[0]], dtype=f)
    cik = np.ascontiguousarray(np.asarray(inp["cache_idx_k"])[0].reshape(2560 * 128, 64)[:CACHE_ROWS[0]], dtype=f)
    stp = np.asarray(inp["state_pool"], f)[0]
    cp = np.asarray(inp["c_prompt"], f)
    cs = np.asarray(inp["c_sample"], f)
    shared = dict(
        ck=ck, cv=cv, cik=cik,
        w_ada=np.asarray(inp["w_ada"], f)[0], b_ada=np.asarray(inp["b_ada"], f)[0][None, :],
        g1=np.asarray(inp["g_norm1"], f)[0], g2=np.asarray(inp["g_norm2"], f)[0], gf=np.asarray(inp["g_final"], f)[None, :],
        w_in=np.asarray(inp["w_in"], f)[0], w_pg=np.asarray(inp["w_pool_grp"], f)[0], psc=np.asarray(inp["pool_scale"], f)[0],
        w_upp=np.asarray(inp["w_up_pool"], f)[0], w_upa=np.asarray(inp["w_up_attn"], f)[0], w_out=np.asarray(inp["w_out"], f)[0],
        w_f1=np.asarray(inp["w_ffn_in"], f)[0], w_f2=np.asarray(inp["w_ffn_out"], f)[0],
    )
    maps = []
    for c in cores:
        b, q = c // 4, c % 4
        blks = own_blocks(q)
        x_own = np.zeros((NALL, D), f)
        qpos = np.zeros((P, 9), f)
        hmask = np.zeros((1, P), f)
        for i, blk in enumerate(blks):
            x_own[i * P:(i + 1) * P] = xp[b, blk * P:(blk + 1) * P]
            qpos[:, i] = blk * P + np.arange(P)
            if blk > 0:
                x_own[1088 + 16 * i:1088 + 16 * (i + 1)] = xp[b, blk * P - 16:blk * P]
                hmask[0, 16 * i:16 * (i + 1)] = 1.0
        x_own[1024:1088] = xs[16 * c:16 * (c + 1)].reshape(64, D)
        qpos[0:64, 8] = 2048 + (np.arange(64) % 4)
        c_tok = np.empty((P, D), f)
        c_tok[0:64] = np.repeat(cs[16 * c:16 * (c + 1)], 4, axis=0)
        c_tok[64:128] = cp[b][None, :]
        meta = np.zeros((P, 512), f)
        meta[:, 0:9] = qpos
        meta[:, 16:144] = (blks[0] * P + np.arange(P, dtype=f))[None, :]
        meta[:, 144:272] = hmask
        m = dict(shared)
        m.update(
            x_seq=np.ascontiguousarray(xp[b]), x_own=x_own, c_tok=c_tok, meta=meta,
            ptm=np.ascontiguousarray(np.broadcast_to(pt[16 * c:16 * (c + 1)].reshape(1, 256), (P, 256))),
            stp=np.ascontiguousarray(stp[16 * c:16 * (c + 1)].reshape(240, 1024)),
        )
        maps.append(m)
    return maps


PER_V = ("x_seq", "x_own", "c_tok", "meta", "ptm", "stp")


def make_in_map_stacked(inp, vcores):
    maps = make_in_maps(inp, cores=vcores)
    m = dict(maps[0])
    for k in PER_V:
        m[k] = np.stack([mm[k] for mm in maps], 0)
    return m


_NC_CACHE = {}


N_PHYS = 8
V_PASS = 8 // N_PHYS


def kernel(**inp):
    if "nc" not in _NC_CACHE:
        _NC_CACHE["nc"] = build(V=V_PASS)[0]
    nc = _NC_CACHE["nc"]
    if V_PASS == 1:
        maps = make_in_maps(inp, cores=range(8))
    else:
        maps = [make_in_map_stacked(inp, list(range(pc * V_PASS, (pc + 1) * V_PASS))) for pc in range(N_PHYS)]
    res = run_bass_kernel_spmd(nc, maps, core_ids=list(range(N_PHYS))).results
    f = np.float32
    y_prompt = np.empty((2, 4096, D), f)
    y_sample = np.empty((128, 4, D), f)
    k_prompt = np.empty((1, 2, 4096, 4, 128), f)
    v_prompt = np.empty((1, 2, 4096, 4, 128), f)
    idxk_prompt = np.empty((1, 2, 4096, 64), f)
    pool_prompt = np.empty((1, 2, 15, 1024), f)
    k_sample = np.empty((1, 128, 4, 4, 128), f)
    v_sample = np.empty((1, 128, 4, 4, 128), f)
    idxk_sample = np.empty((1, 128, 4, 64), f)
    pool_sample = np.empty((1, 128, 15, 1024), f)
    for c in range(8):
        b, q = c // 4, c % 4
        r = {k: (np.asarray(v)[c % V_PASS] if V_PASS > 1 else np.asarray(v)) for k, v in res[c // V_PASS].items()}
        for i, blk in enumerate(own_blocks(q)):
            y_prompt[b, blk * P:(blk + 1) * P] = r["y_own"][i * P:(i + 1) * P]
        y_sample[16 * c:16 * (c + 1)] = r["y_own"][1024:1088].reshape(16, 4, D)
        if q == 0:
            k_prompt[0, b] = r["k_all"].reshape(4096, 4, 128)
            v_prompt[0, b] = r["v_all"].reshape(4096, 4, 128)
            idxk_prompt[0, b] = r["ki_all"]
            pool_prompt[0, b] = r["pool_last"]
        k_sample[0, 16 * c:16 * (c + 1)] = r["ks_o"].reshape(16, 4, 4, 128)
        v_sample[0, 16 * c:16 * (c + 1)] = r["vs_o"].reshape(16, 4, 4, 128)
        idxk_sample[0, 16 * c:16 * (c + 1)] = r["kis_o"].reshape(16, 4, 64)
        pool_sample[0, 16 * c:16 * (c + 1)] = r["pool_s"].reshape(16, 15, 1024)
    return (y_prompt, y_sample, k_prompt, v_prompt, idxk_prompt, pool_prompt, k_sample, v_sample, idxk_sample, pool_sample)
```
